# Optimizing a Trainium2 kernel written in Bass

```python
import math
import jax, jax.numpy as jnp
from jax import lax
import numpy as np

D_MODEL = 1024
BATCH = 16
SEQ = 2048
DEPTH = 2

GRID_W = 64
CTX_LEN = 256
N_EVEN = (DEPTH + 1) // 2
N_ODD = DEPTH // 2
RNN_WIDTH = D_MODEL // 2
LRU_HEADS = 8
LRU_HEAD_DIM = RNN_WIDTH // LRU_HEADS
LRU_CONV = 4
LRU_C = 8.0
LRU_A_MIN = 0.9
LRU_A_MAX = 0.999
HYENA_WIDTH = D_MODEL // 2
HYENA_ORDER = 2
HYENA_SHORT_CONV = 3
FILTER_EMB = 33
FILTER_HIDDEN = 64
HYENA_DECAY_TARGET = 1e-2
HYENA_FAST_PCT = 0.3
HYENA_SLOW_PCT = 1.5
HYENA_SHIFT = 0.05
IN_WIDTH = 2 * RNN_WIDTH + (HYENA_ORDER + 1) * HYENA_WIDTH
CONF_KERNEL = 31
N_EXPERTS = 16
EC_CAPACITY = 2
EXPERT_FF = 1024
N_MOD = 6
EPS = 1e-6

kernel_name = "hybrid_rglru_hyena_conformer_ecmoe_dit"


def _rmsnorm(x, g):
    xf = x.astype(jnp.float32)
    y = xf * lax.rsqrt(jnp.mean(xf * xf, axis=-1, keepdims=True) + EPS)
    return (y * g.astype(jnp.float32)).astype(x.dtype)


def _layernorm(x, g, b):
    xf = x.astype(jnp.float32)
    mu = jnp.mean(xf, axis=-1, keepdims=True)
    var = jnp.mean(jnp.square(xf - mu), axis=-1, keepdims=True)
    y = (xf - mu) * lax.rsqrt(var + EPS) * g.astype(jnp.float32) + b.astype(jnp.float32)
    return y.astype(x.dtype)


def _adaln(cvec, w_mod, b_mod, n_chunks):
    m = jax.nn.silu(cvec) @ w_mod[:, :n_chunks * D_MODEL] + b_mod[:n_chunks * D_MODEL]
    return [m[:, None, j * D_MODEL:(j + 1) * D_MODEL] for j in range(n_chunks)]


def _modulate(h, shift, scale):
    return h * (1 + scale) + shift


def _dwconv1d(x, w, b):
    k = w.shape[0]
    y = lax.conv_general_dilated(x, w[:, None, :], window_strides=(1,),
                                 padding=[((k - 1) // 2, k // 2)],
                                 dimension_numbers=('NWC', 'WIO', 'NWC'),
                                 feature_group_count=x.shape[-1])
    return y + b


def _dwconv_vertical(x, w, b, rows):
    bsz, n, ch = x.shape
    k = w.shape[0]
    grid = x.reshape(bsz, rows, GRID_W, ch)
    y = lax.conv_general_dilated(grid, w[:, None, None, :], window_strides=(1, 1),
                                 padding=[((k - 1) // 2, k // 2), (0, 0)],
                                 dimension_numbers=('NHWC', 'HWIO', 'NHWC'),
                                 feature_group_count=ch)
    return y.reshape(bsz, n, ch) + b


def _lin_combine(left, right):
    a_l, b_l = left
    a_r, b_r = right
    return a_l * a_r, a_r * b_l + b_r


def _rglru(xa_raw, p, h0):
    f32 = jnp.float32
    xa = _dwconv1d(xa_raw, p['conv_a_w'], p['conv_a_b'])
    bsz, n, _ = xa.shape
    xh = xa.reshape(bsz, n, LRU_HEADS, LRU_HEAD_DIM)

    def gate(w, b):
        z = jnp.einsum('bnhi,khij->kbnhj', xh, w).reshape(2, bsz, n, RNN_WIDTH)
        return jax.nn.sigmoid((z + b[:, None, None, :]).astype(f32))

    r = gate(p['lru_wr'], p['lru_br'])
    i = gate(p['lru_wi'], p['lru_bi'])
    log_a = -LRU_C * r * jax.nn.softplus(-p['lru_lam'].astype(f32))[:, None, None, :]
    a = jnp.exp(log_a)
    u = jnp.sqrt(-jnp.expm1(2.0 * log_a)) * i * xa.astype(f32)[None]
    a = jnp.stack([a[0], jnp.flip(a[1], axis=1)])
    u = jnp.stack([u[0], jnp.flip(u[1], axis=1)])
    a_cum, h = lax.associative_scan(_lin_combine, (a, u), axis=2)
    h = h + a_cum * h0[:, :, None, :]
    return h, h[:, :, -1]


def _hyena_filters_fft(n, p):
    f32 = jnp.float32
    t = jnp.linspace(0.0, 1.0, n, dtype=f32)[:, None]
    bands = (FILTER_EMB - 1) // 2
    w = (2.0 * math.pi / n) * jnp.arange(n, dtype=f32)[:, None]
    f = jnp.linspace(1e-4, bands - 1, bands, dtype=f32)[None, :]
    z = jnp.concatenate([t, jnp.cos(f * w), -jnp.sin(f * w)], axis=-1)
    freq = p['filt_freq'].astype(f32)
    hid = jnp.sin(freq[0] * (z @ p['filt_w1'].astype(f32) + p['filt_b1'].astype(f32)))
    hid = jnp.sin(freq[1] * (hid @ p['filt_w2'].astype(f32) + p['filt_b2'].astype(f32)))
    filt = (hid @ p['filt_w3'].astype(f32) + p['filt_b3'].astype(f32))
    filt = filt.reshape(n, HYENA_ORDER, 2, HYENA_WIDTH)
    deltas = jnp.abs(jnp.linspace(math.log(HYENA_DECAY_TARGET) / HYENA_SLOW_PCT,
                                  math.log(HYENA_DECAY_TARGET) / HYENA_FAST_PCT,
                                  HYENA_WIDTH, dtype=f32))
    window = jnp.exp(-t * deltas[None, :]) + HYENA_SHIFT
    filt = filt * window[:, None, None, :]
    fwd, bwd = filt[:, :, 0], filt[:, :, 1]
    kern = jnp.concatenate([fwd, jnp.zeros((1, HYENA_ORDER, HYENA_WIDTH), f32), bwd[:0:-1]], axis=0)
    return jnp.fft.rfft(kern, axis=0)


def _fftconv(u, kf, bias):
    n = u.shape[1]
    uf = u.astype(jnp.float32)
    y = jnp.fft.irfft(jnp.fft.rfft(uf, n=2 * n, axis=1) * kf[None], n=2 * n, axis=1)[:, :n]
    return (y + uf * bias.astype(jnp.float32)).astype(u.dtype)


def _even_mixer(h, p, h0):
    proj = h @ p['w_in']
    gate = proj[..., :RNN_WIDTH]
    xa_raw = proj[..., RNN_WIDTH:2 * RNN_WIDTH]
    hy = proj[..., 2 * RNN_WIDTH:]
    hs, h_last = _rglru(xa_raw, p, h0)
    y_a = jax.nn.gelu(gate) * (hs[0] + jnp.flip(hs[1], axis=1)).astype(h.dtype)
    u = _dwconv1d(hy, p['conv_b_w'], p['conv_b_b'])
    v, *gates = jnp.split(u, HYENA_ORDER + 1, axis=-1)
    kf = _hyena_filters_fft(h.shape[1], p)
    z = v
    for o in range(HYENA_ORDER):
        z = gates[o] * _fftconv(z, kf[:, o], p['filt_bias'][o])
    return jnp.concatenate([y_a, z], axis=-1) @ p['w_out'], h_last


def _conformer_conv(h, p, rows):
    u = h @ p['cf_w1'] + p['cf_b1']
    u = u[..., :D_MODEL] * jax.nn.sigmoid(u[..., D_MODEL:])
    if rows is None:
        u = _dwconv1d(u, p['cf_dw_w'], p['cf_dw_b'])
    else:
        u = _dwconv_vertical(u, p['cf_dw_w'], p['cf_dw_b'], rows)
    u = jax.nn.silu(_layernorm(u, p['cf_ln_g'], p['cf_ln_b']))
    return u @ p['cf_w2'] + p['cf_b2']


def _ec_moe(h, router, w1, w3, w2):
    bsz, n, _ = h.shape
    cap = EC_CAPACITY * n // N_EXPERTS
    probs = jax.nn.softmax(jnp.einsum('bnd,de->ben', h, router).astype(jnp.float32), axis=1)
    gates, idx = lax.top_k(probs, cap)
    bidx = jnp.arange(bsz)[:, None, None]
    xs = h[bidx, idx]
    a = jnp.einsum('becd,edf->becf', xs, w1)
    g = jnp.einsum('becd,edf->becf', xs, w3)
    y = jnp.einsum('becf,efd->becd', jax.nn.silu(a) * g, w2) * gates[..., None].astype(h.dtype)
    return jnp.zeros_like(h).at[bidx, idx].add(y)


def setup_inputs(seed: int = 0) -> dict:
    key = jax.random.key(seed)
    ks = iter(jax.random.split(key, 48))
    f32 = jnp.float32
    D = D_MODEL

    def nrm(shape, scale):
        return jax.random.normal(next(ks), shape, f32) * scale

    x = nrm((BATCH, SEQ, D), 1.0)
    c = nrm((BATCH, D), 1.0)
    ctx = nrm((BATCH, CTX_LEN, D), 1.0)
    c_ctx = nrm((D,), 1.0)
    norm1_g = 1.0 + nrm((DEPTH, D), 0.1)
    norm2_g = 1.0 + nrm((DEPTH, D), 0.1)
    w_mod = nrm((DEPTH, D, N_MOD * D), 0.5 * D ** -0.5)
    b_mod = nrm((DEPTH, N_MOD * D), 0.01)
    w_in = nrm((N_EVEN, D, IN_WIDTH), D ** -0.5)
    conv_a_w = nrm((N_EVEN, LRU_CONV, RNN_WIDTH), LRU_CONV ** -0.5)
    conv_a_b = nrm((N_EVEN, RNN_WIDTH), 0.01)
    lru_wr = nrm((N_EVEN, 2, LRU_HEADS, LRU_HEAD_DIM, LRU_HEAD_DIM), LRU_HEAD_DIM ** -0.5)
    lru_br = nrm((N_EVEN, 2, RNN_WIDTH), 0.01)
    lru_wi = nrm((N_EVEN, 2, LRU_HEADS, LRU_HEAD_DIM, LRU_HEAD_DIM), LRU_HEAD_DIM ** -0.5)
    lru_bi = nrm((N_EVEN, 2, RNN_WIDTH), 0.01)
    lo, hi = LRU_A_MIN ** (1.0 / LRU_C), LRU_A_MAX ** (1.0 / LRU_C)
    ua = jax.random.uniform(next(ks), (N_EVEN, 2, RNN_WIDTH), f32, lo, hi)
    lru_lam = jnp.log(ua) - jnp.log1p(-ua)
    conv_b_w = nrm((N_EVEN, HYENA_SHORT_CONV, (HYENA_ORDER + 1) * HYENA_WIDTH), HYENA_SHORT_CONV ** -0.5)
    conv_b_b = nrm((N_EVEN, (HYENA_ORDER + 1) * HYENA_WIDTH), 0.01)
    filt_w1 = nrm((N_EVEN, FILTER_EMB, FILTER_HIDDEN), FILTER_EMB ** -0.5)
    filt_b1 = nrm((N_EVEN, FILTER_HIDDEN), 0.1)
    filt_w2 = nrm((N_EVEN, FILTER_HIDDEN, FILTER_HIDDEN), FILTER_HIDDEN ** -0.5)
    filt_b2 = nrm((N_EVEN, FILTER_HIDDEN), 0.1)
    filt_w3 = nrm((N_EVEN, FILTER_HIDDEN, HYENA_ORDER * 2 * HYENA_WIDTH), 0.1 * FILTER_HIDDEN ** -0.5)
    filt_b3 = nrm((N_EVEN, HYENA_ORDER * 2 * HYENA_WIDTH), 0.01)
    filt_freq = 1.0 + nrm((N_EVEN, 2, FILTER_HIDDEN), 0.1)
    filt_bias = nrm((N_EVEN, HYENA_ORDER, HYENA_WIDTH), 0.5)
    w_out = nrm((N_EVEN, RNN_WIDTH + HYENA_WIDTH, D), (RNN_WIDTH + HYENA_WIDTH) ** -0.5)
    cf_w1 = nrm((N_ODD, D, 2 * D), D ** -0.5)
    cf_b1 = nrm((N_ODD, 2 * D), 0.01)
    cf_dw_w = nrm((N_ODD, CONF_KERNEL, D), CONF_KERNEL ** -0.5)
    cf_dw_b = nrm((N_ODD, D), 0.01)
    cf_ln_g = 1.0 + nrm((N_ODD, D), 0.1)
    cf_ln_b = nrm((N_ODD, D), 0.01)
    cf_w2 = nrm((N_ODD, D, D), D ** -0.5)
    cf_b2 = nrm((N_ODD, D), 0.01)
    router = nrm((DEPTH, D, N_EXPERTS), D ** -0.5)
    exp_w1 = nrm((DEPTH, N_EXPERTS, D, EXPERT_FF), D ** -0.5)
    exp_w3 = nrm((DEPTH, N_EXPERTS, D, EXPERT_FF), D ** -0.5)
    exp_w2 = nrm((DEPTH, N_EXPERTS, EXPERT_FF, D), EXPERT_FF ** -0.5)
    final_g = 1.0 + nrm((D,), 0.1)
    return {"x": x, "c": c, "ctx": ctx, "c_ctx": c_ctx,
            "norm1_g": norm1_g, "norm2_g": norm2_g, "w_mod": w_mod, "b_mod": b_mod,
            "w_in": w_in, "conv_a_w": conv_a_w, "conv_a_b": conv_a_b,
            "lru_wr": lru_wr, "lru_br": lru_br, "lru_wi": lru_wi, "lru_bi": lru_bi, "lru_lam": lru_lam,
            "conv_b_w": conv_b_w, "conv_b_b": conv_b_b,
            "filt_w1": filt_w1, "filt_b1": filt_b1, "filt_w2": filt_w2, "filt_b2": filt_b2,
            "filt_w3": filt_w3, "filt_b3": filt_b3, "filt_freq": filt_freq, "filt_bias": filt_bias,
            "w_out": w_out,
            "cf_w1": cf_w1, "cf_b1": cf_b1, "cf_dw_w": cf_dw_w, "cf_dw_b": cf_dw_b,
            "cf_ln_g": cf_ln_g, "cf_ln_b": cf_ln_b, "cf_w2": cf_w2, "cf_b2": cf_b2,
            "router": router, "exp_w1": exp_w1, "exp_w3": exp_w3, "exp_w2": exp_w2,
            "final_g": final_g}


def reference(x, c, ctx, c_ctx, norm1_g, norm2_g, w_mod, b_mod,
              w_in, conv_a_w, conv_a_b, lru_wr, lru_br, lru_wi, lru_bi, lru_lam,
              conv_b_w, conv_b_b, filt_w1, filt_b1, filt_w2, filt_b2, filt_w3, filt_b3,
              filt_freq, filt_bias, w_out,
              cf_w1, cf_b1, cf_dw_w, cf_dw_b, cf_ln_g, cf_ln_b, cf_w2, cf_b2,
              router, exp_w1, exp_w3, exp_w2, final_g):
    n_lat = x.shape[1]
    rows = n_lat // GRID_W
    last_ctx_layer = ((DEPTH - 1) // 2) * 2
    xl, xc = x, ctx
    for l in range(DEPTH):
        carry_ctx = l < last_ctx_layer
        sh1, sc1, g1, sh2, sc2, g2 = _adaln(c, w_mod[l], b_mod[l], N_MOD)
        hl = _modulate(_rmsnorm(xl, norm1_g[l]), sh1, sc1)
        if l % 2 == 0 or carry_ctx:
            cm = _adaln(c_ctx[None], w_mod[l], b_mod[l], N_MOD if carry_ctx else 2)
            hc = _modulate(_rmsnorm(xc, norm1_g[l]), cm[0], cm[1])
        if l % 2 == 0:
            e = l // 2
            pe = {'w_in': w_in[e], 'conv_a_w': conv_a_w[e], 'conv_a_b': conv_a_b[e],
                  'lru_wr': lru_wr[e], 'lru_br': lru_br[e], 'lru_wi': lru_wi[e], 'lru_bi': lru_bi[e],
                  'lru_lam': lru_lam[e], 'conv_b_w': conv_b_w[e], 'conv_b_b': conv_b_b[e],
                  'filt_w1': filt_w1[e], 'filt_b1': filt_b1[e], 'filt_w2': filt_w2[e],
                  'filt_b2': filt_b2[e], 'filt_w3': filt_w3[e], 'filt_b3': filt_b3[e],
                  'filt_freq': filt_freq[e], 'filt_bias': filt_bias[e], 'w_out': w_out[e]}
            h0_ctx = jnp.zeros((2, xc.shape[0], RNN_WIDTH), jnp.float32)
            if carry_ctx:
                yc, ctx_state = _even_mixer(hc, pe, h0_ctx)
            else:
                _, ctx_state = _rglru(hc @ pe['w_in'][:, RNN_WIDTH:2 * RNN_WIDTH], pe, h0_ctx)
            yl, _ = _even_mixer(hl, pe, ctx_state)
        else:
            o = l // 2
            po = {'cf_w1': cf_w1[o], 'cf_b1': cf_b1[o], 'cf_dw_w': cf_dw_w[o], 'cf_dw_b': cf_dw_b[o],
                  'cf_ln_g': cf_ln_g[o], 'cf_ln_b': cf_ln_b[o], 'cf_w2': cf_w2[o], 'cf_b2': cf_b2[o]}
            yl = _conformer_conv(hl, po, rows)
            if carry_ctx:
                yc = _conformer_conv(hc, po, None)
        xl = xl + g1 * yl
        xl = xl + g2 * _ec_moe(_modulate(_rmsnorm(xl, norm2_g[l]), sh2, sc2),
                               router[l], exp_w1[l], exp_w3[l], exp_w2[l])
        if carry_ctx:
            xc = xc + cm[2] * yc
            xc = xc + cm[5] * _ec_moe(_modulate(_rmsnorm(xc, norm2_g[l]), cm[3], cm[4]),
                                      router[l], exp_w1[l], exp_w3[l], exp_w2[l])
    return _rmsnorm(xl, final_g)
```

```python
from contextlib import ExitStack
import math
import numpy as np
import concourse.bass as bass
import concourse.mybir as mybir
from concourse.bass_utils import run_bass_kernel_spmd

F32 = mybir.dt.float32
BF16 = mybir.dt.bfloat16
ALU = mybir.AluOpType
AF = mybir.ActivationFunctionType

NCORES = 8
D = 1024
SEQ = 2048
NS = 2
NT = SEQ // 128
CTX = 256
EPS = 1e-6
NEXP = 16
CAP = 256


class Trk:
    def __init__(self, name, base=None):
        self.name = name
        self.writers = []
        self.readers = []
        self.base = base if base is not None else self
        self.subs = {}

    def k(self, key):
        s = self.subs.get(key)
        if s is None:
            s = Trk(f"{self.name}.{key}", self.base)
            self.subs[key] = s
        return s


class T(Trk):
    def __init__(self, name, h):
        super().__init__(name)
        self.h = h

    def __getitem__(self, idx):
        return self.h[idx]


ENGS = ("pe", "act", "dve", "pool", "sp")


class Prog:
    def __init__(self, nc, es):
        self.nc = nc
        self.es = es
        self.ops = {e: [] for e in ENGS}
        self.ecnt = {e: 0 for e in ENGS}
        self.seen = {e: {} for e in ENGS}
        self.sems = {}
        self.dcnt = {}
        self.nsem = 0
        self.uid = 0
        self.dkey = {}
        self.dfree = []
        self.dfree_sw = []
        self.dall = []
        self.dall_sw = []
        self.dpool = 0

    def _sem(self, key):
        s = self.sems.get(key)
        if s is None:
            self.nsem += 1
            s = self.es.enter_context(self.nc.semaphore(f"sem{self.nsem}"))
            self.sems[key] = s
        return s

    def sbuf(self, name, shape, dtype, es=None):
        es = es or self.es
        self.uid += 1
        h = es.enter_context(self.nc.sbuf_tensor(f"{name}_{self.uid}", list(shape), dtype))
        return T(name, h)

    def psum(self, name, shape, dtype, es=None):
        es = es or self.es
        self.uid += 1
        h = es.enter_context(self.nc.psum_tensor(f"{name}_{self.uid}", list(shape), dtype))
        return T(name, h)

    def dram(self, name, shape, dtype, kind="Internal"):
        h = self.nc.dram_tensor(name, list(shape), dtype, kind=kind)
        return T(name, h.ap())

    def _waits(self, eng, deps):
        need = {}
        for key, cnt in deps:
            if key[0] == "E" and key[1] == eng and eng == "pe":
                continue
            if key[0] == "D":
                cnt = self.dcnt[key]
            if cnt > need.get(key, 0):
                need[key] = cnt
        for key, cnt in need.items():
            if self.seen[eng].get(key, 0) >= cnt:
                continue
            self.seen[eng][key] = cnt
            self.ops[eng].append(("wait", key, cnt))

    def _deps(self, reads, writes, pwrites):
        deps = []
        for r in reads:
            deps += r.writers
        for w in writes:
            deps += w.writers
            deps += w.readers
        for w in pwrites:
            deps += w.readers
        return deps

    def _commit(self, ev, reads, writes, pwrites):
        for r in reads:
            r.readers.append(ev)
        for w in writes:
            w.writers = [ev]
            w.readers = []
        for w in pwrites:
            w.writers.append(ev)

    def op(self, eng, fn, reads=(), writes=(), pwrites=()):
        self._waits(eng, self._deps(reads, writes, pwrites))
        key = ("E", eng)
        self._sem(key)
        self.ecnt[eng] += 1
        ev = (key, self.ecnt[eng])
        self.ops[eng].append(("op", fn, key, 1))
        self._commit(ev, reads, writes, pwrites)

    def dma(self, q, out, in_, reads=(), writes=(), pwrites=(), semof=None, **kw):
        self._waits(q, self._deps(reads, writes, pwrites))
        sw = (q == "pool")
        key = self.dkey.get((id(semof.base), sw))
        if key is None:
            free = self.dfree_sw if sw else self.dfree
            if free:
                idx = free.pop()
            else:
                idx = (self.dpool, sw)
                self.dpool += 1
                (self.dall_sw if sw else self.dall).append(idx)
            key = ("D", idx)
            self.dkey[(id(semof.base), sw)] = key
        self._sem(key)
        self.dcnt[key] = self.dcnt.get(key, 0) + 16
        ev = (key, self.dcnt[key])
        fn = kw.pop("fn", None)
        if fn is None:
            fn = lambda e: e.dma_start(out=out, in_=in_, **kw)
        self.ops[q].append(("op", fn, key, 16))
        self._commit(ev, reads, writes, pwrites)

    def barrier(self):
        for e in ENGS:
            deps = [(("E", e2), self.ecnt[e2]) for e2 in ENGS if self.ecnt[e2] > 0 and not (e == e2 == "pe")]
            deps += [(k, c) for k, c in self.dcnt.items()]
            self._waits(e, deps)
        self.dkey = {}
        self.dfree = list(self.dall)
        self.dfree_sw = list(self.dall_sw)

    def emit(self):
        nc = self.nc
        hmap = {"pe": "tensor", "act": "scalar", "dve": "vector", "pool": "gpsimd", "sp": "sync"}
        with nc.Block() as block:
            for e in ENGS:
                if not self.ops[e]:
                    continue

                def body(engh, e=e):
                    for item in self.ops[e]:
                        if item[0] == "wait":
                            engh.wait_ge(self.sems[item[1]], item[2])
                        else:
                            ins = item[1](engh)
                            ins.then_inc(self.sems[item[2]], item[3])

                getattr(block, hmap[e])(body)


def mm(P, out, lhsT, rhs, start, stop, reads, wr):
    P.op("pe", lambda e: e.matmul(out, lhsT=lhsT, rhs=rhs, start=start, stop=stop),
         reads=reads, writes=[wr] if start else [], pwrites=[] if start else [wr])


def tr(P, out, in_, ident, reads, wr, first=True):
    P.op("pe", lambda e: e.transpose(out, in_, ident), reads=reads,
         writes=[wr] if first else [], pwrites=[] if first else [wr])


def act(P, out, in_, func, reads, writes, eng="act", **kw):
    P.op(eng, lambda e: e.activation(out=out, in_=in_, func=func, **kw), reads=reads, writes=writes)


def tt(P, eng, out, in0, in1, op, reads, writes):
    P.op(eng, lambda e: e.tensor_tensor(out=out, in0=in0, in1=in1, op=op), reads=reads, writes=writes)


def ts(P, eng, out, in0, s1, op0, reads, writes, s2=None, op1=None, **kw):
    if op1 is None:
        P.op(eng, lambda e: e.tensor_scalar(out=out, in0=in0, scalar1=s1, scalar2=None, op0=op0, **kw),
             reads=reads, writes=writes)
    else:
        P.op(eng, lambda e: e.tensor_scalar(out=out, in0=in0, scalar1=s1, scalar2=s2, op0=op0, op1=op1, **kw),
             reads=reads, writes=writes)


def stt(P, out, in0, scalar, in1, op0, op1, reads, writes):
    P.op("dve", lambda e: e.scalar_tensor_tensor(out=out, in0=in0, scalar=scalar, in1=in1, op0=op0, op1=op1),
         reads=reads, writes=writes)


def cp(P, eng, out, in_, reads, writes):
    if eng == "act":
        P.op(eng, lambda e: e.copy(out=out, in_=in_), reads=reads, writes=writes)
    else:
        P.op(eng, lambda e: e.tensor_copy(out=out, in_=in_), reads=reads, writes=writes)


class Ctx:
    pass


def stage_adaln(P, G, ph):
    sT = P.sbuf("sT", [128, 8, 3], F32, ph)
    sS = P.sbuf("sS", [128, 8, 3], F32, ph)
    wb = [P.sbuf(f"wmod{i}", [128, 8, 256], F32, ph) for i in range(3)]
    bm = [P.sbuf(f"bm{i}", [3, 256], F32, ph) for i in range(3)]
    mr = [P.sbuf(f"mr{i}", [3, 256], F32, ph) for i in range(3)]
    ps = G.ps
    P.dma("sp", sT[:], G.csT[:], writes=[sT], semof=sT)
    act(P, sS[:], sT[:], AF.Silu, [sT], [sS])
    thunks = []
    chunks = [(l, ch) for l in range(2) for ch in range(24)]

    def load(it):
        l, ch = chunks[it]
        w = wb[it % 3]
        b = bm[it % 3]
        P.dma("pool", w[:], G.w_mod[l, :, ch * 256:(ch + 1) * 256].rearrange("(kt p) f -> p kt f", p=128), writes=[w], semof=w)
        P.dma("pool", b[:], G.b_mod[l:l + 1, ch * 256:(ch + 1) * 256].broadcast_to([3, 256]), writes=[b], semof=b)

    def work(it):
        l, ch = chunks[it]
        w = wb[it % 3]
        b = bm[it % 3]
        m = mr[it % 3]
        p = ps[6 + it % 2]
        for kt in range(8):
            mm(P, p[0:3, 0:256], sS[:, kt, :], w[:, kt, :], kt == 0, kt == 7, [sS, w], p)
        tt(P, "dve", m[:], p[0:3, 0:256], b[:], ALU.add, [p, b], [m])
        P.dma("pool", G.mrow[l, :, ch * 256:(ch + 1) * 256], m[:], reads=[m], writes=[G.mrow.k((l, ch))], semof=m)
        if it + 2 < len(chunks):
            load(it + 2)

    load(0)
    load(1)
    for it in range(len(chunks)):
        thunks.append(lambda it=it: work(it))
    return thunks


def norm_stage(P, G, ph, src_tile, ntiles, g_ap, l, row, ish, isc, hT=None, h_tm=None, probs=None, rt=None, pcol=0):
    norm_jobs(P, G, ntiles, [dict(src_tile=src_tile, g_ap=g_ap, l=l, row=row, ish=ish, isc=isc, hT=hT, h_tm=h_tm, probs=probs, rt=rt, pcol=pcol)])


def norm_jobs(P, G, ntiles, jobs):
    nj = len(jobs)
    with ExitStack() as loc:
        ps = G.ps
        for ji, J in enumerate(jobs):
            J["gb"] = P.sbuf("gb", [128, D], F32, loc)
            J["A"] = P.sbuf("A", [128, D], F32, loc)
            J["B"] = P.sbuf("B", [128, D], F32, loc)
            J["xt"] = [P.sbuf(f"xt{i}", [128, D], F32, loc) for i in range(3)]
            J["hn"] = [P.sbuf(f"hn{i}", [128, D], F32, loc) for i in range(3)]
            J["junk"] = P.sbuf("junk", [128, D], F32, loc)
            J["st"] = [P.sbuf(f"st{i}", [128, 4], F32, loc) for i in range(3)]
            if J["probs"] is not None:
                J["hTf"] = [P.sbuf(f"hTf{i}", [128, 8, 128], F32, loc) for i in range(2)]
                J["sm"] = [P.sbuf(f"sm{i}", [128, 24], F32, loc) for i in range(2)]
            gb, A, B, l, row, isc, ish = J["gb"], J["A"], J["B"], J["l"], J["row"], J["isc"], J["ish"]
            P.dma("sp", gb[:], J["g_ap"].broadcast_to([128, D]), writes=[gb], semof=gb)
            P.dma("sp", A[:], G.mrow[l, row:row + 1, isc * D:(isc + 1) * D].broadcast_to([128, D]),
                  reads=[G.mrow.k((l, isc * 4 + i_)) for i_ in range(4)], writes=[A], semof=A)
            P.dma("sp", B[:], G.mrow[l, row:row + 1, ish * D:(ish + 1) * D].broadcast_to([128, D]),
                  reads=[G.mrow.k((l, ish * 4 + i_)) for i_ in range(4)], writes=[B], semof=B)
            stt(P, A[:], A[:], 1.0, gb[:], ALU.add, ALU.mult, [A, gb], [A])
            if nj == 1:
                J["trb"] = lambda t, half: ps[(t % 2) * 2 + half]
                J["lgb"] = lambda t: ps[4 + t % 2]
            else:
                J["trb"] = lambda t, half, ji=ji: ps[ji * 3 + half]
                J["lgb"] = lambda t, ji=ji: ps[ji * 3 + 2]
        def nload(J, t):
            x = J["xt"][t % 3]
            rd, ap = J["src_tile"](t)
            P.dma("sp", x[:], ap, reads=rd, writes=[x], semof=x)

        for t0 in range(min(2, ntiles)):
            for J in jobs:
                nload(J, t0)
        for t in range(ntiles):
            for J in jobs:
                hT, h_tm, probs, rt, pcol = J["hT"], J["h_tm"], J["probs"], J["rt"], J["pcol"]
                A, B = J["A"], J["B"]
                if t + 2 < ntiles:
                    nload(J, t + 2)
                x = J["xt"][t % 3]
                h = J["hn"][t % 3]
                s_ = J["st"][t % 3]
                junk = J["junk"]
                act(P, junk[:], x[:], AF.Square, [x], [junk, s_.k(0)], accum_out=s_[:, 0:1])
                act(P, s_[:, 1:2], s_[:, 0:1], AF.Sqrt, [s_.k(0)], [s_.k(1)], scale=1.0 / D, bias=G.epsc[:, 0:1])
                P.op("dve", lambda e, o=s_[:, 2:3], i=s_[:, 1:2]: e.reciprocal(out=o, in_=i), reads=[s_.k(1)], writes=[s_.k(2)])
                stt(P, h[:], x[:], s_[:, 2:3], A[:], ALU.mult, ALU.mult, [x, s_.k(2), A], [h])
                tt(P, "dve", h[:], h[:], B[:], ALU.add, [h, B], [h])
                if h_tm is not None:
                    P.dma("sp", G.h2d[h_tm, t * 128:(t + 1) * 128, :], h[:], reads=[h], writes=[G.h2d.k((h_tm, t))], semof=h)
                if hT is not None or probs is not None:
                    for half in range(2):
                        p = J["trb"](t, half)
                        for j in range(4):
                            k = half * 4 + j
                            tr(P, p[:, j * 128:(j + 1) * 128], h[:, k * 128:(k + 1) * 128], G.ident[:], [h, G.ident], p, first=(j == 0))
                        pv = p[:].rearrange("p (j n) -> p j n", j=4)
                        if hT is not None:
                            cp(P, "act" if half == 0 else "dve", hT[:, half * 4:half * 4 + 4, t * 128:(t + 1) * 128], pv, [p], [hT.k((t, half))])
                        if probs is not None:
                            f = J["hTf"][t % 2]
                            cp(P, "dve" if half == 0 else "act", f[:, half * 4:half * 4 + 4, :], pv, [p], [f.k(half)])
                if probs is not None:
                    f = J["hTf"][t % 2]
                    pl = J["lgb"](t)
                    m = J["sm"][t % 2]
                    for k in range(8):
                        mm(P, pl[:, 0:16], f[:, k, :], rt[:, k, :], k == 0, k == 7, [f.k(0), f.k(1), rt], pl)
                    P.op("dve", lambda e, o=m[:, 16:17], i=pl[:, 0:16]: e.tensor_reduce(out=o, in_=i, axis=mybir.AxisListType.X, op=ALU.max),
                         reads=[pl], writes=[m.k(1)])
                    ts(P, "dve", m[:, 17:18], m[:, 16:17], -1.0, ALU.mult, [m.k(1)], [m.k(2)])
                    act(P, m[:, 0:16], pl[:, 0:16], AF.Exp, [pl, m.k(2)], [m.k(0), m.k(3)], bias=m[:, 17:18], scale=1.0, accum_out=m[:, 18:19])
                    P.op("dve", lambda e, o=m[:, 19:20], i=m[:, 18:19]: e.reciprocal(out=o, in_=i), reads=[m.k(3)], writes=[m.k(4)])
                    ts(P, "dve", probs[:, t, pcol:pcol + 16], m[:, 0:16], m[:, 19:20], ALU.mult, [m.k(0), m.k(4)], [probs.k((t, pcol))])
    P.barrier()


def whole(Tobj, n):
    return [Tobj.k(i) for i in range(n)]


def route_stage(P, G, probs, code_tm, pg_tm):
    NP = NS * NEXP
    with ExitStack() as loc:
        PT = P.sbuf("PT", [NP, SEQ], F32, loc)
        msk = P.sbuf("msk", [NP, SEQ], F32, loc)
        cum = P.sbuf("cum", [NP, SEQ], F32, loc)
        ones = P.sbuf("ones", [NP, SEQ], F32, loc)
        sc = P.sbuf("sc", [NP, 8], F32, loc)
        ps = G.ps
        for q in range(4):
            for j in range(4):
                t = q * 4 + j
                tr(P, ps[q][0:NP, j * 128:(j + 1) * 128], probs[:, t, :], G.ident[:], [probs, G.ident], ps[q], first=(j == 0))
            cp(P, "act" if q % 2 else "dve", PT[:, q * 512:(q + 1) * 512], ps[q][0:NP, :], [ps[q]], [PT.k(q)])
        PTk = whole(PT, 4)
        P.op("dve", lambda e: e.memset(ones[:], 1.0), writes=[ones])
        mid, cnt, g = (sc[:, i:i + 1] for i in range(3))
        P.op("dve", lambda e: e.memset(mid, 0.5), writes=[sc.k(0)])
        NIT = 26
        for it in range(NIT):
            w_next = 0.5 ** (it + 2)
            ts(P, "dve", msk[:], PT[:], mid, ALU.is_ge, PTk + [sc.k(0)], [msk, sc.k(1)], s2=0.0, op1=ALU.add, accum_out=cnt)
            ts(P, "dve", g, cnt, float(CAP), ALU.is_ge, [sc.k(1)], [sc.k(2)], s2=2.0 * w_next, op1=ALU.mult)
            stt(P, mid, g, -w_next, mid, ALU.add, ALU.add, [sc.k(2), sc.k(0)], [sc.k(0)])
        ts(P, "dve", mid, mid, -(0.5 ** (NIT + 1)), ALU.add, [sc.k(0)], [sc.k(0)])
        ts(P, "dve", msk[:], PT[:], mid, ALU.is_ge, PTk + [sc.k(0)], [msk])
        P.op("dve", lambda e: e.tensor_tensor_scan(out=cum[:], data0=ones[:], data1=msk[:], initial=0.0, op0=ALU.mult, op1=ALU.add),
             reads=[ones, msk], writes=[cum])
        tt(P, "dve", cum[:], cum[:], msk[:], ALU.mult, [cum, msk], [cum])
        ts(P, "dve", cum[:], cum[:], -1.0, ALU.add, [cum], [cum])
        tt(P, "dve", msk[:], msk[:], PT[:], ALU.mult, [msk] + PTk, [msk])
        for src, dst, pb in ((cum, code_tm, ps[4]), (msk, pg_tm, ps[5])):
            for t in range(NT):
                tr(P, pb[:, t * NP:(t + 1) * NP], src[0:NP, t * 128:(t + 1) * 128], G.ident[0:NP, 0:NP], [src, G.ident], pb, first=(t == 0))
            cp(P, "act", dst[:].rearrange("p t e -> p (t e)"), pb[:, 0:NT * NP], [pb], [dst])
    P.barrier()


def expert_stage(P, G, l, code_tm, pg_tm, igate):
    I32 = mybir.dt.int32
    h2flat = G.h2d[:].rearrange("s n d -> (s n) d")
    xrflat = G.xres[:].rearrange("s n d -> (s n) d")
    with ExitStack() as loc:
        wb = [P.sbuf(f"wexp{i}", [128, 8, D], BF16, loc) for i in range(5)]
        S = [P.sbuf(f"S{s}", [128, NT, CAP], BF16, loc) for s in range(NS)]
        R = [P.sbuf(f"R{s}", [128, NT, 4], BF16, loc) for s in range(NS)]
        rl = [P.sbuf(f"rlo{s}", [128, NT], F32, loc) for s in range(NS)]
        xs = [[P.sbuf(f"xs{i}_{c}", [128, D], F32, loc) for c in range(4)] for i in range(2)]
        ig = [P.sbuf(f"ig{i}", [128, 4, 4], F32, loc) for i in range(2)]
        idx = [P.sbuf(f"idx{i}", [128, 4], I32, loc) for i in range(2)]
        gate = [P.sbuf(f"gate{i}", [128, 4], F32, loc) for i in range(2)]
        xsT = [P.sbuf(f"xsT{i}", [128, 8, NS * CAP], BF16, loc) for i in range(2)]
        actT = P.sbuf("actT", [128, 8, NS * CAP], BF16, loc)
        tmp = [P.sbuf(f"sil{i}", [128, NS * CAP], F32, loc) for i in range(2)]
        ysb = [P.sbuf(f"ysb{c}", [128, D], F32, loc) for c in range(4)]
        g2 = [P.sbuf(f"g2_{s}", [128, D], F32, loc) for s in range(NS)]
        ps = G.ps
        for s in range(NS):
            P.dma("sp", g2[s][:], G.mrow[l, s:s + 1, igate * D:(igate + 1) * D].broadcast_to([128, D]), writes=[g2[s]], semof=g2[s])
            cp(P, "dve", R[s][:, :, 0:2], G.npos[:], [G.npos], [R[s].k(0)])
        st = {"wi": 0, "pi": 0}
        wts = {}

        offs = P.sbuf("offs", [128, 4], F32, loc)
        for c in range(4):
            P.op("dve", lambda en, c=c: en.memset(offs[:, c:c + 1], float((c // 2) * SEQ)), pwrites=[offs])

        def prep1_ops(e):
            ops = []
            for s in range(NS):
                col = s * NEXP + e
                for t in range(NT):
                    ops.append(lambda s=s, t=t, col=col: ts(P, "dve", S[s][:, t, :], G.iota[:, 0:CAP], code_tm[:, t, col:col + 1], ALU.is_equal,
                                                           [G.iota, code_tm], [S[s].k(t)]))
                ops.append(lambda s=s, col=col: cp(P, "dve", R[s][:, :, 2:3], pg_tm[:, :, col:col + 1], [pg_tm], [R[s].k(1)]))
                ops.append(lambda s=s, col=col: tt(P, "dve", rl[s][:].rearrange("p (t o) -> p t o", o=1), pg_tm[:, :, col:col + 1], R[s][:, :, 2:3],
                                                   ALU.subtract, [pg_tm, R[s].k(1)], [rl[s]]))
                ops.append(lambda s=s: cp(P, "dve", R[s][:, :, 3:4], rl[s][:].rearrange("p (t o) -> p t o", o=1), [rl[s]], [R[s].k(2)]))
            return ops

        def prep2(e):
            b = e % 2
            p = ps[6 + (e % 2)]
            for s in range(NS):
                Sk = whole(S[s], NT)
                Rk = [R[s].k(0), R[s].k(1), R[s].k(2)]
                for ch in range(2):
                    c = s * 2 + ch
                    for t in range(NT):
                        mm(P, p[:, c * 4:(c + 1) * 4], S[s][:, t, ch * 128:(ch + 1) * 128], R[s][:, t, :], t == 0, t == NT - 1, Sk + Rk,
                           p if c == 0 else p.k(c))
            cp(P, "dve", ig[b][:].rearrange("p c f -> p (c f)"), p[:, 0:16], [p, p.k(1), p.k(2), p.k(3)], [ig[b]])
            stt(P, gate[b][:].rearrange("p (c o) -> p c o", o=1), ig[b][:, :, 0:1], 128.0, ig[b][:, :, 1:2], ALU.mult, ALU.add, [ig[b]], [gate[b]])
            tt(P, "dve", idx[b][:], gate[b][:], offs[:], ALU.add, [gate[b], offs], [idx[b]])
            tt(P, "dve", gate[b][:].rearrange("p (c o) -> p c o", o=1), ig[b][:, :, 2:3], ig[b][:, :, 3:4], ALU.add, [ig[b], gate[b]], [gate[b]])
            for c in range(4):
                x_ = xs[b][c]
                P.dma("pool", None, None, reads=[idx[b]], writes=[x_], semof=x_,
                      fn=lambda en, o=x_[:, :], ia=idx[b][:, c:c + 1]: en.indirect_dma_start(
                          out=o, out_offset=None, in_=h2flat, in_offset=bass.IndirectOffsetOnAxis(ap=ia, axis=0)))
            if e == 0:
                wts[0] = (wload("exp_w1", 0, wb[0]), wload("exp_w3", 0, wb[1]))
                wts["w2", 0] = wload("exp_w2", 0, wb[4])

        def wload(nm, e, w):
            for hk in range(2):
                P.dma("pool", w[:, hk * 4:(hk + 1) * 4, :],
                      getattr(G, nm)[l, e, hk * 512:(hk + 1) * 512, :].rearrange("(kt p) f -> p kt f", p=128),
                      writes=[w] if hk == 0 else [], pwrites=[] if hk == 0 else [w], semof=w)
            return w

        def compute(e):
            b = e % 2
            w1, w3 = wts.pop(e)
            w2 = wts.pop(("w2", e))
            xT = xsT[b]
            nxt = prep1_ops(e + 1) if e + 1 < NEXP else []
            if e + 1 < NEXP:
                e1 = e + 1
                wts[e1] = (wload("exp_w1", e1, wb[(e1 % 2) * 2]), wload("exp_w3", e1, wb[(e1 % 2) * 2 + 1]))
            for c in range(4):
                x_ = xs[b][c]
                for half in range(2):
                    p = ps[st["pi"] % 6]; st["pi"] += 1
                    for jj in range(4):
                        k = half * 4 + jj
                        tr(P, p[:, jj * 128:(jj + 1) * 128], x_[:, k * 128:(k + 1) * 128], G.ident[:], [x_, G.ident], p, first=(jj == 0))
                    cp(P, "act" if half else "dve", xT[:, half * 4:half * 4 + 4, c * 128:(c + 1) * 128],
                       p[:].rearrange("p (j n) -> p j n", j=4), [p], [xT.k((c, half))])
            xk = [xT.k((c, half)) for c in range(4) for half in range(2)]
            per = (len(nxt) + 7) // 8
            for fo in range(8):
                pa = ps[st["pi"] % 6]; st["pi"] += 1
                pg = ps[st["pi"] % 6]; st["pi"] += 1
                for k in range(8):
                    mm(P, pa[:], w1[:, k, fo * 128:(fo + 1) * 128], xT[:, k, :], k == 0, k == 7, [w1] + xk, pa)
                for k in range(8):
                    mm(P, pg[:], w3[:, k, fo * 128:(fo + 1) * 128], xT[:, k, :], k == 0, k == 7, [w3] + xk, pg)
                tm = tmp[fo % 2]
                act(P, tm[:], pa[:], AF.Silu, [pa], [tm])
                tt(P, "dve", actT[:, fo, :], tm[:], pg[:], ALU.mult, [tm, pg], [actT.k(fo)])
                for fn in nxt[fo * per:(fo + 1) * per]:
                    fn()
            if e + 1 < NEXP:
                prep2(e + 1)
            atk = whole(actT, 8)
            for c in range(4):
                s_ = c // 2
                yb = ysb[c]
                for dh in range(2):
                    p = ps[st["pi"] % 6]; st["pi"] += 1
                    for f in range(8):
                        mm(P, p[:], actT[:, f, c * 128:(c + 1) * 128], w2[:, f, dh * 512:(dh + 1) * 512], f == 0, f == 7, [w2] + atk, p)
                    stt(P, yb[:, dh * 512:(dh + 1) * 512], p[:], gate[b][:, c:c + 1], g2[s_][:, dh * 512:(dh + 1) * 512], ALU.mult, ALU.mult,
                        [p, gate[b], g2[s_]], [yb.k(dh)])
                if c == 3 and e + 1 < NEXP:
                    wts["w2", e + 1] = wload("exp_w2", e + 1, wb[4])
                P.dma("pool", None, None, reads=[yb.k(0), yb.k(1), idx[b]], writes=[G.xres.k("sc")], semof=yb,
                      fn=lambda en, i_=yb[:, :], ia=idx[b][:, c:c + 1]: en.indirect_dma_start(
                          out=xrflat, out_offset=bass.IndirectOffsetOnAxis(ap=ia, axis=0), in_=i_, in_offset=None, compute_op=ALU.add))

        for fn in prep1_ops(0):
            fn()
        prep2(0)
        for e in range(NEXP):
            compute(e)
    P.barrier()


def load_win(P, G, loc, c0, c1):
    n = (c1 - c0) // 512
    win = P.sbuf("win", [128, 8, c1 - c0], BF16, loc)
    for j in range(n):
        P.dma("pool", win[:, :, j * 512:(j + 1) * 512], G.w_in[0, :, c0 + j * 512:c0 + (j + 1) * 512].rearrange("(kt p) f -> p kt f", p=128),
              writes=[win.k(j)], semof=win)
    return win, whole(win, n)


def lru_part(P, G, win, wink, hT, ntok, h0, ya_s=None, hfin=None):
    nq = (ntok + 511) // 512
    qs = min(512, ntok)
    with ExitStack() as loc:
        sets = [{n: P.sbuf(n + str(z), [128, ntok + 4], F32, loc) for n in ("xr", "xa", "rr", "ii", "tmp", "hf", "hb")} for z in range(2)]
        xabs = [P.sbuf(f"xab{z}", [128, ntok], BF16, loc) for z in range(2)]
        yab = [P.sbuf(f"yab{i}", [128, ntok], BF16, loc) for i in range(2)]
        ps = G.ps
        st = {"pi": 0}

        def bank():
            p = ps[st["pi"] % 8]
            st["pi"] += 1
            return p

        for z in range(2):
            P.op("dve", lambda e, b_=sets[z]["xr"]: e.memset(b_[:], 0.0), writes=[sets[z]["xr"]])

        def chain(ci):
            xr, xa, rr, ii, tmp, hf, hb = (sets[ci % 2][n] for n in ("xr", "xa", "rr", "ii", "tmp", "hf", "hb"))
            xab = xabs[ci % 2]
            for q in range(nq):
                p = bank()
                for k in range(8):
                    mm(P, p[:, 0:qs], win[:, k, 512 + ci * 128:512 + (ci + 1) * 128], hT[:, k, q * 512:q * 512 + qs], k == 0, k == 7, [hT] + wink, p)
                cp(P, "act", xr[:, 1 + q * 512:1 + q * 512 + qs], p[:, 0:qs], [p], [xr])
                yield
            ts(P, "dve", xa[:, 0:ntok], xr[:, 0:ntok], G.caw[:, ci, 0:1], ALU.mult, [xr, G.caw], [xa], s2=G.cab[:, ci:ci + 1], op1=ALU.add)
            yield
            for j in range(1, 4):
                stt(P, xa[:, 0:ntok], xr[:, j:j + ntok], G.caw[:, ci, j:j + 1], xa[:, 0:ntok], ALU.mult, ALU.add, [xr, xa, G.caw], [xa])
                yield
            cp(P, "act", xab[:], xa[:, 0:ntok], [xa], [xab])
            yield
            for k in range(2):
                for gi, (dst, bias) in enumerate(((rr, G.lbr), (ii, G.lbi))):
                    for q in range(nq):
                        p = bank()
                        mm(P, p[:, 0:qs], G.bd[:, (gi * 2 + k) * 4 + ci, :], xab[:, q * 512:q * 512 + qs], True, True, [G.bd, xab], p)
                        act(P, dst[:, q * 512:q * 512 + qs], p[:, 0:qs], AF.Sigmoid, [p, bias], [dst], bias=bias[:, k * 4 + ci:k * 4 + ci + 1], scale=1.0)
                        yield
                act(P, tmp[:, 0:ntok], rr[:, 0:ntok], AF.Exp, [rr, G.spc2], [tmp], scale=G.spc2[:, k * 4 + ci:k * 4 + ci + 1])
                yield
                act(P, rr[:, 0:ntok], rr[:, 0:ntok], AF.Exp, [rr, G.spc], [rr], scale=G.spc[:, k * 4 + ci:k * 4 + ci + 1])
                yield
                act(P, tmp[:, 0:ntok], tmp[:, 0:ntok], AF.Sqrt, [tmp], [tmp], scale=-1.0, bias=G.onec[:, 0:1])
                yield
                tt(P, "dve", ii[:, 0:ntok], ii[:, 0:ntok], tmp[:, 0:ntok], ALU.mult, [ii, tmp], [ii])
                yield
                tt(P, "dve", ii[:, 0:ntok], ii[:, 0:ntok], xa[:, 0:ntok], ALU.mult, [ii, xa], [ii])
                yield
                init = 0.0 if h0 is None else h0[:, ci, k:k + 1]
                rds = [rr, ii] + ([] if h0 is None else [h0])
                if k == 0:
                    P.op("dve", lambda e, o=hf[:, 0:ntok], a=rr[:, 0:ntok], u=ii[:, 0:ntok], i0=init:
                         e.tensor_tensor_scan(out=o, data0=a, data1=u, initial=i0, op0=ALU.mult, op1=ALU.add), reads=rds, writes=[hf])
                else:
                    P.op("dve", lambda e, o=hb[:, 0:ntok][:, ::-1], a=rr[:, 0:ntok][:, ::-1], u=ii[:, 0:ntok][:, ::-1], i0=init:
                         e.tensor_tensor_scan(out=o, data0=a, data1=u, initial=i0, op0=ALU.mult, op1=ALU.add), reads=rds, writes=[hb])
                yield
            if hfin is not None:
                cp(P, "dve", hfin[:, ci, 0:1], hf[:, ntok - 1:ntok], [hf], [hfin.k((ci, 0))])
                cp(P, "dve", hfin[:, ci, 1:2], hb[:, 0:1], [hb], [hfin.k((ci, 1))])
                yield
            if ya_s is not None:
                tt(P, "dve", hf[:, 0:ntok], hf[:, 0:ntok], hb[:, 0:ntok], ALU.add, [hf, hb], [hf])
                yield
                for q in range(nq):
                    p = bank()
                    for k in range(8):
                        mm(P, p[:, 0:qs], win[:, k, ci * 128:(ci + 1) * 128], hT[:, k, q * 512:q * 512 + qs], k == 0, k == 7, [hT] + wink, p)
                    act(P, tmp[:, q * 512:q * 512 + qs], p[:, 0:qs], AF.Gelu_apprx_tanh, [p], [tmp])
                    yield
                yb = yab[ci % 2]
                tt(P, "dve", yb[:], tmp[:, 0:ntok], hf[:, 0:ntok], ALU.mult, [tmp, hf], [yb])
                P.dma("sp", G.yad[ya_s, ci], yb[:], reads=[yb], writes=[G.yad.k((ya_s, ci))], semof=yb)
                yield

        for pair in ((0, 1), (2, 3)):
            gens = [chain(ci) for ci in pair]
            while gens:
                for g_ in list(gens):
                    try:
                        next(g_)
                    except StopIteration:
                        gens.remove(g_)
    P.barrier()


def hy_inproj(P, G, win, wink, hT, s):
    with ExitStack() as loc:
        xr = [P.sbuf(f"hxr{i}", [128, SEQ + 2], F32, loc) for i in range(2)]
        t0 = P.sbuf("hyt", [128, SEQ], F32, loc)
        ob = [P.sbuf(f"hyo{i}", [128, SEQ], BF16, loc) for i in range(2)]
        v_tm = P.sbuf("vtm", [128, NT, 512], BF16, loc)
        ps = G.ps
        pi = 0
        for b in xr:
            P.op("dve", lambda e, b=b: e.memset(b[:], 0.0), writes=[b])
        for fi in range(12):
            part, ci = fi // 4, fi % 4
            x = xr[fi % 2]
            for q in range(4):
                p = ps[pi % 4]; pi += 1
                for k in range(8):
                    mm(P, p[:], win[:, k, fi * 128:(fi + 1) * 128], hT[:, k, q * 512:(q + 1) * 512], k == 0, k == 7, [hT] + wink, p)
                cp(P, "act", x[:, 1 + q * 512:1 + (q + 1) * 512], p[:], [p], [x])
            o = ob[fi % 2]
            ts(P, "dve", t0[:], x[:, 0:SEQ], G.cbw[:, fi, 0:1], ALU.mult, [x, G.cbw, G.cbb], [t0], s2=G.cbb[:, fi:fi + 1], op1=ALU.add)
            stt(P, t0[:], x[:, 1:1 + SEQ], G.cbw[:, fi, 1:2], t0[:], ALU.mult, ALU.add, [x, t0, G.cbw], [t0])
            stt(P, t0[:], x[:, 2:2 + SEQ], G.cbw[:, fi, 2:3], t0[:], ALU.mult, ALU.add, [x, t0, G.cbw], [t0])
            cp(P, "act", o[:], t0[:], [t0], [o])
            P.dma("sp", G.hyd[s, part, ci], o[:], reads=[o], writes=[G.hyd.k((s, part, ci))], semof=o)
            if part == 0:
                for g in range(4):
                    p = ps[4 + g]
                    for jj in range(4):
                        tt_ = g * 4 + jj
                        tr(P, p[:, jj * 128:(jj + 1) * 128], t0[:, tt_ * 128:(tt_ + 1) * 128], G.ident[:], [t0, G.ident], p, first=(jj == 0))
                    cp(P, "dve", v_tm[:, g * 4:(g + 1) * 4, ci * 128:(ci + 1) * 128], p[:].rearrange("p (j c) -> p j c", j=4), [p], [v_tm.k((g, ci))])
        P.dma("sp", G.vtm[s], v_tm[:], reads=[v_tm.k((g, ci)) for g in range(4) for ci in range(4)], writes=[G.vtm.k(s)], semof=v_tm)
    P.barrier()


def bg_step(G):
    if getattr(G, "bg", None):
        G.bg.pop(0)()


def hy_filters(P, G):
    N2 = 2 * SEQ
    with ExitStack() as loc:
        zp = P.sbuf("zp", [33, SEQ + 1], F32, loc)
        w1 = P.sbuf("fw1", [33, 64], F32, loc)
        w2 = P.sbuf("fw2", [64, 64], F32, loc)
        w3 = P.sbuf("fw3", [65, 2048], F32, loc)
        h1 = P.sbuf("fh1", [64, SEQ + 1], F32, loc)
        h2 = P.sbuf("fh2", [65, SEQ + 1], F32, loc)
        rtmp = P.sbuf("rtmp", [64, 512], F32, loc)
        fc = P.sbuf("fc", [64, 8], F32, loc)
        ke = P.sbuf("ke", [128, NT, 2, 512], BF16, loc)
        ko = P.sbuf("ko", [128, NT, 2, 512], BF16, loc)
        wn = [P.sbuf(f"wn{i}", [128, 2, 512], F32, loc) for i in range(2)]
        ft = [P.sbuf(f"ftmp{i}", [128, 2, 512], F32, loc) for i in range(2)]
        cf = [P.sbuf(f"cff{i}", [128, 2, NT, 128], BF16, loc) for i in range(2)]
        kt = [P.sbuf(f"ktb{i}", [128, 2, 512], F32, loc) for i in range(2)]
        k2 = [P.sbuf(f"kt2{i}", [128, 2, 512], F32, loc) for i in range(2)]
        ps = G.ps
        P.dma("sp", zp[:], G.zposT[:], writes=[zp], semof=zp)
        P.dma("sp", w1[:], G.filt_w1[0], writes=[w1], semof=w1)
        P.dma("sp", w2[:], G.filt_w2[0], writes=[w2], semof=w2)
        P.dma("sp", w3[0:64, :], G.filt_w3[0], writes=[w3.k(0)], semof=w3)
        P.dma("sp", w3[64:65, :], G.filt_b3[0:1, :], writes=[w3.k(1)], semof=w3)
        P.dma("sp", fc[:, 0:4], G.filt_c[:], writes=[fc], semof=fc)
        tt(P, "dve", fc[:, 4:6], fc[:, 0:2], fc[:, 2:4], ALU.mult, [fc], [fc.k(1)])
        P.op("dve", lambda e: e.memset(h2[:], 1.0), writes=[h2])
        TWO_PI = 2.0 * math.pi

        def sin_layer(dst, src_ps, fcol, bcol, rd):
            act(P, dst, src_ps, AF.Identity, rd + [fc, fc.k(1)], [h1 if dst is not None else h1], scale=fc[:, fcol:fcol + 1], bias=fc[:, bcol:bcol + 1])

        for layer in range(2):
            src = zp if layer == 0 else h1
            wt = w1 if layer == 0 else w2
            dstT = h1 if layer == 0 else h2
            kk = 33 if layer == 0 else 64
            for q in range(5):
                bg_step(G)
                c0 = q * 512
                n = min(512, SEQ + 1 - c0)
                p = ps[q % 4]
                mm(P, p[0:64, 0:n], wt[0:kk, :], src[0:kk, c0:c0 + n], True, True, [wt, src], p)
                d = dstT[0:64, c0:c0 + n]
                act(P, d, p[0:64, 0:n], AF.Identity, [p, fc, fc.k(1)], [dstT], scale=fc[:, 2 + layer:3 + layer], bias=fc[:, 4 + layer:5 + layer])
                MAGIC = 12582912.0
                kk_ = rtmp[0:64, 0:n]
                ts(P, "dve", kk_, d, 1.0 / TWO_PI, ALU.mult, [dstT], [rtmp], s2=MAGIC, op1=ALU.add)
                ts(P, "dve", kk_, kk_, -MAGIC, ALU.add, [rtmp], [rtmp])
                stt(P, d, kk_, -TWO_PI, d, ALU.mult, ALU.add, [rtmp, dstT], [dstT])
                ts(P, "dve", d, d, -math.pi, ALU.max, [dstT], [dstT], s2=math.pi, op1=ALU.min)
                act(P, d, d, AF.Sin, [dstT], [dstT])
        P.op("dve", lambda e: e.memset(h2[:, SEQ:SEQ + 1], 0.0), writes=[h2])
        it = 0
        for lt in range(NT):
            for o in range(2):
                bg_step(G)
                w = wn[it % 2]
                f = ft[it % 2]
                it += 1
                P.dma("sp", w[:], G.win2[lt], writes=[w], semof=w)
                pf = ps[(it % 2) * 2]
                pb = ps[(it % 2) * 2 + 1]
                mm(P, pf[:], h2[:, lt * 128:(lt + 1) * 128], w3[:, o * 1024:o * 1024 + 512], True, True, [h2, w3.k(0), w3.k(1)], pf)
                mm(P, pb[:], h2[:, lt * 128 + 1:(lt + 1) * 128 + 1], w3[:, o * 1024 + 512:(o + 1) * 1024], True, True, [h2, w3.k(0), w3.k(1)], pb)
                tt(P, "dve", f[:, 0, :], pf[:], w[:, 0, :], ALU.mult, [pf, w], [f.k(0)])
                tt(P, "dve", f[:, 1, :], pb[:], w[:, 1, :], ALU.mult, [pb, w], [f.k(1)])
                tt(P, "dve", ke[:, lt, o, :], f[:, 0, :], f[:, 1, :], ALU.add, [f.k(0), f.k(1)], [ke.k((lt, o))])
                tt(P, "dve", ko[:, lt, o, :], f[:, 1, :], f[:, 0, :], ALU.subtract, [f.k(0), f.k(1)], [ko.k((lt, o))])
        kek = [ke.k((lt, o)) for lt in range(NT) for o in range(2)]
        kok = [ko.k((lt, o)) for lt in range(NT) for o in range(2)]
        for fj in range(NT):
            c = cf[fj % 2]
            P.dma("sp", c[:], G.csf[fj], writes=[c], semof=c)
            for o in range(2):
                bg_step(G)
                pr = ps[4]
                pi_ = ps[5]
                k_ = kt[o]
                k2_ = k2[o]
                for lt in range(NT):
                    mm(P, pr[:], c[:, 0, lt, :], ke[:, lt, o, :], lt == 0, lt == NT - 1, [c] + kek, pr)
                for lt in range(NT):
                    mm(P, pi_[:], c[:, 1, lt, :], ko[:, lt, o, :], lt == 0, lt == NT - 1, [c] + kok, pi_)
                ts(P, "dve", k2_[:, 0, :], pi_[:], G.rot[:, fj, 1:2], ALU.mult, [pi_, G.rot], [k2_.k(0)])
                ts(P, "dve", k2_[:, 1, :], pi_[:], G.rot[:, fj, 0:1], ALU.mult, [pi_, G.rot], [k2_.k(1)])
                stt(P, k_[:, 0, :], pr[:], G.rot[:, fj, 0:1], k2_[:, 0, :], ALU.mult, ALU.subtract, [pr, G.rot, k2_.k(0)], [k_.k(0)])
                stt(P, k_[:, 1, :], pr[:], G.rot[:, fj, 1:2], k2_[:, 1, :], ALU.mult, ALU.add, [pr, G.rot, k2_.k(1)], [k_.k(1)])
                P.dma("sp", G.ktab[o, fj], k_[:], reads=[k_.k(0), k_.k(1)], writes=[G.ktab.k((o, fj))], semof=k_)
        while G.bg:
            bg_step(G)
    P.barrier()


def hy_conv(P, G, o, z_tm, zT, xgT, want_tm):
    with ExitStack() as loc:
        cf = [P.sbuf(f"cf{i}", [128, 2, NT, 128], BF16, loc) for i in range(2)]
        kt = [P.sbuf(f"kt{i}", [128, 2, 512], F32, loc) for i in range(2)]
        Y = P.sbuf("Y", [128, NT, 2, 512], BF16, loc)
        tmp = [P.sbuf(f"yt{i}", [128, 4, 512], F32, loc) for i in range(2)]
        ci_ = [P.sbuf(f"ci{i}", [128, NT, 512], BF16, loc) for i in range(2)]
        ps = G.ps
        ztk = [z_tm.k((g, ci)) for g in range(4) for ci in range(4)]
        for fj in range(NT):
            c = cf[fj % 2]
            k_ = kt[fj % 2]
            t_ = tmp[fj % 2]
            P.dma("sp", c[:], G.csf[fj], writes=[c], semof=c)
            P.dma("act", k_[:], G.ktab[o, fj], writes=[k_], semof=k_)
            pr = ps[(fj % 2) * 2]
            pq = ps[(fj % 2) * 2 + 1]
            for tt_ in range(NT):
                mm(P, pr[:], c[:, 0, tt_, :], z_tm[:, tt_, :], tt_ == 0, tt_ == NT - 1, [c] + ztk, pr)
            for tt_ in range(NT):
                mm(P, pq[:], c[:, 1, tt_, :], z_tm[:, tt_, :], tt_ == 0, tt_ == NT - 1, [c] + ztk, pq)
            tt(P, "dve", t_[:, 0, :], pr[:], k_[:, 0, :], ALU.mult, [pr, k_], [t_.k(0)])
            tt(P, "dve", t_[:, 1, :], pq[:], k_[:, 1, :], ALU.mult, [pq, k_], [t_.k(1)])
            tt(P, "dve", t_[:, 2, :], pq[:], k_[:, 0, :], ALU.mult, [pq, k_], [t_.k(2)])
            tt(P, "dve", t_[:, 3, :], pr[:], k_[:, 1, :], ALU.mult, [pr, k_], [t_.k(3)])
            tt(P, "dve", Y[:, fj, 0, :], t_[:, 0, :], t_[:, 1, :], ALU.add, [t_.k(0), t_.k(1)], [Y.k(fj)])
            tt(P, "dve", Y[:, fj, 1, :], t_[:, 2, :], t_[:, 3, :], ALU.subtract, [t_.k(2), t_.k(3)], [Y.k((fj, 1))])
        Yk = [Y.k(fj) for fj in range(NT)] + [Y.k((fj, 1)) for fj in range(NT)]
        for tq in range(4):
            for cs in range(2):
                P.dma("sp", ci_[cs][:], G.csi[tq, cs], writes=[ci_[cs]], semof=ci_[cs])
            for ci in range(4):
                p = ps[4 + ci]
                n = 0
                for cs in range(2):
                    for fj in range(NT):
                        mm(P, p[:], Y[:, fj, cs, ci * 128:(ci + 1) * 128], ci_[cs][:, fj, :], n == 0, n == 2 * NT - 1, Yk + ci_, p)
                        n += 1
                t_ = tmp[ci % 2]
                sl = slice(tq * 512, (tq + 1) * 512)
                zk = zT.k((ci, tq))
                stt(P, t_[:, 0, :], zT[:, ci, sl], G.fbias[:, o * 4 + ci:o * 4 + ci + 1], p[:], ALU.mult, ALU.add, [zk, G.fbias, p], [t_.k(0)])
                tt(P, "dve", t_[:, 1, :], t_[:, 0, :], xgT[:, ci, sl], ALU.mult, [t_.k(0), xgT], [t_.k(1)])
                cp(P, "act", zT[:, ci, sl], t_[:, 1, :], [t_.k(1)], [zk])
                if want_tm:
                    pt = ps[ci % 4]
                    for jj in range(4):
                        tr(P, pt[:, jj * 128:(jj + 1) * 128], t_[:, 1, jj * 128:(jj + 1) * 128], G.ident[:], [t_.k(1), G.ident], pt, first=(jj == 0))
                    cp(P, "dve", z_tm[:, tq * 4:(tq + 1) * 4, ci * 128:(ci + 1) * 128], pt[:].rearrange("p (j c) -> p j c", j=4), [pt], [z_tm.k((tq, ci))])
    P.barrier()


def mixer_out(P, G, l, s, kT_list, w_ap, bias_ap, igate, src=None):
    with ExitStack() as loc:
        w = P.sbuf("wout", [128, 8, D], BF16, loc)
        g1 = P.sbuf("g1", [128, D], F32, loc)
        xt = [P.sbuf(f"ox{i}", [128, D], F32, loc) for i in range(4)]
        tmp = [P.sbuf(f"ot{i}", [128, D], F32, loc) for i in range(2)]
        ps = G.ps
        for hk in range(2):
            P.dma("pool", w[:, hk * 4:(hk + 1) * 4, :], w_ap[hk * 512:(hk + 1) * 512, :].rearrange("(kt p) f -> p kt f", p=128),
                  writes=[w.k(hk)], semof=w)
        wk = whole(w, 2)
        if bias_ap is not None:
            brow = P.sbuf("brow", [1, D], BF16, loc)
            P.dma("pool", brow[:], bias_ap, writes=[brow], semof=brow)
        P.dma("sp", g1[:], G.mrow[l, s:s + 1, igate * D:(igate + 1) * D].broadcast_to([128, D]), writes=[g1], semof=g1)
        xsrc_ = src if src is not None else G.xres

        def xload(t):
            P.dma("sp", xt[t % 4][:], xsrc_[s, t * 128:(t + 1) * 128, :], writes=[xt[t % 4]], semof=xt[t % 4])

        xload(0)
        xload(1)
        for t in range(NT):
            if t + 2 < NT:
                xload(t + 2)
            x = xt[t % 4]
            tm = tmp[t % 2]
            for dh in range(2):
                p = ps[(t % 4) * 2 + dh]
                for k in range(8):
                    kt_, idx = kT_list[k]
                    mm(P, p[:], kt_[:, idx, t * 128:(t + 1) * 128], w[:, k, dh * 512:(dh + 1) * 512], k == 0,
                       (k == 7 and bias_ap is None), [kt_] + wk, p)
                if bias_ap is not None:
                    mm(P, p[:], G.onesb[0:1, :], brow[0:1, dh * 512:(dh + 1) * 512], False, True, [G.onesb, brow], p)
                tt(P, "dve", tm[:, dh * 512:(dh + 1) * 512], p[:], g1[:, dh * 512:(dh + 1) * 512], ALU.mult, [p, g1], [tm.k(dh)])
            tt(P, "dve", x[:], x[:], tm[:], ALU.add, [x, tm.k(0), tm.k(1)], [x])
            P.dma("sp", G.xres[s, t * 128:(t + 1) * 128, :], x[:], reads=[x], writes=[G.xres.k((s, t))], semof=x)
    P.barrier()


def conformer(P, G, s, hT, sT):
    PADC = 15 * 64
    with ExitStack() as loc:
        u = P.sbuf("cfu", [128, 8, SEQ], F32, loc)
        with ExitStack() as l1:
            w1 = P.sbuf("cfw1", [128, 8, 2 * D], BF16, l1)
            gl = [P.sbuf(f"cfgl{i}", [128, SEQ + 2 * PADC], BF16, l1) for i in range(2)]
            dg = [P.sbuf(f"cfdg{i}", [128, 31, 128], BF16, l1) for i in range(2)]
            sg = [P.sbuf(f"cfsg{i}", [128, 512], F32, l1) for i in range(2)]
            ps = G.ps
            for j in range(4):
                P.dma("pool", w1[:, :, j * 512:(j + 1) * 512], G.cf_w1[0, :, j * 512:(j + 1) * 512].rearrange("(kt p) f -> p kt f", p=128),
                      writes=[w1.k(j)], semof=w1)
            w1k = whole(w1, 4)
            for b in gl:
                P.op("dve", lambda e, b=b: e.memset(b[:], 0.0), writes=[b])
            pi = 0
            for ci in range(8):
                g_ = gl[ci % 2]
                d_ = dg[ci % 2]
                for j in range(31):
                    ts(P, "dve", d_[:, j, :], G.ident[:], G.cfdw[:, ci, j:j + 1], ALU.mult, [G.ident, G.cfdw], [d_.k(j)])
                for q in range(4):
                    pa = ps[pi % 4]; pi += 1
                    pg = ps[pi % 4]; pi += 1
                    s_ = sg[q % 2]
                    for k in range(8):
                        mm(P, pa[:], w1[:, k, ci * 128:(ci + 1) * 128], hT[:, k, q * 512:(q + 1) * 512], k == 0, k == 7, [hT] + w1k, pa)
                    for k in range(8):
                        mm(P, pg[:], w1[:, k, D + ci * 128:D + (ci + 1) * 128], hT[:, k, q * 512:(q + 1) * 512], k == 0, k == 7, [hT] + w1k, pg)
                    act(P, s_[:], pg[:], AF.Sigmoid, [pg, G.cfb1], [s_], bias=G.cfb1[:, 8 + ci:9 + ci], scale=1.0)
                    stt(P, g_[:, PADC + q * 512:PADC + (q + 1) * 512], pa[:], G.cfb1[:, ci:ci + 1], s_[:], ALU.add, ALU.mult, [pa, G.cfb1, s_], [g_.k(q)])
                gk = whole(g_, 4)
                dk = whole(d_, 31)
                for q in range(4):
                    pc = ps[4 + q]
                    taps = [j for j in range(31) if 64 * j + 512 * q + 512 > PADC and 64 * j + 512 * q < PADC + SEQ]
                    for n, j in enumerate(taps):
                        c0 = 64 * j + 512 * q
                        mm(P, pc[:], d_[:, j, :], g_[:, c0:c0 + 512], n == 0, n == len(taps) - 1, gk + dk, pc)
                    act(P, u[:, ci, q * 512:(q + 1) * 512], pc[:], AF.Identity, [pc, G.cfdb], [u.k(ci)] if q == 0 else [], bias=G.cfdb[:, ci:ci + 1], scale=1.0) if q == 0 else \
                        P.op("act", lambda e, o=u[:, ci, q * 512:(q + 1) * 512], i=pc[:], b=G.cfdb[:, ci:ci + 1]: e.activation(out=o, in_=i, func=AF.Identity, bias=b, scale=1.0),
                             reads=[pc, G.cfdb], pwrites=[u.k(ci)])
        P.barrier()
        with ExitStack() as l2:
            sq = [P.sbuf(f"cfsq{i}", [128, 512], F32, l2) for i in range(2)]
            mean = P.sbuf("cfmean", [128, 512], F32, l2)
            rstd = P.sbuf("cfrstd", [128, 512], F32, l2)
            t1 = [P.sbuf(f"cft{i}", [128, 512], F32, l2) for i in range(2)]
            ps = G.ps
            uk = whole(u, 8)
            for q in range(4):
                sl = slice(q * 512, (q + 1) * 512)
                pm = ps[(q % 2) * 2]
                pv = ps[(q % 2) * 2 + 1]
                for ci in range(8):
                    mm(P, pm[:], G.onesm[:], u[:, ci, sl], ci == 0, ci == 7, [G.onesm] + uk, pm)
                for ci in range(8):
                    s_ = sq[ci % 2]
                    act(P, s_[:], u[:, ci, sl], AF.Square, uk, [s_])
                    mm(P, pv[:], G.onesm[:], s_[:], ci == 0, ci == 7, [G.onesm, s_], pv)
                cp(P, "act", mean[:], pm[:], [pm], [mean])
                tt(P, "dve", rstd[:], mean[:], mean[:], ALU.mult, [mean], [rstd])
                tt(P, "dve", rstd[:], pv[:], rstd[:], ALU.subtract, [pv, rstd], [rstd])
                act(P, rstd[:], rstd[:], AF.Sqrt, [rstd], [rstd], scale=1.0, bias=G.epsc[:, 0:1])
                P.op("dve", lambda e: e.reciprocal(out=rstd[:], in_=rstd[:]), reads=[rstd], writes=[rstd])
                for ci in range(8):
                    t_ = t1[ci % 2]
                    tt(P, "dve", t_[:], u[:, ci, sl], mean[:], ALU.subtract, uk + [mean], [t_])
                    tt(P, "dve", t_[:], t_[:], rstd[:], ALU.mult, [t_, rstd], [t_])
                    act(P, sT[:, ci, sl], t_[:], AF.Silu, [t_, G.cfln], [sT.k((ci, q))], scale=G.cfln[:, ci:ci + 1], bias=G.cfln[:, 8 + ci:9 + ci])
    P.barrier()


INPUT_SPECS = [
    ("x", [NS, SEQ, D], F32), ("ctx", [NS, CTX, D], F32), ("csT", [128, 8, 3], F32),
    ("w_mod", [2, D, 6 * D], F32), ("b_mod", [2, 6 * D], F32), ("norm1_g", [2, D], F32), ("norm2_g", [2, D], F32),
    ("final_g", [1, D], F32), ("ident_d", [128, 128], F32), ("iota_d", [128, CAP], F32), ("npos_d", [128, NT, 2], F32),
    ("w_in", [1, D, 2560], F32), ("caw_d", [128, 4, 4], F32), ("cab_d", [128, 4], F32),
    ("lbr_d", [128, 8], F32), ("lbi_d", [128, 8], F32), ("lam_d", [128, 8], F32),
    ("lru_wr", [1, 2, 8, 64, 64], F32), ("lru_wi", [1, 2, 8, 64, 64], F32),
    ("cbw_d", [128, 12, 3], F32), ("cbb_d", [128, 12], F32), ("fbias_d", [128, 8], F32), ("filt_c", [64, 4], F32),
    ("filt_w1", [1, 33, 64], F32), ("filt_w2", [1, 64, 64], F32), ("filt_w3", [1, 64, 2048], F32), ("filt_b3", [1, 2048], F32),
    ("zposT", [33, SEQ + 1], F32), ("win2", [NT, 128, 2, 512], F32), ("rot_d", [128, NT, 2], F32),
    ("csf", [NT, 128, 2, NT, 128], BF16), ("csi", [4, 2, 128, NT, 512], BF16),
    ("w_out", [1, D, D], F32),
    ("cf_w1", [1, D, 2 * D], F32), ("cfb1_d", [128, 16], F32), ("cfdw_d", [128, 8, 31], F32), ("cfdb_d", [128, 8], F32),
    ("cfln_d", [128, 16], F32), ("cf_w2", [1, D, D], F32), ("cf_b2", [1, D], F32),
    ("router_d", [2, 128, 8, NEXP], F32),
    ("exp_w1", [2, NEXP, D, D], F32), ("exp_w3", [2, NEXP, D, D], F32), ("exp_w2", [2, NEXP, D, D], F32),
]


def declare(nc, P, G, dbg):
    for name, shape, dt in INPUT_SPECS:
        setattr(G, name, T(name, nc.dram_tensor(name, list(shape), dt, kind="ExternalInput").ap()))
    kind = "ExternalOutput" if dbg else "Internal"
    G.mrow = P.dram("mrow", [2, 3, 6 * D], F32, kind)
    G.xres = P.dram("xres", [NS, SEQ, D], F32, kind)
    G.ktab = P.dram("ktab", [2, NT, 128, 2, 512], F32, kind)
    G.hyd = P.dram("hyd", [NS, 3, 4, 128, SEQ], BF16, kind)
    G.vtm = P.dram("vtm", [NS, 128, NT, 512], BF16, kind)
    G.yad = P.dram("yad", [NS, 4, 128, SEQ], BF16, kind)
    G.h2d = P.dram("h2d", [NS, SEQ, D], F32, kind)
    G.out = P.dram("out", [NS, SEQ, D], F32, "ExternalOutput")
    if dbg:
        G.dbg_ig = P.dram("dbg_ig", [NEXP, 128, 4, 4], F32, kind)
        G.dbg_xs = P.dram("dbg_xs", [4, 128, D], F32, kind)
        G.dbg_idx = P.dram("dbg_idx", [128, 4], mybir.dt.int32, kind)


def setup_consts(P, G):
    def ld(name, src, shape, dt=F32, q="sp"):
        t = P.sbuf(name, shape, dt)
        P.dma(q, t[:], src[:], writes=[t], semof=t)
        setattr(G, name, t)
        return t

    ld("ident", G.ident_d, [128, 128]); ld("iota", G.iota_d, [128, CAP]); ld("npos", G.npos_d, [128, NT, 2])
    ld("caw", G.caw_d, [128, 4, 4]); ld("cab", G.cab_d, [128, 4]); ld("lbr", G.lbr_d, [128, 8]); ld("lbi", G.lbi_d, [128, 8])
    lam = ld("lam", G.lam_d, [128, 8])
    ld("cbw", G.cbw_d, [128, 12, 3]); ld("cbb", G.cbb_d, [128, 12]); ld("fbias", G.fbias_d, [128, 8])
    ld("rot", G.rot_d, [128, NT, 2])
    ld("cfb1", G.cfb1_d, [128, 16]); ld("cfdw", G.cfdw_d, [128, 8, 31]); ld("cfdb", G.cfdb_d, [128, 8]); ld("cfln", G.cfln_d, [128, 16])
    G.epsc = P.sbuf("epsc", [128, 1], F32)
    G.onec = P.sbuf("onec", [128, 1], F32)
    G.onesm = P.sbuf("onesm", [128, 128], F32)
    G.onesb = P.sbuf("onesb", [1, 128], BF16)
    G.spc = P.sbuf("spc", [128, 8], F32)
    G.bd = P.sbuf("bd", [128, 16, 128], BF16)
    P.op("dve", lambda e: e.memset(G.epsc[:], EPS), writes=[G.epsc])
    P.op("dve", lambda e: e.memset(G.onec[:], 1.0), writes=[G.onec])
    P.op("dve", lambda e: e.memset(G.onesm[:], 1.0 / D), writes=[G.onesm])
    P.op("dve", lambda e: e.memset(G.onesb[:], 1.0), writes=[G.onesb])
    P.op("dve", lambda e: e.memset(G.bd[:], 0.0), writes=[G.bd])
    ep = P.sbuf("ep", [128, 8], F32)
    t = P.sbuf("ept", [128, 8], F32)
    act(P, ep[:], lam[:], AF.Exp, [lam], [ep], scale=-1.0)
    ts(P, "dve", t[:], ep[:], 1.0 / 3.0, ALU.mult, [ep], [t], s2=-0.5, op1=ALU.add)
    tt(P, "dve", t[:], t[:], ep[:], ALU.mult, [t, ep], [t])
    ts(P, "dve", t[:], t[:], 1.0, ALU.add, [t], [t])
    tt(P, "dve", t[:], t[:], ep[:], ALU.mult, [t, ep], [t])
    ts(P, "dve", G.spc[:], t[:], -8.0, ALU.mult, [t], [G.spc])
    G.spc2 = P.sbuf("spc2", [128, 8], F32)
    ts(P, "dve", G.spc2[:], t[:], -16.0, ALU.mult, [t], [G.spc2])
    for gi, wsrc in enumerate((G.lru_wr, G.lru_wi)):
        for k in range(2):
            for ci in range(4):
                for hh in range(2):
                    idx = (gi * 2 + k) * 4 + ci
                    P.dma("pool", G.bd[hh * 64:(hh + 1) * 64, idx, hh * 64:(hh + 1) * 64], wsrc[0, k, ci * 2 + hh],
                          writes=[G.bd], semof=G.bd)
    P.barrier()


def build(dbg=False, upto=99):
    nc = bass.Bass("TRN2", target_bir_lowering=False)
    es = ExitStack()
    with es:
        P = Prog(nc, es)
        G = Ctx()
        declare(nc, P, G, dbg)
        G.ps = [P.psum(f"ps{i}", [128, 512], F32) for i in range(8)]
        setup_consts(P, G)
        ada_stack = ExitStack()
        G.bg = stage_adaln(P, G, ada_stack)
        if upto < 1:
            while G.bg:
                G.bg.pop(0)()
            P.barrier()
            ada_stack.close()

        def xsrc(tensor, s):
            return lambda t: ([], tensor[s, t * 128:(t + 1) * 128, :])

        def moe_layer(l, src):
            with ExitStack() as ml:
                code = P.sbuf("code", [128, NT, NS * NEXP], F32, ml)
                pg = P.sbuf("pg", [128, NT, NS * NEXP], F32, ml)
                with ExitStack() as ns:
                    probs = P.sbuf("probs", [128, NT, NS * NEXP], F32, ns)
                    rt = P.sbuf("rt", [128, 8, NEXP], F32, ns)
                    P.dma("sp", rt[:], G.router_d[l], writes=[rt], semof=rt)
                    norm_jobs(P, G, NT, [dict(src_tile=xsrc(src, s), g_ap=G.norm2_g[l:l + 1, :], l=l, row=s, ish=3, isc=4, hT=None, h_tm=s,
                                              probs=probs, rt=rt, pcol=s * NEXP) for s in range(NS)])
                    route_stage(P, G, probs, code, pg)
                if upto >= 4 + 10 * l:
                    expert_stage(P, G, l, code, pg, 5)

        if upto >= 1:
            hy_filters(P, G)
            ada_stack.close()
            for s in range(NS):
                with ExitStack() as la:
                    hTc = P.sbuf("hTc", [128, 8, CTX], BF16, la)
                    h0 = P.sbuf("h0", [128, 4, 2], F32, la)
                    hT = P.sbuf("hT", [128, 8, SEQ], BF16, la)
                    with ExitStack() as lw:
                        win, wink = load_win(P, G, lw, 0, 1024)
                        norm_stage(P, G, la, xsrc(G.ctx, s), CTX // 128, G.norm1_g[0:1, :], 0, 2, 0, 1, hT=hTc)
                        lru_part(P, G, win, wink, hTc, CTX, None, hfin=h0)
                        norm_stage(P, G, la, xsrc(G.x, s), NT, G.norm1_g[0:1, :], 0, s, 0, 1, hT=hT)
                        lru_part(P, G, win, wink, hT, SEQ, h0, ya_s=s)
                    with ExitStack() as lw:
                        win, wink = load_win(P, G, lw, 1024, 2560)
                        hy_inproj(P, G, win, wink, hT, s)
                if upto >= 2:
                    with ExitStack() as lb:
                        zT = P.sbuf("zT", [128, 4, SEQ], BF16, lb)
                        x1T = P.sbuf("x1T", [128, 4, SEQ], BF16, lb)
                        x2T = P.sbuf("x2T", [128, 4, SEQ], BF16, lb)
                        yaT = P.sbuf("yaT", [128, 4, SEQ], BF16, lb)
                        z_tm = P.sbuf("z_tm", [128, NT, 512], BF16, lb)
                        for ci in range(4):
                            P.dma("sp", zT[:, ci, :], G.hyd[s, 0, ci], writes=[zT.k((ci, 0))], semof=zT)
                            P.dma("sp", x1T[:, ci, :], G.hyd[s, 1, ci], pwrites=[x1T], semof=x1T)
                            P.dma("sp", x2T[:, ci, :], G.hyd[s, 2, ci], pwrites=[x2T], semof=x2T)
                            P.dma("sp", yaT[:, ci, :], G.yad[s, ci], pwrites=[yaT], semof=yaT)
                        P.dma("sp", z_tm[:], G.vtm[s], writes=[z_tm], semof=z_tm)
                        P.barrier()
                        hy_conv(P, G, 0, z_tm, zT, x1T, True)
                        hy_conv(P, G, 1, z_tm, zT, x2T, False)
                        kl = [(yaT, i) for i in range(4)] + [(zT, i) for i in range(4)]
                        mixer_out0(P, G, s, kl)
        def snap(name):
            if not dbg:
                return
            t = P.dram("snap_" + name, [NS, SEQ, D], F32, "ExternalOutput")
            for s in range(NS):
                P.dma("sp", t[s], G.xres[s], writes=[t.k(s)], semof=t)
            P.barrier()

        if upto >= 3:
            moe_layer(0, G.xres)
            snap("xl0")
        if upto >= 11:
            for s in range(NS):
                with ExitStack() as la:
                    hT = P.sbuf("hT1", [128, 8, SEQ], BF16, la)
                    norm_stage(P, G, la, xsrc(G.xres, s), NT, G.norm1_g[1:2, :], 1, s, 0, 1, hT=hT)
                    conformer(P, G, s, hT, hT)
                    mixer_out(P, G, 1, s, [(hT, i) for i in range(8)], G.cf_w2[0], G.cf_b2[0:1, :], 2)
            snap("xl1a")
        if upto >= 12:
            moe_layer(1, G.xres)
        if upto >= 20:
            final_norm(P, G)
        P.barrier()
        P.emit()
    return nc


def mixer_out0(P, G, s, kl):
    mixer_out(P, G, 0, s, kl, G.w_out[0], None, 2, src=G.x)


def final_norm(P, G):
    with ExitStack() as loc:
        NB, AHEAD = 6, 4
        gb = P.sbuf("fg", [128, D], F32, loc)
        xt = [P.sbuf(f"fx{i}", [128, D], F32, loc) for i in range(NB)]
        junks = [P.sbuf(f"fjunk{i}", [128, D], F32, loc) for i in range(2)]
        st = [P.sbuf(f"fst{i}", [128, 4], F32, loc) for i in range(NB)]
        P.dma("sp", gb[:], G.final_g[0:1, :].broadcast_to([128, D]), writes=[gb], semof=gb)
        items = [(t, s) for t in range(NT) for s in range(NS)]

        def load(i):
            t, s = items[i]
            x = xt[i % NB]
            P.dma("sp", x[:], G.xres[s, t * 128:(t + 1) * 128, :], writes=[x], semof=x)

        for i in range(AHEAD):
            load(i)
        for i, (t, s) in enumerate(items):
            if i + AHEAD < len(items):
                load(i + AHEAD)
            x = xt[i % NB]
            s_ = st[i % NB]
            junk = junks[i % 2]
            act(P, junk[:], x[:], AF.Square, [x], [junk, s_.k(0)], accum_out=s_[:, 0:1])
            act(P, s_[:, 1:2], s_[:, 0:1], AF.Sqrt, [s_.k(0)], [s_.k(1)], scale=1.0 / D, bias=G.epsc[:, 0:1])
            P.op("dve", lambda e, o=s_[:, 2:3], i_=s_[:, 1:2]: e.reciprocal(out=o, in_=i_), reads=[s_.k(1)], writes=[s_.k(2)])
            stt(P, x[:], x[:], s_[:, 2:3], gb[:], ALU.mult, ALU.mult, [x, s_.k(2), gb], [x])
            P.dma("sp", G.out[s, t * 128:(t + 1) * 128, :], x[:], reads=[x], writes=[G.out.k((s, t))], semof=x)
    P.barrier()


_CONST = {}


def host_consts():
    if _CONST:
        return _CONST
    import ml_dtypes
    bf = ml_dtypes.bfloat16
    n = SEQ
    N2 = 2 * n
    t = np.arange(n, dtype=np.float64)
    phi = 2 * np.pi * np.outer(t + 0.5, t + 0.5) / N2
    C = np.cos(phi); S = np.sin(phi)
    CS = np.stack([C, S])
    csf = CS.reshape(2, NT, 128, NT, 128).transpose(3, 2, 0, 1, 4)
    csi = CS.reshape(2, NT, 128, 4, 512).transpose(3, 0, 2, 1, 4)
    _CONST["csf"] = np.ascontiguousarray(csf).astype(bf)
    _CONST["csi"] = np.ascontiguousarray(csi).astype(bf)
    al = np.pi * (t + 0.5) / N2
    rot = np.stack([np.cos(al), np.sin(al)], -1) * (2.0 / N2)
    _CONST["rot_d"] = np.ascontiguousarray(rot.reshape(NT, 128, 2).transpose(1, 0, 2)).astype(np.float32)
    f32 = np.float32
    tt_ = np.linspace(0.0, 1.0, n, dtype=f32)[:, None]
    bands = 16
    w = (f32(2.0 * math.pi / n) * np.arange(n, dtype=f32))[:, None]
    fr = np.linspace(1e-4, bands - 1, bands, dtype=f32)[None, :]
    z = np.concatenate([tt_, np.cos(fr * w), -np.sin(fr * w)], axis=-1).astype(f32)
    zT = np.zeros((33, n + 1), f32); zT[:, :n] = z.T
    _CONST["zposT"] = zT
    deltas = np.abs(np.linspace(math.log(1e-2) / 1.5, math.log(1e-2) / 0.3, 512, dtype=f32))
    window = (np.exp(-tt_ * deltas[None, :]) + f32(0.05)).astype(f32)
    wsh = np.zeros_like(window); wsh[:-1] = window[1:]
    w2 = np.stack([window, wsh], 1)
    _CONST["win2"] = np.ascontiguousarray(w2.reshape(NT, 128, 2, 512)).astype(f32)
    _CONST["ident_d"] = np.eye(128, dtype=f32)
    _CONST["iota_d"] = np.tile(np.arange(CAP, dtype=f32)[None, :], (128, 1))
    npos = np.zeros((128, NT, 2), f32); npos[:, :, 0] = np.arange(NT)[None, :]; npos[:, :, 1] = np.arange(128)[:, None]
    _CONST["npos_d"] = npos
    return _CONST


def _cols(v, nt):
    return np.ascontiguousarray(np.asarray(v, np.float32).reshape(nt, 128).T)


def host_shared(inputs):
    f = lambda a: np.ascontiguousarray(np.asarray(a, dtype=np.float32))
    m = dict(host_consts())
    for k in ("w_mod", "b_mod", "norm1_g", "norm2_g", "w_in", "lru_wr", "lru_wi", "filt_w1", "filt_w2", "filt_w3", "filt_b3",
              "w_out", "cf_w1", "cf_w2", "cf_b2", "exp_w1", "exp_w3", "exp_w2"):
        m[k] = f(inputs[k])
    m["final_g"] = f(inputs["final_g"]).reshape(1, D)
    m["caw_d"] = np.ascontiguousarray(f(inputs["conv_a_w"])[0].reshape(4, 4, 128).transpose(2, 1, 0))
    m["cab_d"] = _cols(inputs["conv_a_b"][0], 4)
    m["lbr_d"] = _cols(np.asarray(inputs["lru_br"])[0].reshape(-1), 8)
    m["lbi_d"] = _cols(np.asarray(inputs["lru_bi"])[0].reshape(-1), 8)
    m["lam_d"] = _cols(np.asarray(inputs["lru_lam"])[0].reshape(-1), 8)
    m["cbw_d"] = np.ascontiguousarray(f(inputs["conv_b_w"])[0].reshape(3, 12, 128).transpose(2, 1, 0))
    m["cbb_d"] = _cols(inputs["conv_b_b"][0], 12)
    m["fbias_d"] = _cols(np.asarray(inputs["filt_bias"])[0].reshape(-1), 8)
    m["filt_c"] = np.ascontiguousarray(np.stack([f(inputs["filt_b1"])[0], f(inputs["filt_b2"])[0],
                                                 f(inputs["filt_freq"])[0, 0], f(inputs["filt_freq"])[0, 1]], -1))
    m["cfb1_d"] = _cols(inputs["cf_b1"][0], 16)
    m["cfdw_d"] = np.ascontiguousarray(f(inputs["cf_dw_w"])[0].reshape(31, 8, 128).transpose(2, 1, 0))
    m["cfdb_d"] = _cols(inputs["cf_dw_b"][0], 8)
    m["cfln_d"] = np.ascontiguousarray(np.concatenate([_cols(inputs["cf_ln_g"][0], 8), _cols(inputs["cf_ln_b"][0], 8)], 1))
    m["router_d"] = np.ascontiguousarray(f(inputs["router"]).reshape(2, 8, 128, NEXP).transpose(0, 2, 1, 3))
    return m


def host_prep(inputs, core, shared=None):
    f = lambda a: np.ascontiguousarray(np.asarray(a, dtype=np.float32))
    m = dict(shared if shared is not None else host_shared(inputs))
    s0 = core * NS
    cc = np.stack([np.asarray(inputs["c"])[s0], np.asarray(inputs["c"])[s0 + 1], np.asarray(inputs["c_ctx"])]).astype(np.float32)
    m["x"] = f(inputs["x"][s0:s0 + NS])
    m["ctx"] = f(inputs["ctx"][s0:s0 + NS])
    m["csT"] = f(cc.T.reshape(8, 128, 3).transpose(1, 0, 2))
    return m


def kernel(**inputs):
    nc = build()
    shared = host_shared(inputs)
    in_maps = [host_prep(inputs, c, shared) for c in range(NCORES)]
    res = run_bass_kernel_spmd(nc, in_maps, core_ids=list(range(NCORES)))
    return np.concatenate([np.asarray(r["out"], dtype=np.float32) for r in res.results], axis=0)
```

```python
from contextlib import ExitStack
import math
import numpy as np
import concourse.bass as bass
import concourse.mybir as mybir
from concourse.bass_utils import run_bass_kernel_spmd

F32 = mybir.dt.float32
BF16 = mybir.dt.bfloat16
ALU = mybir.AluOpType
AF = mybir.ActivationFunctionType

NCORES = 8
D = 1024
SEQ = 2048
NS = 2
NT = SEQ // 128
CTX = 256
EPS = 1e-6
NEXP = 16
CAP = 256


class Trk:
    def __init__(self, name, base=None):
        self.name = name
        self.writers = []
        self.readers = []
        self.base = base if base is not None else self
        self.subs = {}

    def k(self, key):
        s = self.subs.get(key)
        if s is None:
            s = Trk(f"{self.name}.{key}", self.base)
            self.subs[key] = s
        return s


class T(Trk):
    def __init__(self, name, h):
        super().__init__(name)
        self.h = h

    def __getitem__(self, idx):
        return self.h[idx]


ENGS = ("pe", "act", "dve", "pool", "sp")


class Prog:
    def __init__(self, nc, es):
        self.nc = nc
        self.es = es
        self.ops = {e: [] for e in ENGS}
        self.ecnt = {e: 0 for e in ENGS}
        self.seen = {e: {} for e in ENGS}
        self.sems = {}
        self.dcnt = {}
        self.nsem = 0
        self.uid = 0
        self.dkey = {}
        self.dfree = []
        self.dfree_sw = []
        self.dall = []
        self.dall_sw = []
        self.dpool = 0

    def _sem(self, key):
        s = self.sems.get(key)
        if s is None:
            self.nsem += 1
            s = self.es.enter_context(self.nc.semaphore(f"sem{self.nsem}"))
            self.sems[key] = s
        return s

    def sbuf(self, name, shape, dtype, es=None):
        es = es or self.es
        self.uid += 1
        h = es.enter_context(self.nc.sbuf_tensor(f"{name}_{self.uid}", list(shape), dtype))
        return T(name, h)

    def psum(self, name, shape, dtype, es=None):
        es = es or self.es
        self.uid += 1
        h = es.enter_context(self.nc.psum_tensor(f"{name}_{self.uid}", list(shape), dtype))
        return T(name, h)

    def dram(self, name, shape, dtype, kind="Internal"):
        h = self.nc.dram_tensor(name, list(shape), dtype, kind=kind)
        return T(name, h.ap())

    def _waits(self, eng, deps):
        need = {}
        for key, cnt in deps:
            if key[0] == "E" and key[1] == eng and eng == "pe":
                continue
            if key[0] == "D":
                cnt = self.dcnt[key]
            if cnt > need.get(key, 0):
                need[key] = cnt
        for key, cnt in need.items():
            if self.seen[eng].get(key, 0) >= cnt:
                continue
            self.seen[eng][key] = cnt
            self.ops[eng].append(("wait", key, cnt))

    def _deps(self, reads, writes, pwrites):
        deps = []
        for r in reads:
            deps += r.writers
        for w in writes:
            deps += w.writers
            deps += w.readers
        for w in pwrites:
            deps += w.readers
        return deps

    def _commit(self, ev, reads, writes, pwrites):
        for r in reads:
            r.readers.append(ev)
        for w in writes:
            w.writers = [ev]
            w.readers = []
        for w in pwrites:
            w.writers.append(ev)

    def op(self, eng, fn, reads=(), writes=(), pwrites=()):
        self._waits(eng, self._deps(reads, writes, pwrites))
        key = ("E", eng)
        self._sem(key)
        self.ecnt[eng] += 1
        ev = (key, self.ecnt[eng])
        self.ops[eng].append(("op", fn, key, 1))
        self._commit(ev, reads, writes, pwrites)

    def dma(self, q, out, in_, reads=(), writes=(), pwrites=(), semof=None, **kw):
        self._waits(q, self._deps(reads, writes, pwrites))
        sw = (q == "pool")
        key = self.dkey.get((id(semof.base), sw))
        if key is None:
            free = self.dfree_sw if sw else self.dfree
            if free:
                idx = free.pop()
            else:
                idx = (self.dpool, sw)
                self.dpool += 1
                (self.dall_sw if sw else self.dall).append(idx)
            key = ("D", idx)
            self.dkey[(id(semof.base), sw)] = key
        self._sem(key)
        self.dcnt[key] = self.dcnt.get(key, 0) + 16
        ev = (key, self.dcnt[key])
        fn = kw.pop("fn", None)
        if fn is None:
            fn = lambda e: e.dma_start(out=out, in_=in_, **kw)
        self.ops[q].append(("op", fn, key, 16))
        self._commit(ev, reads, writes, pwrites)

    def barrier(self):
        for e in ENGS:
            deps = [(("E", e2), self.ecnt[e2]) for e2 in ENGS if self.ecnt[e2] > 0 and not (e == e2 == "pe")]
            deps += [(k, c) for k, c in self.dcnt.items()]
            self._waits(e, deps)
        self.dkey = {}
        self.dfree = list(self.dall)
        self.dfree_sw = list(self.dall_sw)

    def emit(self):
        nc = self.nc
        hmap = {"pe": "tensor", "act": "scalar", "dve": "vector", "pool": "gpsimd", "sp": "sync"}
        with nc.Block() as block:
            for e in ENGS:
                if not self.ops[e]:
                    continue

                def body(engh, e=e):
                    for item in self.ops[e]:
                        if item[0] == "wait":
                            engh.wait_ge(self.sems[item[1]], item[2])
                        else:
                            ins = item[1](engh)
                            ins.then_inc(self.sems[item[2]], item[3])

                getattr(block, hmap[e])(body)


def mm(P, out, lhsT, rhs, start, stop, reads, wr):
    P.op("pe", lambda e: e.matmul(out, lhsT=lhsT, rhs=rhs, start=start, stop=stop),
         reads=reads, writes=[wr] if start else [], pwrites=[] if start else [wr])


def tr(P, out, in_, ident, reads, wr, first=True):
    P.op("pe", lambda e: e.transpose(out, in_, ident), reads=reads,
         writes=[wr] if first else [], pwrites=[] if first else [wr])


def act(P, out, in_, func, reads, writes, eng="act", **kw):
    P.op(eng, lambda e: e.activation(out=out, in_=in_, func=func, **kw), reads=reads, writes=writes)


def tt(P, eng, out, in0, in1, op, reads, writes):
    P.op(eng, lambda e: e.tensor_tensor(out=out, in0=in0, in1=in1, op=op), reads=reads, writes=writes)


def ts(P, eng, out, in0, s1, op0, reads, writes, s2=None, op1=None, **kw):
    if op1 is None:
        P.op(eng, lambda e: e.tensor_scalar(out=out, in0=in0, scalar1=s1, scalar2=None, op0=op0, **kw),
             reads=reads, writes=writes)
    else:
        P.op(eng, lambda e: e.tensor_scalar(out=out, in0=in0, scalar1=s1, scalar2=s2, op0=op0, op1=op1, **kw),
             reads=reads, writes=writes)


def stt(P, out, in0, scalar, in1, op0, op1, reads, writes):
    P.op("dve", lambda e: e.scalar_tensor_tensor(out=out, in0=in0, scalar=scalar, in1=in1, op0=op0, op1=op1),
         reads=reads, writes=writes)


def cp(P, eng, out, in_, reads, writes):
    if eng == "act":
        P.op(eng, lambda e: e.copy(out=out, in_=in_), reads=reads, writes=writes)
    else:
        P.op(eng, lambda e: e.tensor_copy(out=out, in_=in_), reads=reads, writes=writes)


class Ctx:
    pass


def stage_adaln(P, G, ph):
    sT = P.sbuf("sT", [128, 8, 3], F32, ph)
    sS = P.sbuf("sS", [128, 8, 3], F32, ph)
    wb = [P.sbuf(f"wmod{i}", [128, 8, 256], F32, ph) for i in range(3)]
    bm = [P.sbuf(f"bm{i}", [3, 256], F32, ph) for i in range(3)]
    mr = [P.sbuf(f"mr{i}", [3, 256], F32, ph) for i in range(3)]
    ps = G.ps
    P.dma("sp", sT[:], G.csT[:], writes=[sT], semof=sT)
    act(P, sS[:], sT[:], AF.Silu, [sT], [sS])
    thunks = []
    chunks = [(l, ch) for l in range(2) for ch in range(24)]

    def load(it):
        l, ch = chunks[it]
        w = wb[it % 3]
        b = bm[it % 3]
        P.dma("pool", w[:], G.w_mod[l, :, ch * 256:(ch + 1) * 256].rearrange("(kt p) f -> p kt f", p=128), writes=[w], semof=w)
        P.dma("pool", b[:], G.b_mod[l:l + 1, ch * 256:(ch + 1) * 256].broadcast_to([3, 256]), writes=[b], semof=b)

    def work(it):
        l, ch = chunks[it]
        w = wb[it % 3]
        b = bm[it % 3]
        m = mr[it % 3]
        p = ps[6 + it % 2]
        for kt in range(8):
            mm(P, p[0:3, 0:256], sS[:, kt, :], w[:, kt, :], kt == 0, kt == 7, [sS, w], p)
        tt(P, "dve", m[:], p[0:3, 0:256], b[:], ALU.add, [p, b], [m])
        P.dma("pool", G.mrow[l, :, ch * 256:(ch + 1) * 256], m[:], reads=[m], writes=[G.mrow.k((l, ch))], semof=m)
        if it + 2 < len(chunks):
            load(it + 2)

    load(0)
    load(1)
    for it in range(len(chunks)):
        thunks.append(lambda it=it: work(it))
    return thunks


def norm_stage(P, G, ph, src_tile, ntiles, g_ap, l, row, ish, isc, hT=None, h_tm=None, probs=None, rt=None, pcol=0):
    norm_jobs(P, G, ntiles, [dict(src_tile=src_tile, g_ap=g_ap, l=l, row=row, ish=ish, isc=isc, hT=hT, h_tm=h_tm, probs=probs, rt=rt, pcol=pcol)])


def norm_jobs(P, G, ntiles, jobs):
    nj = len(jobs)
    with ExitStack() as loc:
        ps = G.ps
        for ji, J in enumerate(jobs):
            J["gb"] = P.sbuf("gb", [128, D], F32, loc)
            J["A"] = P.sbuf("A", [128, D], F32, loc)
            J["B"] = P.sbuf("B", [128, D], F32, loc)
            J["xt"] = [P.sbuf(f"xt{i}", [128, D], F32, loc) for i in range(3)]
            J["hn"] = [P.sbuf(f"hn{i}", [128, D], F32, loc) for i in range(3)]
            J["junk"] = P.sbuf("junk", [128, D], F32, loc)
            J["st"] = [P.sbuf(f"st{i}", [128, 4], F32, loc) for i in range(3)]
            if J["probs"] is not None:
                J["hTf"] = [P.sbuf(f"hTf{i}", [128, 8, 128], F32, loc) for i in range(2)]
                J["sm"] = [P.sbuf(f"sm{i}", [128, 24], F32, loc) for i in range(2)]
            gb, A, B, l, row, isc, ish = J["gb"], J["A"], J["B"], J["l"], J["row"], J["isc"], J["ish"]
            P.dma("sp", gb[:], J["g_ap"].broadcast_to([128, D]), writes=[gb], semof=gb)
            P.dma("sp", A[:], G.mrow[l, row:row + 1, isc * D:(isc + 1) * D].broadcast_to([128, D]),
                  reads=[G.mrow.k((l, isc * 4 + i_)) for i_ in range(4)], writes=[A], semof=A)
            P.dma("sp", B[:], G.mrow[l, row:row + 1, ish * D:(ish + 1) * D].broadcast_to([128, D]),
                  reads=[G.mrow.k((l, ish * 4 + i_)) for i_ in range(4)], writes=[B], semof=B)
            stt(P, A[:], A[:], 1.0, gb[:], ALU.add, ALU.mult, [A, gb], [A])
            if nj == 1:
                J["trb"] = lambda t, half: ps[(t % 2) * 2 + half]
                J["lgb"] = lambda t: ps[4 + t % 2]
            else:
                J["trb"] = lambda t, half, ji=ji: ps[ji * 3 + half]
                J["lgb"] = lambda t, ji=ji: ps[ji * 3 + 2]
        def nload(J, t):
            x = J["xt"][t % 3]
            rd, ap = J["src_tile"](t)
            P.dma("sp", x[:], ap, reads=rd, writes=[x], semof=x)

        for t0 in range(min(2, ntiles)):
            for J in jobs:
                nload(J, t0)
        for t in range(ntiles):
            for J in jobs:
                hT, h_tm, probs, rt, pcol = J["hT"], J["h_tm"], J["probs"], J["rt"], J["pcol"]
                A, B = J["A"], J["B"]
                if t + 2 < ntiles:
                    nload(J, t + 2)
                x = J["xt"][t % 3]
                h = J["hn"][t % 3]
                s_ = J["st"][t % 3]
                junk = J["junk"]
                act(P, junk[:], x[:], AF.Square, [x], [junk, s_.k(0)], accum_out=s_[:, 0:1])
                act(P, s_[:, 1:2], s_[:, 0:1], AF.Sqrt, [s_.k(0)], [s_.k(1)], scale=1.0 / D, bias=G.epsc[:, 0:1])
                P.op("dve", lambda e, o=s_[:, 2:3], i=s_[:, 1:2]: e.reciprocal(out=o, in_=i), reads=[s_.k(1)], writes=[s_.k(2)])
                stt(P, h[:], x[:], s_[:, 2:3], A[:], ALU.mult, ALU.mult, [x, s_.k(2), A], [h])
                tt(P, "dve", h[:], h[:], B[:], ALU.add, [h, B], [h])
                if h_tm is not None:
                    P.dma("sp", G.h2d[h_tm, t * 128:(t + 1) * 128, :], h[:], reads=[h], writes=[G.h2d.k((h_tm, t))], semof=h)
                if hT is not None or probs is not None:
                    for half in range(2):
                        p = J["trb"](t, half)
                        for j in range(4):
                            k = half * 4 + j
                            tr(P, p[:, j * 128:(j + 1) * 128], h[:, k * 128:(k + 1) * 128], G.ident[:], [h, G.ident], p, first=(j == 0))
                        pv = p[:].rearrange("p (j n) -> p j n", j=4)
                        if hT is not None:
                            cp(P, "act" if half == 0 else "dve", hT[:, half * 4:half * 4 + 4, t * 128:(t + 1) * 128], pv, [p], [hT.k((t, half))])
                        if probs is not None:
                            f = J["hTf"][t % 2]
                            cp(P, "dve" if half == 0 else "act", f[:, half * 4:half * 4 + 4, :], pv, [p], [f.k(half)])
                if probs is not None:
                    f = J["hTf"][t % 2]
                    pl = J["lgb"](t)
                    m = J["sm"][t % 2]
                    for k in range(8):
                        mm(P, pl[:, 0:16], f[:, k, :], rt[:, k, :], k == 0, k == 7, [f.k(0), f.k(1), rt], pl)
                    P.op("dve", lambda e, o=m[:, 16:17], i=pl[:, 0:16]: e.tensor_reduce(out=o, in_=i, axis=mybir.AxisListType.X, op=ALU.max),
                         reads=[pl], writes=[m.k(1)])
                    ts(P, "dve", m[:, 17:18], m[:, 16:17], -1.0, ALU.mult, [m.k(1)], [m.k(2)])
                    act(P, m[:, 0:16], pl[:, 0:16], AF.Exp, [pl, m.k(2)], [m.k(0), m.k(3)], bias=m[:, 17:18], scale=1.0, accum_out=m[:, 18:19])
                    P.op("dve", lambda e, o=m[:, 19:20], i=m[:, 18:19]: e.reciprocal(out=o, in_=i), reads=[m.k(3)], writes=[m.k(4)])
                    ts(P, "dve", probs[:, t, pcol:pcol + 16], m[:, 0:16], m[:, 19:20], ALU.mult, [m.k(0), m.k(4)], [probs.k((t, pcol))])
    P.barrier()


def whole(Tobj, n):
    return [Tobj.k(i) for i in range(n)]


def route_stage(P, G, probs, code_tm, pg_tm):
    NP = NS * NEXP
    with ExitStack() as loc:
        PT = P.sbuf("PT", [NP, SEQ], F32, loc)
        msk = P.sbuf("msk", [NP, SEQ], F32, loc)
        cum = P.sbuf("cum", [NP, SEQ], F32, loc)
        ones = P.sbuf("ones", [NP, SEQ], F32, loc)
        sc = P.sbuf("sc", [NP, 8], F32, loc)
        ps = G.ps
        for q in range(4):
            for j in range(4):
                t = q * 4 + j
                tr(P, ps[q][0:NP, j * 128:(j + 1) * 128], probs[:, t, :], G.ident[:], [probs, G.ident], ps[q], first=(j == 0))
            cp(P, "act" if q % 2 else "dve", PT[:, q * 512:(q + 1) * 512], ps[q][0:NP, :], [ps[q]], [PT.k(q)])
        PTk = whole(PT, 4)
        P.op("dve", lambda e: e.memset(ones[:], 1.0), writes=[ones])
        mid, cnt, g = (sc[:, i:i + 1] for i in range(3))
        P.op("dve", lambda e: e.memset(mid, 0.5), writes=[sc.k(0)])
        NIT = 24
        for it in range(NIT):
            w_next = 0.5 ** (it + 2)
            ts(P, "dve", msk[:], PT[:], mid, ALU.is_ge, PTk + [sc.k(0)], [msk, sc.k(1)], s2=0.0, op1=ALU.add, accum_out=cnt)
            ts(P, "dve", g, cnt, float(CAP), ALU.is_ge, [sc.k(1)], [sc.k(2)], s2=2.0 * w_next, op1=ALU.mult)
            stt(P, mid, g, -w_next, mid, ALU.add, ALU.add, [sc.k(2), sc.k(0)], [sc.k(0)])
        ts(P, "dve", mid, mid, -(0.5 ** (NIT + 1)), ALU.add, [sc.k(0)], [sc.k(0)])
        ts(P, "dve", msk[:], PT[:], mid, ALU.is_ge, PTk + [sc.k(0)], [msk])
        P.op("dve", lambda e: e.tensor_tensor_scan(out=cum[:], data0=ones[:], data1=msk[:], initial=0.0, op0=ALU.mult, op1=ALU.add),
             reads=[ones, msk], writes=[cum])
        tt(P, "dve", cum[:], cum[:], msk[:], ALU.mult, [cum, msk], [cum])
        ts(P, "dve", cum[:], cum[:], -1.0, ALU.add, [cum], [cum])
        tt(P, "dve", msk[:], msk[:], PT[:], ALU.mult, [msk] + PTk, [msk])
        for src, dst, pb in ((cum, code_tm, ps[4]), (msk, pg_tm, ps[5])):
            for t in range(NT):
                tr(P, pb[:, t * NP:(t + 1) * NP], src[0:NP, t * 128:(t + 1) * 128], G.ident[0:NP, 0:NP], [src, G.ident], pb, first=(t == 0))
            cp(P, "act", dst[:].rearrange("p t e -> p (t e)"), pb[:, 0:NT * NP], [pb], [dst])
    P.barrier()


def expert_stage(P, G, l, code_tm, pg_tm, igate):
    I32 = mybir.dt.int32
    h2flat = G.h2d[:].rearrange("s n d -> (s n) d")
    xrflat = G.xres[:].rearrange("s n d -> (s n) d")
    with ExitStack() as loc:
        wb = [P.sbuf(f"wexp{i}", [128, 8, D], BF16, loc) for i in range(5)]
        S = [P.sbuf(f"S{s}", [128, NT, CAP], BF16, loc) for s in range(NS)]
        R = [P.sbuf(f"R{s}", [128, NT, 4], BF16, loc) for s in range(NS)]
        rl = [P.sbuf(f"rlo{s}", [128, NT], F32, loc) for s in range(NS)]
        xs = [[P.sbuf(f"xs{i}_{c}", [128, D], F32, loc) for c in range(4)] for i in range(2)]
        ig = [P.sbuf(f"ig{i}", [128, 4, 4], F32, loc) for i in range(2)]
        idx = [P.sbuf(f"idx{i}", [128, 4], I32, loc) for i in range(2)]
        gate = [P.sbuf(f"gate{i}", [128, 4], F32, loc) for i in range(2)]
        xsT = [P.sbuf(f"xsT{i}", [128, 8, NS * CAP], BF16, loc) for i in range(2)]
        actT = P.sbuf("actT", [128, 8, NS * CAP], BF16, loc)
        tmp = [P.sbuf(f"sil{i}", [128, NS * CAP], F32, loc) for i in range(2)]
        ysb = [P.sbuf(f"ysb{c}", [128, D], F32, loc) for c in range(4)]
        g2 = [P.sbuf(f"g2_{s}", [128, D], F32, loc) for s in range(NS)]
        ps = G.ps
        for s in range(NS):
            P.dma("sp", g2[s][:], G.mrow[l, s:s + 1, igate * D:(igate + 1) * D].broadcast_to([128, D]), writes=[g2[s]], semof=g2[s])
            cp(P, "dve", R[s][:, :, 0:2], G.npos[:], [G.npos], [R[s].k(0)])
        st = {"wi": 0, "pi": 0}
        wts = {}

        offs = P.sbuf("offs", [128, 4], F32, loc)
        for c in range(4):
            P.op("dve", lambda en, c=c: en.memset(offs[:, c:c + 1], float((c // 2) * SEQ)), pwrites=[offs])

        def prep1_ops(e):
            ops = []
            for s in range(NS):
                col = s * NEXP + e
                for t in range(NT):
                    ops.append(lambda s=s, t=t, col=col: ts(P, "dve", S[s][:, t, :], G.iota[:, 0:CAP], code_tm[:, t, col:col + 1], ALU.is_equal,
                                                           [G.iota, code_tm], [S[s].k(t)]))
                ops.append(lambda s=s, col=col: cp(P, "dve", R[s][:, :, 2:3], pg_tm[:, :, col:col + 1], [pg_tm], [R[s].k(1)]))
                ops.append(lambda s=s, col=col: tt(P, "dve", rl[s][:].rearrange("p (t o) -> p t o", o=1), pg_tm[:, :, col:col + 1], R[s][:, :, 2:3],
                                                   ALU.subtract, [pg_tm, R[s].k(1)], [rl[s]]))
                ops.append(lambda s=s: cp(P, "dve", R[s][:, :, 3:4], rl[s][:].rearrange("p (t o) -> p t o", o=1), [rl[s]], [R[s].k(2)]))
            return ops

        def prep2(e):
            b = e % 2
            p = ps[6 + (e % 2)]
            for s in range(NS):
                Sk = whole(S[s], NT)
                Rk = [R[s].k(0), R[s].k(1), R[s].k(2)]
                for ch in range(2):
                    c = s * 2 + ch
                    for t in range(NT):
                        mm(P, p[:, c * 4:(c + 1) * 4], S[s][:, t, ch * 128:(ch + 1) * 128], R[s][:, t, :], t == 0, t == NT - 1, Sk + Rk,
                           p if c == 0 else p.k(c))
            cp(P, "dve", ig[b][:].rearrange("p c f -> p (c f)"), p[:, 0:16], [p, p.k(1), p.k(2), p.k(3)], [ig[b]])
            stt(P, gate[b][:].rearrange("p (c o) -> p c o", o=1), ig[b][:, :, 0:1], 128.0, ig[b][:, :, 1:2], ALU.mult, ALU.add, [ig[b]], [gate[b]])
            tt(P, "dve", idx[b][:], gate[b][:], offs[:], ALU.add, [gate[b], offs], [idx[b]])
            tt(P, "dve", gate[b][:].rearrange("p (c o) -> p c o", o=1), ig[b][:, :, 2:3], ig[b][:, :, 3:4], ALU.add, [ig[b], gate[b]], [gate[b]])
            for c in range(4):
                x_ = xs[b][c]
                P.dma("pool", None, None, reads=[idx[b]], writes=[x_], semof=x_,
                      fn=lambda en, o=x_[:, :], ia=idx[b][:, c:c + 1]: en.indirect_dma_start(
                          out=o, out_offset=None, in_=h2flat, in_offset=bass.IndirectOffsetOnAxis(ap=ia, axis=0)))
            if e == 0:
                wts[0] = (wload("exp_w1", 0, wb[0]), wload("exp_w3", 0, wb[1]))
                wts["w2", 0] = wload("exp_w2", 0, wb[4])

        def wload(nm, e, w):
            for hk in range(2):
                P.dma("pool", w[:, hk * 4:(hk + 1) * 4, :],
                      getattr(G, nm)[l, e, hk * 512:(hk + 1) * 512, :].rearrange("(kt p) f -> p kt f", p=128),
                      writes=[w] if hk == 0 else [], pwrites=[] if hk == 0 else [w], semof=w)
            return w

        def compute(e):
            b = e % 2
            w1, w3 = wts.pop(e)
            w2 = wts.pop(("w2", e))
            xT = xsT[b]
            nxt = prep1_ops(e + 1) if e + 1 < NEXP else []
            if e + 1 < NEXP:
                e1 = e + 1
                wts[e1] = (wload("exp_w1", e1, wb[(e1 % 2) * 2]), wload("exp_w3", e1, wb[(e1 % 2) * 2 + 1]))
            for c in range(4):
                x_ = xs[b][c]
                for half in range(2):
                    p = ps[st["pi"] % 6]; st["pi"] += 1
                    for jj in range(4):
                        k = half * 4 + jj
                        tr(P, p[:, jj * 128:(jj + 1) * 128], x_[:, k * 128:(k + 1) * 128], G.ident[:], [x_, G.ident], p, first=(jj == 0))
                    cp(P, "act" if half else "dve", xT[:, half * 4:half * 4 + 4, c * 128:(c + 1) * 128],
                       p[:].rearrange("p (j n) -> p j n", j=4), [p], [xT.k((c, half))])
            xk = [xT.k((c, half)) for c in range(4) for half in range(2)]
            per = (len(nxt) + 7) // 8
            for fo in range(8):
                pa = ps[st["pi"] % 6]; st["pi"] += 1
                pg = ps[st["pi"] % 6]; st["pi"] += 1
                for k in range(8):
                    mm(P, pa[:], w1[:, k, fo * 128:(fo + 1) * 128], xT[:, k, :], k == 0, k == 7, [w1] + xk, pa)
                for k in range(8):
                    mm(P, pg[:], w3[:, k, fo * 128:(fo + 1) * 128], xT[:, k, :], k == 0, k == 7, [w3] + xk, pg)
                tm = tmp[fo % 2]
                act(P, tm[:], pa[:], AF.Silu, [pa], [tm])
                tt(P, "dve", actT[:, fo, :], tm[:], pg[:], ALU.mult, [tm, pg], [actT.k(fo)])
                for fn in nxt[fo * per:(fo + 1) * per]:
                    fn()
            if e + 1 < NEXP:
                prep2(e + 1)
            atk = whole(actT, 8)
            for c in range(4):
                s_ = c // 2
                yb = ysb[c]
                for dh in range(2):
                    p = ps[st["pi"] % 6]; st["pi"] += 1
                    for f in range(8):
                        mm(P, p[:], actT[:, f, c * 128:(c + 1) * 128], w2[:, f, dh * 512:(dh + 1) * 512], f == 0, f == 7, [w2] + atk, p)
                    stt(P, yb[:, dh * 512:(dh + 1) * 512], p[:], gate[b][:, c:c + 1], g2[s_][:, dh * 512:(dh + 1) * 512], ALU.mult, ALU.mult,
                        [p, gate[b], g2[s_]], [yb.k(dh)])
                if c == 3 and e + 1 < NEXP:
                    wts["w2", e + 1] = wload("exp_w2", e + 1, wb[4])
                P.dma("pool", None, None, reads=[yb.k(0), yb.k(1), idx[b]], writes=[G.xres.k("sc")], semof=yb,
                      fn=lambda en, i_=yb[:, :], ia=idx[b][:, c:c + 1]: en.indirect_dma_start(
                          out=xrflat, out_offset=bass.IndirectOffsetOnAxis(ap=ia, axis=0), in_=i_, in_offset=None, compute_op=ALU.add))

        for fn in prep1_ops(0):
            fn()
        prep2(0)
        for e in range(NEXP):
            compute(e)
    P.barrier()


def load_win(P, G, loc, c0, c1):
    n = (c1 - c0) // 512
    win = P.sbuf("win", [128, 8, c1 - c0], BF16, loc)
    for j in range(n):
        P.dma("pool", win[:, :, j * 512:(j + 1) * 512], G.w_in[0, :, c0 + j * 512:c0 + (j + 1) * 512].rearrange("(kt p) f -> p kt f", p=128),
              writes=[win.k(j)], semof=win)
    return win, whole(win, n)


def lru_part(P, G, win, wink, hT, ntok, h0, ya_s=None, hfin=None):
    nq = (ntok + 511) // 512
    qs = min(512, ntok)
    with ExitStack() as loc:
        sets = [{n: P.sbuf(n + str(z), [128, ntok + 4], F32, loc) for n in ("xr", "xa", "rr", "ii", "tmp", "hf", "hb")} for z in range(2)]
        xabs = [P.sbuf(f"xab{z}", [128, ntok], BF16, loc) for z in range(2)]
        yab = [P.sbuf(f"yab{i}", [128, ntok], BF16, loc) for i in range(2)]
        ps = G.ps
        st = {"pi": 0}

        def bank():
            p = ps[st["pi"] % 8]
            st["pi"] += 1
            return p

        for z in range(2):
            P.op("dve", lambda e, b_=sets[z]["xr"]: e.memset(b_[:], 0.0), writes=[sets[z]["xr"]])

        def chain(ci):
            xr, xa, rr, ii, tmp, hf, hb = (sets[ci % 2][n] for n in ("xr", "xa", "rr", "ii", "tmp", "hf", "hb"))
            xab = xabs[ci % 2]
            for q in range(nq):
                p = bank()
                for k in range(8):
                    mm(P, p[:, 0:qs], win[:, k, 512 + ci * 128:512 + (ci + 1) * 128], hT[:, k, q * 512:q * 512 + qs], k == 0, k == 7, [hT] + wink, p)
                cp(P, "act", xr[:, 1 + q * 512:1 + q * 512 + qs], p[:, 0:qs], [p], [xr])
                yield
            ts(P, "dve", xa[:, 0:ntok], xr[:, 0:ntok], G.caw[:, ci, 0:1], ALU.mult, [xr, G.caw], [xa], s2=G.cab[:, ci:ci + 1], op1=ALU.add)
            yield
            for j in range(1, 4):
                stt(P, xa[:, 0:ntok], xr[:, j:j + ntok], G.caw[:, ci, j:j + 1], xa[:, 0:ntok], ALU.mult, ALU.add, [xr, xa, G.caw], [xa])
                yield
            cp(P, "act", xab[:], xa[:, 0:ntok], [xa], [xab])
            yield
            for k in range(2):
                for gi, (dst, bias) in enumerate(((rr, G.lbr), (ii, G.lbi))):
                    for q in range(nq):
                        p = bank()
                        mm(P, p[:, 0:qs], G.bd[:, (gi * 2 + k) * 4 + ci, :], xab[:, q * 512:q * 512 + qs], True, True, [G.bd, xab], p)
                        act(P, dst[:, q * 512:q * 512 + qs], p[:, 0:qs], AF.Sigmoid, [p, bias], [dst], bias=bias[:, k * 4 + ci:k * 4 + ci + 1], scale=1.0)
                        yield
                act(P, tmp[:, 0:ntok], rr[:, 0:ntok], AF.Exp, [rr, G.spc2], [tmp], scale=G.spc2[:, k * 4 + ci:k * 4 + ci + 1])
                yield
                act(P, rr[:, 0:ntok], rr[:, 0:ntok], AF.Exp, [rr, G.spc], [rr], scale=G.spc[:, k * 4 + ci:k * 4 + ci + 1])
                yield
                act(P, tmp[:, 0:ntok], tmp[:, 0:ntok], AF.Sqrt, [tmp], [tmp], scale=-1.0, bias=G.onec[:, 0:1])
                yield
                tt(P, "dve", ii[:, 0:ntok], ii[:, 0:ntok], tmp[:, 0:ntok], ALU.mult, [ii, tmp], [ii])
                yield
                tt(P, "dve", ii[:, 0:ntok], ii[:, 0:ntok], xa[:, 0:ntok], ALU.mult, [ii, xa], [ii])
                yield
                init = 0.0 if h0 is None else h0[:, ci, k:k + 1]
                rds = [rr, ii] + ([] if h0 is None else [h0])
                if k == 0:
                    P.op("dve", lambda e, o=hf[:, 0:ntok], a=rr[:, 0:ntok], u=ii[:, 0:ntok], i0=init:
                         e.tensor_tensor_scan(out=o, data0=a, data1=u, initial=i0, op0=ALU.mult, op1=ALU.add), reads=rds, writes=[hf])
                else:
                    P.op("dve", lambda e, o=hb[:, 0:ntok][:, ::-1], a=rr[:, 0:ntok][:, ::-1], u=ii[:, 0:ntok][:, ::-1], i0=init:
                         e.tensor_tensor_scan(out=o, data0=a, data1=u, initial=i0, op0=ALU.mult, op1=ALU.add), reads=rds, writes=[hb])
                yield
            if hfin is not None:
                cp(P, "dve", hfin[:, ci, 0:1], hf[:, ntok - 1:ntok], [hf], [hfin.k((ci, 0))])
                cp(P, "dve", hfin[:, ci, 1:2], hb[:, 0:1], [hb], [hfin.k((ci, 1))])
                yield
            if ya_s is not None:
                tt(P, "dve", hf[:, 0:ntok], hf[:, 0:ntok], hb[:, 0:ntok], ALU.add, [hf, hb], [hf])
                yield
                for q in range(nq):
                    p = bank()
                    for k in range(8):
                        mm(P, p[:, 0:qs], win[:, k, ci * 128:(ci + 1) * 128], hT[:, k, q * 512:q * 512 + qs], k == 0, k == 7, [hT] + wink, p)
                    act(P, tmp[:, q * 512:q * 512 + qs], p[:, 0:qs], AF.Gelu_apprx_tanh, [p], [tmp])
                    yield
                yb = yab[ci % 2]
                tt(P, "dve", yb[:], tmp[:, 0:ntok], hf[:, 0:ntok], ALU.mult, [tmp, hf], [yb])
                P.dma("sp", G.yad[ya_s, ci], yb[:], reads=[yb], writes=[G.yad.k((ya_s, ci))], semof=yb)
                yield

        for pair in ((0, 1), (2, 3)):
            gens = [chain(ci) for ci in pair]
            while gens:
                for g_ in list(gens):
                    try:
                        next(g_)
                    except StopIteration:
                        gens.remove(g_)
    P.barrier()


def hy_inproj(P, G, win, wink, hT, s):
    with ExitStack() as loc:
        xr = [P.sbuf(f"hxr{i}", [128, SEQ + 2], F32, loc) for i in range(2)]
        t0 = P.sbuf("hyt", [128, SEQ], F32, loc)
        ob = [P.sbuf(f"hyo{i}", [128, SEQ], BF16, loc) for i in range(2)]
        v_tm = P.sbuf("vtm", [128, NT, 512], BF16, loc)
        ps = G.ps
        pi = 0
        for b in xr:
            P.op("dve", lambda e, b=b: e.memset(b[:], 0.0), writes=[b])
        for fi in range(12):
            part, ci = fi // 4, fi % 4
            x = xr[fi % 2]
            for q in range(4):
                p = ps[pi % 4]; pi += 1
                for k in range(8):
                    mm(P, p[:], win[:, k, fi * 128:(fi + 1) * 128], hT[:, k, q * 512:(q + 1) * 512], k == 0, k == 7, [hT] + wink, p)
                cp(P, "act", x[:, 1 + q * 512:1 + (q + 1) * 512], p[:], [p], [x])
            o = ob[fi % 2]
            ts(P, "dve", t0[:], x[:, 0:SEQ], G.cbw[:, fi, 0:1], ALU.mult, [x, G.cbw, G.cbb], [t0], s2=G.cbb[:, fi:fi + 1], op1=ALU.add)
            stt(P, t0[:], x[:, 1:1 + SEQ], G.cbw[:, fi, 1:2], t0[:], ALU.mult, ALU.add, [x, t0, G.cbw], [t0])
            stt(P, t0[:], x[:, 2:2 + SEQ], G.cbw[:, fi, 2:3], t0[:], ALU.mult, ALU.add, [x, t0, G.cbw], [t0])
            cp(P, "act", o[:], t0[:], [t0], [o])
            P.dma("sp", G.hyd[s, part, ci], o[:], reads=[o], writes=[G.hyd.k((s, part, ci))], semof=o)
            if part == 0:
                for g in range(4):
                    p = ps[4 + g]
                    for jj in range(4):
                        tt_ = g * 4 + jj
                        tr(P, p[:, jj * 128:(jj + 1) * 128], t0[:, tt_ * 128:(tt_ + 1) * 128], G.ident[:], [t0, G.ident], p, first=(jj == 0))
                    cp(P, "dve", v_tm[:, g * 4:(g + 1) * 4, ci * 128:(ci + 1) * 128], p[:].rearrange("p (j c) -> p j c", j=4), [p], [v_tm.k((g, ci))])
        P.dma("sp", G.vtm[s], v_tm[:], reads=[v_tm.k((g, ci)) for g in range(4) for ci in range(4)], writes=[G.vtm.k(s)], semof=v_tm)
    P.barrier()


def bg_step(G):
    if getattr(G, "bg", None):
        G.bg.pop(0)()


def hy_filters(P, G):
    N2 = 2 * SEQ
    with ExitStack() as loc:
        zp = P.sbuf("zp", [33, SEQ + 1], F32, loc)
        w1 = P.sbuf("fw1", [33, 64], F32, loc)
        w2 = P.sbuf("fw2", [64, 64], F32, loc)
        w3 = P.sbuf("fw3", [65, 2048], F32, loc)
        h1 = P.sbuf("fh1", [64, SEQ + 1], F32, loc)
        h2 = P.sbuf("fh2", [65, SEQ + 1], F32, loc)
        rtmp = P.sbuf("rtmp", [64, 512], F32, loc)
        fc = P.sbuf("fc", [64, 8], F32, loc)
        ke = P.sbuf("ke", [128, NT, 2, 512], BF16, loc)
        ko = P.sbuf("ko", [128, NT, 2, 512], BF16, loc)
        wn = [P.sbuf(f"wn{i}", [128, 2, 512], F32, loc) for i in range(2)]
        ft = [P.sbuf(f"ftmp{i}", [128, 2, 512], F32, loc) for i in range(2)]
        cf = [P.sbuf(f"cff{i}", [128, 2, NT, 128], BF16, loc) for i in range(2)]
        kt = [P.sbuf(f"ktb{i}", [128, 2, 512], F32, loc) for i in range(2)]
        k2 = [P.sbuf(f"kt2{i}", [128, 2, 512], F32, loc) for i in range(2)]
        ps = G.ps
        P.dma("sp", zp[:], G.zposT[:], writes=[zp], semof=zp)
        P.dma("sp", w1[:], G.filt_w1[0], writes=[w1], semof=w1)
        P.dma("sp", w2[:], G.filt_w2[0], writes=[w2], semof=w2)
        P.dma("sp", w3[0:64, :], G.filt_w3[0], writes=[w3.k(0)], semof=w3)
        P.dma("sp", w3[64:65, :], G.filt_b3[0:1, :], writes=[w3.k(1)], semof=w3)
        P.dma("sp", fc[:, 0:4], G.filt_c[:], writes=[fc], semof=fc)
        tt(P, "dve", fc[:, 4:6], fc[:, 0:2], fc[:, 2:4], ALU.mult, [fc], [fc.k(1)])
        P.op("dve", lambda e: e.memset(h2[:], 1.0), writes=[h2])
        TWO_PI = 2.0 * math.pi

        def sin_layer(dst, src_ps, fcol, bcol, rd):
            act(P, dst, src_ps, AF.Identity, rd + [fc, fc.k(1)], [h1 if dst is not None else h1], scale=fc[:, fcol:fcol + 1], bias=fc[:, bcol:bcol + 1])

        for layer in range(2):
            src = zp if layer == 0 else h1
            wt = w1 if layer == 0 else w2
            dstT = h1 if layer == 0 else h2
            kk = 33 if layer == 0 else 64
            for q in range(5):
                bg_step(G)
                c0 = q * 512
                n = min(512, SEQ + 1 - c0)
                p = ps[q % 4]
                mm(P, p[0:64, 0:n], wt[0:kk, :], src[0:kk, c0:c0 + n], True, True, [wt, src], p)
                d = dstT[0:64, c0:c0 + n]
                act(P, d, p[0:64, 0:n], AF.Identity, [p, fc, fc.k(1)], [dstT], scale=fc[:, 2 + layer:3 + layer], bias=fc[:, 4 + layer:5 + layer])
                MAGIC = 12582912.0
                kk_ = rtmp[0:64, 0:n]
                ts(P, "dve", kk_, d, 1.0 / TWO_PI, ALU.mult, [dstT], [rtmp], s2=MAGIC, op1=ALU.add)
                ts(P, "dve", kk_, kk_, -MAGIC, ALU.add, [rtmp], [rtmp])
                stt(P, d, kk_, -TWO_PI, d, ALU.mult, ALU.add, [rtmp, dstT], [dstT])
                ts(P, "dve", d, d, -math.pi, ALU.max, [dstT], [dstT], s2=math.pi, op1=ALU.min)
                act(P, d, d, AF.Sin, [dstT], [dstT])
        P.op("dve", lambda e: e.memset(h2[:, SEQ:SEQ + 1], 0.0), writes=[h2])
        it = 0
        for lt in range(NT):
            for o in range(2):
                bg_step(G)
                w = wn[it % 2]
                f = ft[it % 2]
                it += 1
                P.dma("sp", w[:], G.win2[lt], writes=[w], semof=w)
                pf = ps[(it % 2) * 2]
                pb = ps[(it % 2) * 2 + 1]
                mm(P, pf[:], h2[:, lt * 128:(lt + 1) * 128], w3[:, o * 1024:o * 1024 + 512], True, True, [h2, w3.k(0), w3.k(1)], pf)
                mm(P, pb[:], h2[:, lt * 128 + 1:(lt + 1) * 128 + 1], w3[:, o * 1024 + 512:(o + 1) * 1024], True, True, [h2, w3.k(0), w3.k(1)], pb)
                tt(P, "dve", f[:, 0, :], pf[:], w[:, 0, :], ALU.mult, [pf, w], [f.k(0)])
                tt(P, "dve", f[:, 1, :], pb[:], w[:, 1, :], ALU.mult, [pb, w], [f.k(1)])
                tt(P, "dve", ke[:, lt, o, :], f[:, 0, :], f[:, 1, :], ALU.add, [f.k(0), f.k(1)], [ke.k((lt, o))])
                tt(P, "dve", ko[:, lt, o, :], f[:, 1, :], f[:, 0, :], ALU.subtract, [f.k(0), f.k(1)], [ko.k((lt, o))])
        kek = [ke.k((lt, o)) for lt in range(NT) for o in range(2)]
        kok = [ko.k((lt, o)) for lt in range(NT) for o in range(2)]
        for fj in range(NT):
            c = cf[fj % 2]
            P.dma("sp", c[:], G.csf[fj], writes=[c], semof=c)
            for o in range(2):
                bg_step(G)
                pr = ps[4]
                pi_ = ps[5]
                k_ = kt[o]
                k2_ = k2[o]
                for lt in range(NT):
                    mm(P, pr[:], c[:, 0, lt, :], ke[:, lt, o, :], lt == 0, lt == NT - 1, [c] + kek, pr)
                for lt in range(NT):
                    mm(P, pi_[:], c[:, 1, lt, :], ko[:, lt, o, :], lt == 0, lt == NT - 1, [c] + kok, pi_)
                ts(P, "dve", k2_[:, 0, :], pi_[:], G.rot[:, fj, 1:2], ALU.mult, [pi_, G.rot], [k2_.k(0)])
                ts(P, "dve", k2_[:, 1, :], pi_[:], G.rot[:, fj, 0:1], ALU.mult, [pi_, G.rot], [k2_.k(1)])
                stt(P, k_[:, 0, :], pr[:], G.rot[:, fj, 0:1], k2_[:, 0, :], ALU.mult, ALU.subtract, [pr, G.rot, k2_.k(0)], [k_.k(0)])
                stt(P, k_[:, 1, :], pr[:], G.rot[:, fj, 1:2], k2_[:, 1, :], ALU.mult, ALU.add, [pr, G.rot, k2_.k(1)], [k_.k(1)])
                P.dma("sp", G.ktab[o, fj], k_[:], reads=[k_.k(0), k_.k(1)], writes=[G.ktab.k((o, fj))], semof=k_)
        while G.bg:
            bg_step(G)
    P.barrier()


def hy_conv(P, G, o, z_tm, zT, xgT, want_tm):
    with ExitStack() as loc:
        cf = [P.sbuf(f"cf{i}", [128, 2, NT, 128], BF16, loc) for i in range(2)]
        kt = [P.sbuf(f"kt{i}", [128, 2, 512], F32, loc) for i in range(2)]
        Y = P.sbuf("Y", [128, NT, 2, 512], BF16, loc)
        tmp = [P.sbuf(f"yt{i}", [128, 4, 512], F32, loc) for i in range(2)]
        ci_ = [P.sbuf(f"ci{i}", [128, NT, 512], BF16, loc) for i in range(2)]
        ps = G.ps
        ztk = [z_tm.k((g, ci)) for g in range(4) for ci in range(4)]
        for fj in range(NT):
            c = cf[fj % 2]
            k_ = kt[fj % 2]
            t_ = tmp[fj % 2]
            P.dma("sp", c[:], G.csf[fj], writes=[c], semof=c)
            P.dma("act", k_[:], G.ktab[o, fj], writes=[k_], semof=k_)
            pr = ps[(fj % 2) * 2]
            pq = ps[(fj % 2) * 2 + 1]
            for tt_ in range(NT):
                mm(P, pr[:], c[:, 0, tt_, :], z_tm[:, tt_, :], tt_ == 0, tt_ == NT - 1, [c] + ztk, pr)
            for tt_ in range(NT):
                mm(P, pq[:], c[:, 1, tt_, :], z_tm[:, tt_, :], tt_ == 0, tt_ == NT - 1, [c] + ztk, pq)
            tt(P, "dve", t_[:, 0, :], pr[:], k_[:, 0, :], ALU.mult, [pr, k_], [t_.k(0)])
            tt(P, "dve", t_[:, 1, :], pq[:], k_[:, 1, :], ALU.mult, [pq, k_], [t_.k(1)])
            tt(P, "dve", t_[:, 2, :], pq[:], k_[:, 0, :], ALU.mult, [pq, k_], [t_.k(2)])
            tt(P, "dve", t_[:, 3, :], pr[:], k_[:, 1, :], ALU.mult, [pr, k_], [t_.k(3)])
            tt(P, "dve", Y[:, fj, 0, :], t_[:, 0, :], t_[:, 1, :], ALU.add, [t_.k(0), t_.k(1)], [Y.k(fj)])
            tt(P, "dve", Y[:, fj, 1, :], t_[:, 2, :], t_[:, 3, :], ALU.subtract, [t_.k(2), t_.k(3)], [Y.k((fj, 1))])
        Yk = [Y.k(fj) for fj in range(NT)] + [Y.k((fj, 1)) for fj in range(NT)]
        for tq in range(4):
            for cs in range(2):
                P.dma("sp", ci_[cs][:], G.csi[tq, cs], writes=[ci_[cs]], semof=ci_[cs])
            for ci in range(4):
                p = ps[4 + ci]
                n = 0
                for cs in range(2):
                    for fj in range(NT):
                        mm(P, p[:], Y[:, fj, cs, ci * 128:(ci + 1) * 128], ci_[cs][:, fj, :], n == 0, n == 2 * NT - 1, Yk + ci_, p)
                        n += 1
                t_ = tmp[ci % 2]
                sl = slice(tq * 512, (tq + 1) * 512)
                zk = zT.k((ci, tq))
                stt(P, t_[:, 0, :], zT[:, ci, sl], G.fbias[:, o * 4 + ci:o * 4 + ci + 1], p[:], ALU.mult, ALU.add, [zk, G.fbias, p], [t_.k(0)])
                tt(P, "dve", t_[:, 1, :], t_[:, 0, :], xgT[:, ci, sl], ALU.mult, [t_.k(0), xgT], [t_.k(1)])
                cp(P, "act", zT[:, ci, sl], t_[:, 1, :], [t_.k(1)], [zk])
                if want_tm:
                    pt = ps[ci % 4]
                    for jj in range(4):
                        tr(P, pt[:, jj * 128:(jj + 1) * 128], t_[:, 1, jj * 128:(jj + 1) * 128], G.ident[:], [t_.k(1), G.ident], pt, first=(jj == 0))
                    cp(P, "dve", z_tm[:, tq * 4:(tq + 1) * 4, ci * 128:(ci + 1) * 128], pt[:].rearrange("p (j c) -> p j c", j=4), [pt], [z_tm.k((tq, ci))])
    P.barrier()


def mixer_out(P, G, l, s, kT_list, w_ap, bias_ap, igate, src=None):
    with ExitStack() as loc:
        w = P.sbuf("wout", [128, 8, D], BF16, loc)
        g1 = P.sbuf("g1", [128, D], F32, loc)
        xt = [P.sbuf(f"ox{i}", [128, D], F32, loc) for i in range(4)]
        tmp = [P.sbuf(f"ot{i}", [128, D], F32, loc) for i in range(2)]
        ps = G.ps
        for hk in range(2):
            P.dma("pool", w[:, hk * 4:(hk + 1) * 4, :], w_ap[hk * 512:(hk + 1) * 512, :].rearrange("(kt p) f -> p kt f", p=128),
                  writes=[w.k(hk)], semof=w)
        wk = whole(w, 2)
        if bias_ap is not None:
            brow = P.sbuf("brow", [1, D], BF16, loc)
            P.dma("pool", brow[:], bias_ap, writes=[brow], semof=brow)
        P.dma("sp", g1[:], G.mrow[l, s:s + 1, igate * D:(igate + 1) * D].broadcast_to([128, D]), writes=[g1], semof=g1)
        xsrc_ = src if src is not None else G.xres

        def xload(t):
            P.dma("sp", xt[t % 4][:], xsrc_[s, t * 128:(t + 1) * 128, :], writes=[xt[t % 4]], semof=xt[t % 4])

        xload(0)
        xload(1)
        for t in range(NT):
            if t + 2 < NT:
                xload(t + 2)
            x = xt[t % 4]
            tm = tmp[t % 2]
            for dh in range(2):
                p = ps[(t % 4) * 2 + dh]
                for k in range(8):
                    kt_, idx = kT_list[k]
                    mm(P, p[:], kt_[:, idx, t * 128:(t + 1) * 128], w[:, k, dh * 512:(dh + 1) * 512], k == 0,
                       (k == 7 and bias_ap is None), [kt_] + wk, p)
                if bias_ap is not None:
                    mm(P, p[:], G.onesb[0:1, :], brow[0:1, dh * 512:(dh + 1) * 512], False, True, [G.onesb, brow], p)
                tt(P, "dve", tm[:, dh * 512:(dh + 1) * 512], p[:], g1[:, dh * 512:(dh + 1) * 512], ALU.mult, [p, g1], [tm.k(dh)])
            tt(P, "dve", x[:], x[:], tm[:], ALU.add, [x, tm.k(0), tm.k(1)], [x])
            P.dma("sp", G.xres[s, t * 128:(t + 1) * 128, :], x[:], reads=[x], writes=[G.xres.k((s, t))], semof=x)
    P.barrier()


def conformer(P, G, s, hT, sT):
    PADC = 15 * 64
    with ExitStack() as loc:
        u = P.sbuf("cfu", [128, 8, SEQ], F32, loc)
        with ExitStack() as l1:
            w1 = P.sbuf("cfw1", [128, 8, 2 * D], BF16, l1)
            gl = [P.sbuf(f"cfgl{i}", [128, SEQ + 2 * PADC], BF16, l1) for i in range(2)]
            dg = [P.sbuf(f"cfdg{i}", [128, 31, 128], BF16, l1) for i in range(2)]
            sg = [P.sbuf(f"cfsg{i}", [128, 512], F32, l1) for i in range(2)]
            ps = G.ps
            for j in range(4):
                P.dma("pool", w1[:, :, j * 512:(j + 1) * 512], G.cf_w1[0, :, j * 512:(j + 1) * 512].rearrange("(kt p) f -> p kt f", p=128),
                      writes=[w1.k(j)], semof=w1)
            w1k = whole(w1, 4)
            for b in gl:
                P.op("dve", lambda e, b=b: e.memset(b[:], 0.0), writes=[b])
            pi = 0
            for ci in range(8):
                g_ = gl[ci % 2]
                d_ = dg[ci % 2]
                for j in range(31):
                    ts(P, "dve", d_[:, j, :], G.ident[:], G.cfdw[:, ci, j:j + 1], ALU.mult, [G.ident, G.cfdw], [d_.k(j)])
                for q in range(4):
                    pa = ps[pi % 4]; pi += 1
                    pg = ps[pi % 4]; pi += 1
                    s_ = sg[q % 2]
                    for k in range(8):
                        mm(P, pa[:], w1[:, k, ci * 128:(ci + 1) * 128], hT[:, k, q * 512:(q + 1) * 512], k == 0, k == 7, [hT] + w1k, pa)
                    for k in range(8):
                        mm(P, pg[:], w1[:, k, D + ci * 128:D + (ci + 1) * 128], hT[:, k, q * 512:(q + 1) * 512], k == 0, k == 7, [hT] + w1k, pg)
                    act(P, s_[:], pg[:], AF.Sigmoid, [pg, G.cfb1], [s_], bias=G.cfb1[:, 8 + ci:9 + ci], scale=1.0)
                    stt(P, g_[:, PADC + q * 512:PADC + (q + 1) * 512], pa[:], G.cfb1[:, ci:ci + 1], s_[:], ALU.add, ALU.mult, [pa, G.cfb1, s_], [g_.k(q)])
                gk = whole(g_, 4)
                dk = whole(d_, 31)
                for q in range(4):
                    pc = ps[4 + q]
                    taps = [j for j in range(31) if 64 * j + 512 * q + 512 > PADC and 64 * j + 512 * q < PADC + SEQ]
                    for n, j in enumerate(taps):
                        c0 = 64 * j + 512 * q
                        mm(P, pc[:], d_[:, j, :], g_[:, c0:c0 + 512], n == 0, n == len(taps) - 1, gk + dk, pc)
                    act(P, u[:, ci, q * 512:(q + 1) * 512], pc[:], AF.Identity, [pc, G.cfdb], [u.k(ci)] if q == 0 else [], bias=G.cfdb[:, ci:ci + 1], scale=1.0) if q == 0 else \
                        P.op("act", lambda e, o=u[:, ci, q * 512:(q + 1) * 512], i=pc[:], b=G.cfdb[:, ci:ci + 1]: e.activation(out=o, in_=i, func=AF.Identity, bias=b, scale=1.0),
                             reads=[pc, G.cfdb], pwrites=[u.k(ci)])
        P.barrier()
        with ExitStack() as l2:
            sq = [P.sbuf(f"cfsq{i}", [128, 512], F32, l2) for i in range(2)]
            mean = P.sbuf("cfmean", [128, 512], F32, l2)
            rstd = P.sbuf("cfrstd", [128, 512], F32, l2)
            t1 = [P.sbuf(f"cft{i}", [128, 512], F32, l2) for i in range(2)]
            ps = G.ps
            uk = whole(u, 8)
            for q in range(4):
                sl = slice(q * 512, (q + 1) * 512)
                pm = ps[(q % 2) * 2]
                pv = ps[(q % 2) * 2 + 1]
                for ci in range(8):
                    mm(P, pm[:], G.onesm[:], u[:, ci, sl], ci == 0, ci == 7, [G.onesm] + uk, pm)
                for ci in range(8):
                    s_ = sq[ci % 2]
                    act(P, s_[:], u[:, ci, sl], AF.Square, uk, [s_])
                    mm(P, pv[:], G.onesm[:], s_[:], ci == 0, ci == 7, [G.onesm, s_], pv)
                cp(P, "act", mean[:], pm[:], [pm], [mean])
                tt(P, "dve", rstd[:], mean[:], mean[:], ALU.mult, [mean], [rstd])
                tt(P, "dve", rstd[:], pv[:], rstd[:], ALU.subtract, [pv, rstd], [rstd])
                act(P, rstd[:], rstd[:], AF.Sqrt, [rstd], [rstd], scale=1.0, bias=G.epsc[:, 0:1])
                P.op("dve", lambda e: e.reciprocal(out=rstd[:], in_=rstd[:]), reads=[rstd], writes=[rstd])
                for ci in range(8):
                    t_ = t1[ci % 2]
                    tt(P, "dve", t_[:], u[:, ci, sl], mean[:], ALU.subtract, uk + [mean], [t_])
                    tt(P, "dve", t_[:], t_[:], rstd[:], ALU.mult, [t_, rstd], [t_])
                    act(P, sT[:, ci, sl], t_[:], AF.Silu, [t_, G.cfln], [sT.k((ci, q))], scale=G.cfln[:, ci:ci + 1], bias=G.cfln[:, 8 + ci:9 + ci])
    P.barrier()


INPUT_SPECS = [
    ("x", [NS, SEQ, D], F32), ("ctx", [NS, CTX, D], F32), ("csT", [128, 8, 3], F32),
    ("w_mod", [2, D, 6 * D], F32), ("b_mod", [2, 6 * D], F32), ("norm1_g", [2, D], F32), ("norm2_g", [2, D], F32),
    ("final_g", [1, D], F32), ("ident_d", [128, 128], F32), ("iota_d", [128, CAP], F32), ("npos_d", [128, NT, 2], F32),
    ("w_in", [1, D, 2560], F32), ("caw_d", [128, 4, 4], F32), ("cab_d", [128, 4], F32),
    ("lbr_d", [128, 8], F32), ("lbi_d", [128, 8], F32), ("lam_d", [128, 8], F32),
    ("lru_wr", [1, 2, 8, 64, 64], F32), ("lru_wi", [1, 2, 8, 64, 64], F32),
    ("cbw_d", [128, 12, 3], F32), ("cbb_d", [128, 12], F32), ("fbias_d", [128, 8], F32), ("filt_c", [64, 4], F32),
    ("filt_w1", [1, 33, 64], F32), ("filt_w2", [1, 64, 64], F32), ("filt_w3", [1, 64, 2048], F32), ("filt_b3", [1, 2048], F32),
    ("zposT", [33, SEQ + 1], F32), ("win2", [NT, 128, 2, 512], F32), ("rot_d", [128, NT, 2], F32),
    ("csf", [NT, 128, 2, NT, 128], BF16), ("csi", [4, 2, 128, NT, 512], BF16),
    ("w_out", [1, D, D], F32),
    ("cf_w1", [1, D, 2 * D], F32), ("cfb1_d", [128, 16], F32), ("cfdw_d", [128, 8, 31], F32), ("cfdb_d", [128, 8], F32),
    ("cfln_d", [128, 16], F32), ("cf_w2", [1, D, D], F32), ("cf_b2", [1, D], F32),
    ("router_d", [2, 128, 8, NEXP], F32),
    ("exp_w1", [2, NEXP, D, D], F32), ("exp_w3", [2, NEXP, D, D], F32), ("exp_w2", [2, NEXP, D, D], F32),
]


def declare(nc, P, G, dbg):
    for name, shape, dt in INPUT_SPECS:
        setattr(G, name, T(name, nc.dram_tensor(name, list(shape), dt, kind="ExternalInput").ap()))
    kind = "ExternalOutput" if dbg else "Internal"
    G.mrow = P.dram("mrow", [2, 3, 6 * D], F32, kind)
    G.xres = P.dram("xres", [NS, SEQ, D], F32, kind)
    G.ktab = P.dram("ktab", [2, NT, 128, 2, 512], F32, kind)
    G.hyd = P.dram("hyd", [NS, 3, 4, 128, SEQ], BF16, kind)
    G.vtm = P.dram("vtm", [NS, 128, NT, 512], BF16, kind)
    G.yad = P.dram("yad", [NS, 4, 128, SEQ], BF16, kind)
    G.h2d = P.dram("h2d", [NS, SEQ, D], F32, kind)
    G.out = P.dram("out", [NS, SEQ, D], F32, "ExternalOutput")
    if dbg:
        G.dbg_ig = P.dram("dbg_ig", [NEXP, 128, 4, 4], F32, kind)
        G.dbg_xs = P.dram("dbg_xs", [4, 128, D], F32, kind)
        G.dbg_idx = P.dram("dbg_idx", [128, 4], mybir.dt.int32, kind)


def setup_consts(P, G):
    def ld(name, src, shape, dt=F32, q="sp"):
        t = P.sbuf(name, shape, dt)
        P.dma(q, t[:], src[:], writes=[t], semof=t)
        setattr(G, name, t)
        return t

    ld("ident", G.ident_d, [128, 128]); ld("iota", G.iota_d, [128, CAP]); ld("npos", G.npos_d, [128, NT, 2])
    ld("caw", G.caw_d, [128, 4, 4]); ld("cab", G.cab_d, [128, 4]); ld("lbr", G.lbr_d, [128, 8]); ld("lbi", G.lbi_d, [128, 8])
    lam = ld("lam", G.lam_d, [128, 8])
    ld("cbw", G.cbw_d, [128, 12, 3]); ld("cbb", G.cbb_d, [128, 12]); ld("fbias", G.fbias_d, [128, 8])
    ld("rot", G.rot_d, [128, NT, 2])
    ld("cfb1", G.cfb1_d, [128, 16]); ld("cfdw", G.cfdw_d, [128, 8, 31]); ld("cfdb", G.cfdb_d, [128, 8]); ld("cfln", G.cfln_d, [128, 16])
    G.epsc = P.sbuf("epsc", [128, 1], F32)
    G.onec = P.sbuf("onec", [128, 1], F32)
    G.onesm = P.sbuf("onesm", [128, 128], F32)
    G.onesb = P.sbuf("onesb", [1, 128], BF16)
    G.spc = P.sbuf("spc", [128, 8], F32)
    G.bd = P.sbuf("bd", [128, 16, 128], BF16)
    P.op("dve", lambda e: e.memset(G.epsc[:], EPS), writes=[G.epsc])
    P.op("dve", lambda e: e.memset(G.onec[:], 1.0), writes=[G.onec])
    P.op("dve", lambda e: e.memset(G.onesm[:], 1.0 / D), writes=[G.onesm])
    P.op("dve", lambda e: e.memset(G.onesb[:], 1.0), writes=[G.onesb])
    P.op("dve", lambda e: e.memset(G.bd[:], 0.0), writes=[G.bd])
    ep = P.sbuf("ep", [128, 8], F32)
    t = P.sbuf("ept", [128, 8], F32)
    act(P, ep[:], lam[:], AF.Exp, [lam], [ep], scale=-1.0)
    ts(P, "dve", t[:], ep[:], 1.0 / 3.0, ALU.mult, [ep], [t], s2=-0.5, op1=ALU.add)
    tt(P, "dve", t[:], t[:], ep[:], ALU.mult, [t, ep], [t])
    ts(P, "dve", t[:], t[:], 1.0, ALU.add, [t], [t])
    tt(P, "dve", t[:], t[:], ep[:], ALU.mult, [t, ep], [t])
    ts(P, "dve", G.spc[:], t[:], -8.0, ALU.mult, [t], [G.spc])
    G.spc2 = P.sbuf("spc2", [128, 8], F32)
    ts(P, "dve", G.spc2[:], t[:], -16.0, ALU.mult, [t], [G.spc2])
    for gi, wsrc in enumerate((G.lru_wr, G.lru_wi)):
        for k in range(2):
            for ci in range(4):
                for hh in range(2):
                    idx = (gi * 2 + k) * 4 + ci
                    P.dma("pool", G.bd[hh * 64:(hh + 1) * 64, idx, hh * 64:(hh + 1) * 64], wsrc[0, k, ci * 2 + hh],
                          writes=[G.bd], semof=G.bd)
    P.barrier()


def build(dbg=False, upto=99):
    nc = bass.Bass("TRN2", target_bir_lowering=False)
    es = ExitStack()
    with es:
        P = Prog(nc, es)
        G = Ctx()
        declare(nc, P, G, dbg)
        G.ps = [P.psum(f"ps{i}", [128, 512], F32) for i in range(8)]
        setup_consts(P, G)
        ada_stack = ExitStack()
        G.bg = stage_adaln(P, G, ada_stack)
        if upto < 1:
            while G.bg:
                G.bg.pop(0)()
            P.barrier()
            ada_stack.close()

        def xsrc(tensor, s):
            return lambda t: ([], tensor[s, t * 128:(t + 1) * 128, :])

        def moe_layer(l, src):
            with ExitStack() as ml:
                code = P.sbuf("code", [128, NT, NS * NEXP], F32, ml)
                pg = P.sbuf("pg", [128, NT, NS * NEXP], F32, ml)
                with ExitStack() as ns:
                    probs = P.sbuf("probs", [128, NT, NS * NEXP], F32, ns)
                    rt = P.sbuf("rt", [128, 8, NEXP], F32, ns)
                    P.dma("sp", rt[:], G.router_d[l], writes=[rt], semof=rt)
                    norm_jobs(P, G, NT, [dict(src_tile=xsrc(src, s), g_ap=G.norm2_g[l:l + 1, :], l=l, row=s, ish=3, isc=4, hT=None, h_tm=s,
                                              probs=probs, rt=rt, pcol=s * NEXP) for s in range(NS)])
                    route_stage(P, G, probs, code, pg)
                if upto >= 4 + 10 * l:
                    expert_stage(P, G, l, code, pg, 5)

        if upto >= 1:
            hy_filters(P, G)
            ada_stack.close()
            for s in range(NS):
                with ExitStack() as la:
                    hTc = P.sbuf("hTc", [128, 8, CTX], BF16, la)
                    h0 = P.sbuf("h0", [128, 4, 2], F32, la)
                    hT = P.sbuf("hT", [128, 8, SEQ], BF16, la)
                    with ExitStack() as lw:
                        win, wink = load_win(P, G, lw, 0, 1024)
                        norm_stage(P, G, la, xsrc(G.ctx, s), CTX // 128, G.norm1_g[0:1, :], 0, 2, 0, 1, hT=hTc)
                        lru_part(P, G, win, wink, hTc, CTX, None, hfin=h0)
                        norm_stage(P, G, la, xsrc(G.x, s), NT, G.norm1_g[0:1, :], 0, s, 0, 1, hT=hT)
                        lru_part(P, G, win, wink, hT, SEQ, h0, ya_s=s)
                    with ExitStack() as lw:
                        win, wink = load_win(P, G, lw, 1024, 2560)
                        hy_inproj(P, G, win, wink, hT, s)
                if upto >= 2:
                    with ExitStack() as lb:
                        zT = P.sbuf("zT", [128, 4, SEQ], BF16, lb)
                        x1T = P.sbuf("x1T", [128, 4, SEQ], BF16, lb)
                        x2T = P.sbuf("x2T", [128, 4, SEQ], BF16, lb)
                        yaT = P.sbuf("yaT", [128, 4, SEQ], BF16, lb)
                        z_tm = P.sbuf("z_tm", [128, NT, 512], BF16, lb)
                        for ci in range(4):
                            P.dma("sp", zT[:, ci, :], G.hyd[s, 0, ci], writes=[zT.k((ci, 0))], semof=zT)
                            P.dma("sp", x1T[:, ci, :], G.hyd[s, 1, ci], pwrites=[x1T], semof=x1T)
                            P.dma("sp", x2T[:, ci, :], G.hyd[s, 2, ci], pwrites=[x2T], semof=x2T)
                            P.dma("sp", yaT[:, ci, :], G.yad[s, ci], pwrites=[yaT], semof=yaT)
                        P.dma("sp", z_tm[:], G.vtm[s], writes=[z_tm], semof=z_tm)
                        P.barrier()
                        hy_conv(P, G, 0, z_tm, zT, x1T, True)
                        hy_conv(P, G, 1, z_tm, zT, x2T, False)
                        kl = [(yaT, i) for i in range(4)] + [(zT, i) for i in range(4)]
                        mixer_out0(P, G, s, kl)
        def snap(name):
            if not dbg:
                return
            t = P.dram("snap_" + name, [NS, SEQ, D], F32, "ExternalOutput")
            for s in range(NS):
                P.dma("sp", t[s], G.xres[s], writes=[t.k(s)], semof=t)
            P.barrier()

        if upto >= 3:
            moe_layer(0, G.xres)
            snap("xl0")
        if upto >= 11:
            for s in range(NS):
                with ExitStack() as la:
                    hT = P.sbuf("hT1", [128, 8, SEQ], BF16, la)
                    norm_stage(P, G, la, xsrc(G.xres, s), NT, G.norm1_g[1:2, :], 1, s, 0, 1, hT=hT)
                    conformer(P, G, s, hT, hT)
                    mixer_out(P, G, 1, s, [(hT, i) for i in range(8)], G.cf_w2[0], G.cf_b2[0:1, :], 2)
            snap("xl1a")
        if upto >= 12:
            moe_layer(1, G.xres)
        if upto >= 20:
            final_norm(P, G)
        P.barrier()
        P.emit()
    return nc


def mixer_out0(P, G, s, kl):
    mixer_out(P, G, 0, s, kl, G.w_out[0], None, 2, src=G.x)


def final_norm(P, G):
    with ExitStack() as loc:
        NB, AHEAD = 6, 4
        gb = P.sbuf("fg", [128, D], F32, loc)
        xt = [P.sbuf(f"fx{i}", [128, D], F32, loc) for i in range(NB)]
        junks = [P.sbuf(f"fjunk{i}", [128, D], F32, loc) for i in range(2)]
        st = [P.sbuf(f"fst{i}", [128, 4], F32, loc) for i in range(NB)]
        P.dma("sp", gb[:], G.final_g[0:1, :].broadcast_to([128, D]), writes=[gb], semof=gb)
        items = [(t, s) for t in range(NT) for s in range(NS)]

        def load(i):
            t, s = items[i]
            x = xt[i % NB]
            P.dma("sp", x[:], G.xres[s, t * 128:(t + 1) * 128, :], writes=[x], semof=x)

        for i in range(AHEAD):
            load(i)
        for i, (t, s) in enumerate(items):
            if i + AHEAD < len(items):
                load(i + AHEAD)
            x = xt[i % NB]
            s_ = st[i % NB]
            junk = junks[i % 2]
            act(P, junk[:], x[:], AF.Square, [x], [junk, s_.k(0)], accum_out=s_[:, 0:1])
            act(P, s_[:, 1:2], s_[:, 0:1], AF.Sqrt, [s_.k(0)], [s_.k(1)], scale=1.0 / D, bias=G.epsc[:, 0:1])
            P.op("dve", lambda e, o=s_[:, 2:3], i_=s_[:, 1:2]: e.reciprocal(out=o, in_=i_), reads=[s_.k(1)], writes=[s_.k(2)])
            stt(P, x[:], x[:], s_[:, 2:3], gb[:], ALU.mult, ALU.mult, [x, s_.k(2), gb], [x])
            P.dma("sp", G.out[s, t * 128:(t + 1) * 128, :], x[:], reads=[x], writes=[G.out.k((s, t))], semof=x)
    P.barrier()


_CONST = {}


def host_consts():
    if _CONST:
        return _CONST
    import ml_dtypes
    bf = ml_dtypes.bfloat16
    n = SEQ
    N2 = 2 * n
    t = np.arange(n, dtype=np.float64)
    phi = 2 * np.pi * np.outer(t + 0.5, t + 0.5) / N2
    C = np.cos(phi); S = np.sin(phi)
    CS = np.stack([C, S])
    csf = CS.reshape(2, NT, 128, NT, 128).transpose(3, 2, 0, 1, 4)
    csi = CS.reshape(2, NT, 128, 4, 512).transpose(3, 0, 2, 1, 4)
    _CONST["csf"] = np.ascontiguousarray(csf).astype(bf)
    _CONST["csi"] = np.ascontiguousarray(csi).astype(bf)
    al = np.pi * (t + 0.5) / N2
    rot = np.stack([np.cos(al), np.sin(al)], -1) * (2.0 / N2)
    _CONST["rot_d"] = np.ascontiguousarray(rot.reshape(NT, 128, 2).transpose(1, 0, 2)).astype(np.float32)
    f32 = np.float32
    tt_ = np.linspace(0.0, 1.0, n, dtype=f32)[:, None]
    bands = 16
    w = (f32(2.0 * math.pi / n) * np.arange(n, dtype=f32))[:, None]
    fr = np.linspace(1e-4, bands - 1, bands, dtype=f32)[None, :]
    z = np.concatenate([tt_, np.cos(fr * w), -np.sin(fr * w)], axis=-1).astype(f32)
    zT = np.zeros((33, n + 1), f32); zT[:, :n] = z.T
    _CONST["zposT"] = zT
    deltas = np.abs(np.linspace(math.log(1e-2) / 1.5, math.log(1e-2) / 0.3, 512, dtype=f32))
    window = (np.exp(-tt_ * deltas[None, :]) + f32(0.05)).astype(f32)
    wsh = np.zeros_like(window); wsh[:-1] = window[1:]
    w2 = np.stack([window, wsh], 1)
    _CONST["win2"] = np.ascontiguousarray(w2.reshape(NT, 128, 2, 512)).astype(f32)
    _CONST["ident_d"] = np.eye(128, dtype=f32)
    _CONST["iota_d"] = np.tile(np.arange(CAP, dtype=f32)[None, :], (128, 1))
    npos = np.zeros((128, NT, 2), f32); npos[:, :, 0] = np.arange(NT)[None, :]; npos[:, :, 1] = np.arange(128)[:, None]
    _CONST["npos_d"] = npos
    return _CONST


def _cols(v, nt):
    return np.ascontiguousarray(np.asarray(v, np.float32).reshape(nt, 128).T)


def host_shared(inputs):
    f = lambda a: np.ascontiguousarray(np.asarray(a, dtype=np.float32))
    m = dict(host_consts())
    for k in ("w_mod", "b_mod", "norm1_g", "norm2_g", "w_in", "lru_wr", "lru_wi", "filt_w1", "filt_w2", "filt_w3", "filt_b3",
              "w_out", "cf_w1", "cf_w2", "cf_b2", "exp_w1", "exp_w3", "exp_w2"):
        m[k] = f(inputs[k])
    m["final_g"] = f(inputs["final_g"]).reshape(1, D)
    m["caw_d"] = np.ascontiguousarray(f(inputs["conv_a_w"])[0].reshape(4, 4, 128).transpose(2, 1, 0))
    m["cab_d"] = _cols(inputs["conv_a_b"][0], 4)
    m["lbr_d"] = _cols(np.asarray(inputs["lru_br"])[0].reshape(-1), 8)
    m["lbi_d"] = _cols(np.asarray(inputs["lru_bi"])[0].reshape(-1), 8)
    m["lam_d"] = _cols(np.asarray(inputs["lru_lam"])[0].reshape(-1), 8)
    m["cbw_d"] = np.ascontiguousarray(f(inputs["conv_b_w"])[0].reshape(3, 12, 128).transpose(2, 1, 0))
    m["cbb_d"] = _cols(inputs["conv_b_b"][0], 12)
    m["fbias_d"] = _cols(np.asarray(inputs["filt_bias"])[0].reshape(-1), 8)
    m["filt_c"] = np.ascontiguousarray(np.stack([f(inputs["filt_b1"])[0], f(inputs["filt_b2"])[0],
                                                 f(inputs["filt_freq"])[0, 0], f(inputs["filt_freq"])[0, 1]], -1))
    m["cfb1_d"] = _cols(inputs["cf_b1"][0], 16)
    m["cfdw_d"] = np.ascontiguousarray(f(inputs["cf_dw_w"])[0].reshape(31, 8, 128).transpose(2, 1, 0))
    m["cfdb_d"] = _cols(inputs["cf_dw_b"][0], 8)
    m["cfln_d"] = np.ascontiguousarray(np.concatenate([_cols(inputs["cf_ln_g"][0], 8), _cols(inputs["cf_ln_b"][0], 8)], 1))
    m["router_d"] = np.ascontiguousarray(f(inputs["router"]).reshape(2, 8, 128, NEXP).transpose(0, 2, 1, 3))
    return m


def host_prep(inputs, core, shared=None):
    f = lambda a: np.ascontiguousarray(np.asarray(a, dtype=np.float32))
    m = dict(shared if shared is not None else host_shared(inputs))
    s0 = core * NS
    cc = np.stack([np.asarray(inputs["c"])[s0], np.asarray(inputs["c"])[s0 + 1], np.asarray(inputs["c_ctx"])]).astype(np.float32)
    m["x"] = f(inputs["x"][s0:s0 + NS])
    m["ctx"] = f(inputs["ctx"][s0:s0 + NS])
    m["csT"] = f(cc.T.reshape(8, 128, 3).transpose(1, 0, 2))
    return m


def kernel(**inputs):
    nc = build()
    shared = host_shared(inputs)
    in_maps = [host_prep(inputs, c, shared) for c in range(NCORES)]
    res = run_bass_kernel_spmd(nc, in_maps, core_ids=list(range(NCORES)))
    return np.concatenate([np.asarray(r["out"], dtype=np.float32) for r in res.results], axis=0)
```

```python
from contextlib import ExitStack
import math
import numpy as np
import concourse.bass as bass
import concourse.mybir as mybir
from concourse.bass_utils import run_bass_kernel_spmd

F32 = mybir.dt.float32
BF16 = mybir.dt.bfloat16
ALU = mybir.AluOpType
AF = mybir.ActivationFunctionType

NCORES = 8
D = 1024
SEQ = 2048
NS = 2
NT = SEQ // 128
CTX = 256
EPS = 1e-6
NEXP = 16
CAP = 256


class Trk:
    def __init__(self, name, base=None):
        self.name = name
        self.writers = []
        self.readers = []
        self.base = base if base is not None else self
        self.subs = {}

    def k(self, key):
        s = self.subs.get(key)
        if s is None:
            s = Trk(f"{self.name}.{key}", self.base)
            self.subs[key] = s
        return s


class T(Trk):
    def __init__(self, name, h):
        super().__init__(name)
        self.h = h

    def __getitem__(self, idx):
        return self.h[idx]


ENGS = ("pe", "act", "dve", "pool", "sp")


class Prog:
    def __init__(self, nc, es):
        self.nc = nc
        self.es = es
        self.ops = {e: [] for e in ENGS}
        self.ecnt = {e: 0 for e in ENGS}
        self.seen = {e: {} for e in ENGS}
        self.sems = {}
        self.dcnt = {}
        self.nsem = 0
        self.uid = 0
        self.dkey = {}
        self.dfree = []
        self.dfree_sw = []
        self.dall = []
        self.dall_sw = []
        self.dpool = 0

    def _sem(self, key):
        s = self.sems.get(key)
        if s is None:
            self.nsem += 1
            s = self.es.enter_context(self.nc.semaphore(f"sem{self.nsem}"))
            self.sems[key] = s
        return s

    def sbuf(self, name, shape, dtype, es=None):
        es = es or self.es
        self.uid += 1
        h = es.enter_context(self.nc.sbuf_tensor(f"{name}_{self.uid}", list(shape), dtype))
        return T(name, h)

    def psum(self, name, shape, dtype, es=None):
        es = es or self.es
        self.uid += 1
        h = es.enter_context(self.nc.psum_tensor(f"{name}_{self.uid}", list(shape), dtype))
        return T(name, h)

    def dram(self, name, shape, dtype, kind="Internal"):
        h = self.nc.dram_tensor(name, list(shape), dtype, kind=kind)
        return T(name, h.ap())

    def _waits(self, eng, deps):
        need = {}
        for key, cnt in deps:
            if key[0] == "E" and key[1] == eng and eng == "pe":
                continue
            if key[0] == "D":
                cnt = self.dcnt[key]
            if cnt > need.get(key, 0):
                need[key] = cnt
        for key, cnt in need.items():
            if self.seen[eng].get(key, 0) >= cnt:
                continue
            self.seen[eng][key] = cnt
            self.ops[eng].append(("wait", key, cnt))

    def _deps(self, reads, writes, pwrites):
        deps = []
        for r in reads:
            deps += r.writers
        for w in writes:
            deps += w.writers
            deps += w.readers
        for w in pwrites:
            deps += w.readers
        return deps

    def _commit(self, ev, reads, writes, pwrites):
        for r in reads:
            r.readers.append(ev)
        for w in writes:
            w.writers = [ev]
            w.readers = []
        for w in pwrites:
            w.writers.append(ev)

    def op(self, eng, fn, reads=(), writes=(), pwrites=()):
        self._waits(eng, self._deps(reads, writes, pwrites))
        key = ("E", eng)
        self._sem(key)
        self.ecnt[eng] += 1
        ev = (key, self.ecnt[eng])
        self.ops[eng].append(("op", fn, key, 1))
        self._commit(ev, reads, writes, pwrites)

    def dma(self, q, out, in_, reads=(), writes=(), pwrites=(), semof=None, **kw):
        self._waits(q, self._deps(reads, writes, pwrites))
        sw = (q == "pool")
        key = self.dkey.get((id(semof.base), sw))
        if key is None:
            free = self.dfree_sw if sw else self.dfree
            if free:
                idx = free.pop()
            else:
                idx = (self.dpool, sw)
                self.dpool += 1
                (self.dall_sw if sw else self.dall).append(idx)
            key = ("D", idx)
            self.dkey[(id(semof.base), sw)] = key
        self._sem(key)
        self.dcnt[key] = self.dcnt.get(key, 0) + 16
        ev = (key, self.dcnt[key])
        fn = kw.pop("fn", None)
        if fn is None:
            fn = lambda e: e.dma_start(out=out, in_=in_, **kw)
        self.ops[q].append(("op", fn, key, 16))
        self._commit(ev, reads, writes, pwrites)

    def barrier(self):
        for e in ENGS:
            deps = [(("E", e2), self.ecnt[e2]) for e2 in ENGS if self.ecnt[e2] > 0 and not (e == e2 == "pe")]
            deps += [(k, c) for k, c in self.dcnt.items()]
            self._waits(e, deps)
        self.dkey = {}
        self.dfree = list(self.dall)
        self.dfree_sw = list(self.dall_sw)

    def emit(self):
        nc = self.nc
        hmap = {"pe": "tensor", "act": "scalar", "dve": "vector", "pool": "gpsimd", "sp": "sync"}
        with nc.Block() as block:
            for e in ENGS:
                if not self.ops[e]:
                    continue

                def body(engh, e=e):
                    for item in self.ops[e]:
                        if item[0] == "wait":
                            engh.wait_ge(self.sems[item[1]], item[2])
                        else:
                            ins = item[1](engh)
                            ins.then_inc(self.sems[item[2]], item[3])

                getattr(block, hmap[e])(body)


def mm(P, out, lhsT, rhs, start, stop, reads, wr):
    P.op("pe", lambda e: e.matmul(out, lhsT=lhsT, rhs=rhs, start=start, stop=stop),
         reads=reads, writes=[wr] if start else [], pwrites=[] if start else [wr])


def tr(P, out, in_, ident, reads, wr, first=True):
    P.op("pe", lambda e: e.transpose(out, in_, ident), reads=reads,
         writes=[wr] if first else [], pwrites=[] if first else [wr])


def act(P, out, in_, func, reads, writes, eng="act", **kw):
    P.op(eng, lambda e: e.activation(out=out, in_=in_, func=func, **kw), reads=reads, writes=writes)


def tt(P, eng, out, in0, in1, op, reads, writes):
    P.op(eng, lambda e: e.tensor_tensor(out=out, in0=in0, in1=in1, op=op), reads=reads, writes=writes)


def ts(P, eng, out, in0, s1, op0, reads, writes, s2=None, op1=None, **kw):
    if op1 is None:
        P.op(eng, lambda e: e.tensor_scalar(out=out, in0=in0, scalar1=s1, scalar2=None, op0=op0, **kw),
             reads=reads, writes=writes)
    else:
        P.op(eng, lambda e: e.tensor_scalar(out=out, in0=in0, scalar1=s1, scalar2=s2, op0=op0, op1=op1, **kw),
             reads=reads, writes=writes)


def stt(P, out, in0, scalar, in1, op0, op1, reads, writes):
    P.op("dve", lambda e: e.scalar_tensor_tensor(out=out, in0=in0, scalar=scalar, in1=in1, op0=op0, op1=op1),
         reads=reads, writes=writes)


def cp(P, eng, out, in_, reads, writes):
    if eng == "act":
        P.op(eng, lambda e: e.copy(out=out, in_=in_), reads=reads, writes=writes)
    else:
        P.op(eng, lambda e: e.tensor_copy(out=out, in_=in_), reads=reads, writes=writes)


class Ctx:
    pass


def stage_adaln(P, G, ph):
    sT = P.sbuf("sT", [128, 8, 3], F32, ph)
    sS = P.sbuf("sS", [128, 8, 3], F32, ph)
    wb = [P.sbuf(f"wmod{i}", [128, 8, 256], F32, ph) for i in range(3)]
    bm = [P.sbuf(f"bm{i}", [3, 256], F32, ph) for i in range(3)]
    mr = [P.sbuf(f"mr{i}", [3, 256], F32, ph) for i in range(3)]
    ps = G.ps
    P.dma("sp", sT[:], G.csT[:], writes=[sT], semof=sT)
    act(P, sS[:], sT[:], AF.Silu, [sT], [sS])
    thunks = []
    chunks = [(l, ch) for l in range(2) for ch in range(24)]

    def load(it):
        l, ch = chunks[it]
        w = wb[it % 3]
        b = bm[it % 3]
        P.dma("pool", w[:], G.w_mod[l, :, ch * 256:(ch + 1) * 256].rearrange("(kt p) f -> p kt f", p=128), writes=[w], semof=w)
        P.dma("pool", b[:], G.b_mod[l:l + 1, ch * 256:(ch + 1) * 256].broadcast_to([3, 256]), writes=[b], semof=b)

    def work(it):
        l, ch = chunks[it]
        w = wb[it % 3]
        b = bm[it % 3]
        m = mr[it % 3]
        p = ps[6 + it % 2]
        for kt in range(8):
            mm(P, p[0:3, 0:256], sS[:, kt, :], w[:, kt, :], kt == 0, kt == 7, [sS, w], p)
        tt(P, "dve", m[:], p[0:3, 0:256], b[:], ALU.add, [p, b], [m])
        P.dma("pool", G.mrow[l, :, ch * 256:(ch + 1) * 256], m[:], reads=[m], writes=[G.mrow.k((l, ch))], semof=m)
        if it + 2 < len(chunks):
            load(it + 2)

    load(0)
    load(1)
    for it in range(len(chunks)):
        thunks.append(lambda it=it: work(it))
    return thunks


def norm_stage(P, G, ph, src_tile, ntiles, g_ap, l, row, ish, isc, hT=None, h_tm=None, probs=None, rt=None, pcol=0):
    norm_jobs(P, G, ntiles, [dict(src_tile=src_tile, g_ap=g_ap, l=l, row=row, ish=ish, isc=isc, hT=hT, h_tm=h_tm, probs=probs, rt=rt, pcol=pcol)])


def norm_jobs(P, G, ntiles, jobs):
    nj = len(jobs)
    with ExitStack() as loc:
        ps = G.ps
        for ji, J in enumerate(jobs):
            J["gb"] = P.sbuf("gb", [128, D], F32, loc)
            J["A"] = P.sbuf("A", [128, D], F32, loc)
            J["B"] = P.sbuf("B", [128, D], F32, loc)
            J["xt"] = [P.sbuf(f"xt{i}", [128, D], F32, loc) for i in range(3)]
            J["hn"] = [P.sbuf(f"hn{i}", [128, D], F32, loc) for i in range(3)]
            J["junk"] = P.sbuf("junk", [128, D], F32, loc)
            J["st"] = [P.sbuf(f"st{i}", [128, 4], F32, loc) for i in range(3)]
            if J["probs"] is not None:
                J["hTf"] = [P.sbuf(f"hTf{i}", [128, 8, 128], F32, loc) for i in range(2)]
                J["sm"] = [P.sbuf(f"sm{i}", [128, 24], F32, loc) for i in range(2)]
            gb, A, B, l, row, isc, ish = J["gb"], J["A"], J["B"], J["l"], J["row"], J["isc"], J["ish"]
            P.dma("sp", gb[:], J["g_ap"].broadcast_to([128, D]), writes=[gb], semof=gb)
            P.dma("sp", A[:], G.mrow[l, row:row + 1, isc * D:(isc + 1) * D].broadcast_to([128, D]),
                  reads=[G.mrow.k((l, isc * 4 + i_)) for i_ in range(4)], writes=[A], semof=A)
            P.dma("sp", B[:], G.mrow[l, row:row + 1, ish * D:(ish + 1) * D].broadcast_to([128, D]),
                  reads=[G.mrow.k((l, ish * 4 + i_)) for i_ in range(4)], writes=[B], semof=B)
            stt(P, A[:], A[:], 1.0, gb[:], ALU.add, ALU.mult, [A, gb], [A])
            if nj == 1:
                J["trb"] = lambda t, half: ps[(t % 2) * 2 + half]
                J["lgb"] = lambda t: ps[4 + t % 2]
            else:
                J["trb"] = lambda t, half, ji=ji: ps[ji * 3 + half]
                J["lgb"] = lambda t, ji=ji: ps[ji * 3 + 2]
        def nload(J, t):
            x = J["xt"][t % 3]
            rd, ap = J["src_tile"](t)
            P.dma("sp", x[:], ap, reads=rd, writes=[x], semof=x)

        for t0 in range(min(2, ntiles)):
            for J in jobs:
                nload(J, t0)
        for t in range(ntiles):
            for J in jobs:
                hT, h_tm, probs, rt, pcol = J["hT"], J["h_tm"], J["probs"], J["rt"], J["pcol"]
                A, B = J["A"], J["B"]
                if t + 2 < ntiles:
                    nload(J, t + 2)
                x = J["xt"][t % 3]
                h = J["hn"][t % 3]
                s_ = J["st"][t % 3]
                junk = J["junk"]
                act(P, junk[:], x[:], AF.Square, [x], [junk, s_.k(0)], accum_out=s_[:, 0:1])
                act(P, s_[:, 1:2], s_[:, 0:1], AF.Sqrt, [s_.k(0)], [s_.k(1)], scale=1.0 / D, bias=G.epsc[:, 0:1])
                P.op("dve", lambda e, o=s_[:, 2:3], i=s_[:, 1:2]: e.reciprocal(out=o, in_=i), reads=[s_.k(1)], writes=[s_.k(2)])
                stt(P, h[:], x[:], s_[:, 2:3], A[:], ALU.mult, ALU.mult, [x, s_.k(2), A], [h])
                tt(P, "dve", h[:], h[:], B[:], ALU.add, [h, B], [h])
                if h_tm is not None:
                    P.dma("sp", G.h2d[h_tm, t * 128:(t + 1) * 128, :], h[:], reads=[h], writes=[G.h2d.k((h_tm, t))], semof=h)
                if hT is not None or probs is not None:
                    for half in range(2):
                        p = J["trb"](t, half)
                        for j in range(4):
                            k = half * 4 + j
                            tr(P, p[:, j * 128:(j + 1) * 128], h[:, k * 128:(k + 1) * 128], G.ident[:], [h, G.ident], p, first=(j == 0))
                        pv = p[:].rearrange("p (j n) -> p j n", j=4)
                        if hT is not None:
                            cp(P, "act" if half == 0 else "dve", hT[:, half * 4:half * 4 + 4, t * 128:(t + 1) * 128], pv, [p], [hT.k((t, half))])
                        if probs is not None:
                            f = J["hTf"][t % 2]
                            cp(P, "dve" if half == 0 else "act", f[:, half * 4:half * 4 + 4, :], pv, [p], [f.k(half)])
                if probs is not None:
                    f = J["hTf"][t % 2]
                    pl = J["lgb"](t)
                    m = J["sm"][t % 2]
                    for k in range(8):
                        mm(P, pl[:, 0:16], f[:, k, :], rt[:, k, :], k == 0, k == 7, [f.k(0), f.k(1), rt], pl)
                    P.op("dve", lambda e, o=m[:, 16:17], i=pl[:, 0:16]: e.tensor_reduce(out=o, in_=i, axis=mybir.AxisListType.X, op=ALU.max),
                         reads=[pl], writes=[m.k(1)])
                    ts(P, "dve", m[:, 17:18], m[:, 16:17], -1.0, ALU.mult, [m.k(1)], [m.k(2)])
                    act(P, m[:, 0:16], pl[:, 0:16], AF.Exp, [pl, m.k(2)], [m.k(0), m.k(3)], bias=m[:, 17:18], scale=1.0, accum_out=m[:, 18:19])
                    P.op("dve", lambda e, o=m[:, 19:20], i=m[:, 18:19]: e.reciprocal(out=o, in_=i), reads=[m.k(3)], writes=[m.k(4)])
                    ts(P, "dve", probs[:, t, pcol:pcol + 16], m[:, 0:16], m[:, 19:20], ALU.mult, [m.k(0), m.k(4)], [probs.k((t, pcol))])
    P.barrier()


def whole(Tobj, n):
    return [Tobj.k(i) for i in range(n)]


def route_stage(P, G, probs, code_tm, pg_tm):
    NP = NS * NEXP
    with ExitStack() as loc:
        PT = P.sbuf("PT", [NP, SEQ], F32, loc)
        msk = P.sbuf("msk", [NP, SEQ], F32, loc)
        cum = P.sbuf("cum", [NP, SEQ], F32, loc)
        ones = P.sbuf("ones", [NP, SEQ], F32, loc)
        sc = P.sbuf("sc", [NP, 8], F32, loc)
        ps = G.ps
        for q in range(4):
            for j in range(4):
                t = q * 4 + j
                tr(P, ps[q][0:NP, j * 128:(j + 1) * 128], probs[:, t, :], G.ident[:], [probs, G.ident], ps[q], first=(j == 0))
            cp(P, "act" if q % 2 else "dve", PT[:, q * 512:(q + 1) * 512], ps[q][0:NP, :], [ps[q]], [PT.k(q)])
        PTk = whole(PT, 4)
        P.op("dve", lambda e: e.memset(ones[:], 1.0), writes=[ones])
        mid, cnt, g = (sc[:, i:i + 1] for i in range(3))
        P.op("dve", lambda e: e.memset(mid, 0.5), writes=[sc.k(0)])
        NIT = 24
        for it in range(NIT):
            w_next = 0.5 ** (it + 2)
            ts(P, "dve", msk[:], PT[:], mid, ALU.is_ge, PTk + [sc.k(0)], [msk, sc.k(1)], s2=0.0, op1=ALU.add, accum_out=cnt)
            ts(P, "dve", g, cnt, float(CAP), ALU.is_ge, [sc.k(1)], [sc.k(2)], s2=2.0 * w_next, op1=ALU.mult)
            stt(P, mid, g, -w_next, mid, ALU.add, ALU.add, [sc.k(2), sc.k(0)], [sc.k(0)])
        ts(P, "dve", mid, mid, -(0.5 ** (NIT + 1)), ALU.add, [sc.k(0)], [sc.k(0)])
        ts(P, "dve", msk[:], PT[:], mid, ALU.is_ge, PTk + [sc.k(0)], [msk])
        P.op("dve", lambda e: e.tensor_tensor_scan(out=cum[:], data0=ones[:], data1=msk[:], initial=0.0, op0=ALU.mult, op1=ALU.add),
             reads=[ones, msk], writes=[cum])
        tt(P, "dve", cum[:], cum[:], msk[:], ALU.mult, [cum, msk], [cum])
        ts(P, "dve", cum[:], cum[:], -1.0, ALU.add, [cum], [cum])
        tt(P, "dve", msk[:], msk[:], PT[:], ALU.mult, [msk] + PTk, [msk])
        for src, dst, pb in ((cum, code_tm, ps[4]), (msk, pg_tm, ps[5])):
            for t in range(NT):
                tr(P, pb[:, t * NP:(t + 1) * NP], src[0:NP, t * 128:(t + 1) * 128], G.ident[0:NP, 0:NP], [src, G.ident], pb, first=(t == 0))
            cp(P, "act", dst[:].rearrange("p t e -> p (t e)"), pb[:, 0:NT * NP], [pb], [dst])
    P.barrier()


def expert_stage(P, G, l, code_tm, pg_tm, igate):
    I32 = mybir.dt.int32
    h2flat = G.h2d[:].rearrange("s n d -> (s n) d")
    xrflat = G.xres[:].rearrange("s n d -> (s n) d")
    with ExitStack() as loc:
        wb = [P.sbuf(f"wexp{i}", [128, 8, D], BF16, loc) for i in range(5)]
        S = [P.sbuf(f"S{s}", [128, NT, CAP], BF16, loc) for s in range(NS)]
        R = [P.sbuf(f"R{s}", [128, NT, 4], BF16, loc) for s in range(NS)]
        rl = [P.sbuf(f"rlo{s}", [128, NT], F32, loc) for s in range(NS)]
        xs = [[P.sbuf(f"xs{i}_{c}", [128, D], F32, loc) for c in range(4)] for i in range(2)]
        ig = [P.sbuf(f"ig{i}", [128, 4, 4], F32, loc) for i in range(2)]
        idx = [P.sbuf(f"idx{i}", [128, 4], I32, loc) for i in range(2)]
        gate = [P.sbuf(f"gate{i}", [128, 4], F32, loc) for i in range(2)]
        xsT = [P.sbuf(f"xsT{i}", [128, 8, NS * CAP], BF16, loc) for i in range(2)]
        actT = P.sbuf("actT", [128, 8, NS * CAP], BF16, loc)
        tmp = [P.sbuf(f"sil{i}", [128, NS * CAP], F32, loc) for i in range(2)]
        ysb = [P.sbuf(f"ysb{c}", [128, D], F32, loc) for c in range(4)]
        g2 = [P.sbuf(f"g2_{s}", [128, D], F32, loc) for s in range(NS)]
        ps = G.ps
        for s in range(NS):
            P.dma("sp", g2[s][:], G.mrow[l, s:s + 1, igate * D:(igate + 1) * D].broadcast_to([128, D]), writes=[g2[s]], semof=g2[s])
            cp(P, "dve", R[s][:, :, 0:2], G.npos[:], [G.npos], [R[s].k(0)])
        st = {"wi": 0, "pi": 0}
        wts = {}

        offs = P.sbuf("offs", [128, 4], F32, loc)
        for c in range(4):
            P.op("dve", lambda en, c=c: en.memset(offs[:, c:c + 1], float((c // 2) * SEQ)), pwrites=[offs])

        def prep1_ops(e):
            ops = []
            for s in range(NS):
                col = s * NEXP + e
                for t in range(NT):
                    ops.append(lambda s=s, t=t, col=col: ts(P, "dve", S[s][:, t, :], G.iota[:, 0:CAP], code_tm[:, t, col:col + 1], ALU.is_equal,
                                                           [G.iota, code_tm], [S[s].k(t)]))
                ops.append(lambda s=s, col=col: cp(P, "dve", R[s][:, :, 2:3], pg_tm[:, :, col:col + 1], [pg_tm], [R[s].k(1)]))
                ops.append(lambda s=s, col=col: tt(P, "dve", rl[s][:].rearrange("p (t o) -> p t o", o=1), pg_tm[:, :, col:col + 1], R[s][:, :, 2:3],
                                                   ALU.subtract, [pg_tm, R[s].k(1)], [rl[s]]))
                ops.append(lambda s=s: cp(P, "dve", R[s][:, :, 3:4], rl[s][:].rearrange("p (t o) -> p t o", o=1), [rl[s]], [R[s].k(2)]))
            return ops

        def prep2(e):
            b = e % 2
            p = ps[6 + (e % 2)]
            for s in range(NS):
                Sk = whole(S[s], NT)
                Rk = [R[s].k(0), R[s].k(1), R[s].k(2)]
                for ch in range(2):
                    c = s * 2 + ch
                    for t in range(NT):
                        mm(P, p[:, c * 4:(c + 1) * 4], S[s][:, t, ch * 128:(ch + 1) * 128], R[s][:, t, :], t == 0, t == NT - 1, Sk + Rk,
                           p if c == 0 else p.k(c))
            cp(P, "dve", ig[b][:].rearrange("p c f -> p (c f)"), p[:, 0:16], [p, p.k(1), p.k(2), p.k(3)], [ig[b]])
            stt(P, gate[b][:].rearrange("p (c o) -> p c o", o=1), ig[b][:, :, 0:1], 128.0, ig[b][:, :, 1:2], ALU.mult, ALU.add, [ig[b]], [gate[b]])
            tt(P, "dve", idx[b][:], gate[b][:], offs[:], ALU.add, [gate[b], offs], [idx[b]])
            tt(P, "dve", gate[b][:].rearrange("p (c o) -> p c o", o=1), ig[b][:, :, 2:3], ig[b][:, :, 3:4], ALU.add, [ig[b], gate[b]], [gate[b]])
            for c in range(4):
                x_ = xs[b][c]
                P.dma("pool", None, None, reads=[idx[b]], writes=[x_], semof=x_,
                      fn=lambda en, o=x_[:, :], ia=idx[b][:, c:c + 1]: en.indirect_dma_start(
                          out=o, out_offset=None, in_=h2flat, in_offset=bass.IndirectOffsetOnAxis(ap=ia, axis=0)))
            if e == 0:
                wts[0] = (wload("exp_w1", 0, wb[0]), wload("exp_w3", 0, wb[1]))
                wts["w2", 0] = wload("exp_w2", 0, wb[4])

        def wload(nm, e, w):
            for hk in range(2):
                P.dma("pool", w[:, hk * 4:(hk + 1) * 4, :],
                      getattr(G, nm)[l, e, hk * 512:(hk + 1) * 512, :].rearrange("(kt p) f -> p kt f", p=128),
                      writes=[w] if hk == 0 else [], pwrites=[] if hk == 0 else [w], semof=w)
            return w

        def compute(e):
            b = e % 2
            w1, w3 = wts.pop(e)
            w2 = wts.pop(("w2", e))
            xT = xsT[b]
            nxt = prep1_ops(e + 1) if e + 1 < NEXP else []
            if e + 1 < NEXP:
                e1 = e + 1
                wts[e1] = (wload("exp_w1", e1, wb[(e1 % 2) * 2]), wload("exp_w3", e1, wb[(e1 % 2) * 2 + 1]))
            for c in range(4):
                x_ = xs[b][c]
                for half in range(2):
                    p = ps[st["pi"] % 6]; st["pi"] += 1
                    for jj in range(4):
                        k = half * 4 + jj
                        tr(P, p[:, jj * 128:(jj + 1) * 128], x_[:, k * 128:(k + 1) * 128], G.ident[:], [x_, G.ident], p, first=(jj == 0))
                    cp(P, "act" if half else "dve", xT[:, half * 4:half * 4 + 4, c * 128:(c + 1) * 128],
                       p[:].rearrange("p (j n) -> p j n", j=4), [p], [xT.k((c, half))])
            xk = [xT.k((c, half)) for c in range(4) for half in range(2)]
            per = (len(nxt) + 7) // 8
            for fo in range(8):
                pa = ps[st["pi"] % 6]; st["pi"] += 1
                pg = ps[st["pi"] % 6]; st["pi"] += 1
                for k in range(8):
                    mm(P, pa[:], w1[:, k, fo * 128:(fo + 1) * 128], xT[:, k, :], k == 0, k == 7, [w1] + xk, pa)
                for k in range(8):
                    mm(P, pg[:], w3[:, k, fo * 128:(fo + 1) * 128], xT[:, k, :], k == 0, k == 7, [w3] + xk, pg)
                tm = tmp[fo % 2]
                act(P, tm[:], pa[:], AF.Silu, [pa], [tm])
                tt(P, "dve", actT[:, fo, :], tm[:], pg[:], ALU.mult, [tm, pg], [actT.k(fo)])
                for fn in nxt[fo * per:(fo + 1) * per]:
                    fn()
            if e + 1 < NEXP:
                prep2(e + 1)
            atk = whole(actT, 8)
            for c in range(4):
                s_ = c // 2
                yb = ysb[c]
                for dh in range(2):
                    p = ps[st["pi"] % 6]; st["pi"] += 1
                    for f in range(8):
                        mm(P, p[:], actT[:, f, c * 128:(c + 1) * 128], w2[:, f, dh * 512:(dh + 1) * 512], f == 0, f == 7, [w2] + atk, p)
                    stt(P, yb[:, dh * 512:(dh + 1) * 512], p[:], gate[b][:, c:c + 1], g2[s_][:, dh * 512:(dh + 1) * 512], ALU.mult, ALU.mult,
                        [p, gate[b], g2[s_]], [yb.k(dh)])
                if c == 3 and e + 1 < NEXP:
                    wts["w2", e + 1] = wload("exp_w2", e + 1, wb[4])
                P.dma("pool", None, None, reads=[yb.k(0), yb.k(1), idx[b]], writes=[G.xres.k("sc")], semof=yb,
                      fn=lambda en, i_=yb[:, :], ia=idx[b][:, c:c + 1]: en.indirect_dma_start(
                          out=xrflat, out_offset=bass.IndirectOffsetOnAxis(ap=ia, axis=0), in_=i_, in_offset=None, compute_op=ALU.add))

        for fn in prep1_ops(0):
            fn()
        prep2(0)
        for e in range(NEXP):
            compute(e)
    P.barrier()


def load_win(P, G, loc, c0, c1):
    n = (c1 - c0) // 512
    win = P.sbuf("win", [128, 8, c1 - c0], BF16, loc)
    for j in range(n):
        P.dma("pool", win[:, :, j * 512:(j + 1) * 512], G.w_in[0, :, c0 + j * 512:c0 + (j + 1) * 512].rearrange("(kt p) f -> p kt f", p=128),
              writes=[win.k(j)], semof=win)
    return win, whole(win, n)


def lru_part(P, G, win, wink, hT, ntok, h0, ya_s=None, hfin=None):
    nq = (ntok + 511) // 512
    qs = min(512, ntok)
    with ExitStack() as loc:
        sets = [{n: P.sbuf(n + str(z), [128, ntok + 4], BF16 if n == "xr" else F32, loc) for n in ("xr", "xa", "rr", "ii", "tmp", "hf", "hb")} for z in range(2)]
        dgc = P.sbuf("dgc", [128, 4, 4, 128], BF16, loc)
        for ci_ in range(4):
            for j_ in range(4):
                ts(P, "dve", dgc[:, ci_, j_, :], G.ident[:], G.caw[:, ci_, j_:j_ + 1], ALU.mult, [G.ident, G.caw], [dgc.k((ci_, j_))])
        dgk = [dgc.k((a_, b_)) for a_ in range(4) for b_ in range(4)]
        xabs = [P.sbuf(f"xab{z}", [128, ntok], BF16, loc) for z in range(2)]
        yab = [P.sbuf(f"yab{i}", [128, ntok], BF16, loc) for i in range(2)]
        ps = G.ps
        st = {"pi": 0}

        def bank():
            p = ps[st["pi"] % 8]
            st["pi"] += 1
            return p

        for z in range(2):
            P.op("dve", lambda e, b_=sets[z]["xr"]: e.memset(b_[:], 0.0), writes=[sets[z]["xr"]])

        def chain(ci):
            xr, xa, rr, ii, tmp, hf, hb = (sets[ci % 2][n] for n in ("xr", "xa", "rr", "ii", "tmp", "hf", "hb"))
            xab = xabs[ci % 2]
            for q in range(nq):
                p = bank()
                for k in range(8):
                    mm(P, p[:, 0:qs], win[:, k, 512 + ci * 128:512 + (ci + 1) * 128], hT[:, k, q * 512:q * 512 + qs], k == 0, k == 7, [hT] + wink, p)
                cp(P, "act", xr[:, 1 + q * 512:1 + q * 512 + qs], p[:, 0:qs], [p], [xr])
                yield
            for q in range(nq):
                p = bank()
                for j in range(4):
                    mm(P, p[:, 0:qs], dgc[:, ci, j, :], xr[:, j + q * 512:j + q * 512 + qs], j == 0, j == 3, [xr] + dgk, p)
                if q == 0:
                    act(P, xa[:, 0:qs], p[:, 0:qs], AF.Identity, [p, G.cab], [xa], bias=G.cab[:, ci:ci + 1], scale=1.0)
                else:
                    P.op("act", lambda e, o=xa[:, q * 512:q * 512 + qs], i_=p[:, 0:qs], b_=G.cab[:, ci:ci + 1]:
                         e.activation(out=o, in_=i_, func=AF.Identity, bias=b_, scale=1.0), reads=[p, G.cab], pwrites=[xa])
                yield
            cp(P, "act", xab[:], xa[:, 0:ntok], [xa], [xab])
            yield
            for k in range(2):
                for gi, (dst, bias) in enumerate(((rr, G.lbr), (ii, G.lbi))):
                    for q in range(nq):
                        p = bank()
                        mm(P, p[:, 0:qs], G.bd[:, (gi * 2 + k) * 4 + ci, :], xab[:, q * 512:q * 512 + qs], True, True, [G.bd, xab], p)
                        act(P, dst[:, q * 512:q * 512 + qs], p[:, 0:qs], AF.Sigmoid, [p, bias], [dst], bias=bias[:, k * 4 + ci:k * 4 + ci + 1], scale=1.0)
                        yield
                act(P, tmp[:, 0:ntok], rr[:, 0:ntok], AF.Exp, [rr, G.spc2], [tmp], scale=G.spc2[:, k * 4 + ci:k * 4 + ci + 1])
                yield
                act(P, rr[:, 0:ntok], rr[:, 0:ntok], AF.Exp, [rr, G.spc], [rr], scale=G.spc[:, k * 4 + ci:k * 4 + ci + 1])
                yield
                act(P, tmp[:, 0:ntok], tmp[:, 0:ntok], AF.Sqrt, [tmp], [tmp], scale=-1.0, bias=G.onec[:, 0:1])
                yield
                tt(P, "dve", ii[:, 0:ntok], ii[:, 0:ntok], tmp[:, 0:ntok], ALU.mult, [ii, tmp], [ii])
                yield
                tt(P, "dve", ii[:, 0:ntok], ii[:, 0:ntok], xa[:, 0:ntok], ALU.mult, [ii, xa], [ii])
                yield
                init = 0.0 if h0 is None else h0[:, ci, k:k + 1]
                rds = [rr, ii] + ([] if h0 is None else [h0])
                if k == 0:
                    P.op("dve", lambda e, o=hf[:, 0:ntok], a=rr[:, 0:ntok], u=ii[:, 0:ntok], i0=init:
                         e.tensor_tensor_scan(out=o, data0=a, data1=u, initial=i0, op0=ALU.mult, op1=ALU.add), reads=rds, writes=[hf])
                else:
                    P.op("dve", lambda e, o=hb[:, 0:ntok][:, ::-1], a=rr[:, 0:ntok][:, ::-1], u=ii[:, 0:ntok][:, ::-1], i0=init:
                         e.tensor_tensor_scan(out=o, data0=a, data1=u, initial=i0, op0=ALU.mult, op1=ALU.add), reads=rds, writes=[hb])
                yield
            if hfin is not None:
                cp(P, "dve", hfin[:, ci, 0:1], hf[:, ntok - 1:ntok], [hf], [hfin.k((ci, 0))])
                cp(P, "dve", hfin[:, ci, 1:2], hb[:, 0:1], [hb], [hfin.k((ci, 1))])
                yield
            if ya_s is not None:
                tt(P, "dve", hf[:, 0:ntok], hf[:, 0:ntok], hb[:, 0:ntok], ALU.add, [hf, hb], [hf])
                yield
                for q in range(nq):
                    p = bank()
                    for k in range(8):
                        mm(P, p[:, 0:qs], win[:, k, ci * 128:(ci + 1) * 128], hT[:, k, q * 512:q * 512 + qs], k == 0, k == 7, [hT] + wink, p)
                    act(P, tmp[:, q * 512:q * 512 + qs], p[:, 0:qs], AF.Gelu_apprx_tanh, [p], [tmp])
                    yield
                yb = yab[ci % 2]
                tt(P, "dve", yb[:], tmp[:, 0:ntok], hf[:, 0:ntok], ALU.mult, [tmp, hf], [yb])
                P.dma("sp", G.yad[ya_s, ci], yb[:], reads=[yb], writes=[G.yad.k((ya_s, ci))], semof=yb)
                yield

        for pair in ((0, 1), (2, 3)):
            gens = [chain(ci) for ci in pair]
            while gens:
                for g_ in list(gens):
                    try:
                        next(g_)
                    except StopIteration:
                        gens.remove(g_)
    P.barrier()


def hy_inproj(P, G, win, wink, hT, s):
    with ExitStack() as loc:
        xr = [P.sbuf(f"hxr{i}", [128, SEQ + 2], F32, loc) for i in range(2)]
        t0 = P.sbuf("hyt", [128, SEQ], F32, loc)
        ob = [P.sbuf(f"hyo{i}", [128, SEQ], BF16, loc) for i in range(2)]
        v_tm = P.sbuf("vtm", [128, NT, 512], BF16, loc)
        ps = G.ps
        pi = 0
        for b in xr:
            P.op("dve", lambda e, b=b: e.memset(b[:], 0.0), writes=[b])
        for fi in range(12):
            part, ci = fi // 4, fi % 4
            x = xr[fi % 2]
            for q in range(4):
                p = ps[pi % 4]; pi += 1
                for k in range(8):
                    mm(P, p[:], win[:, k, fi * 128:(fi + 1) * 128], hT[:, k, q * 512:(q + 1) * 512], k == 0, k == 7, [hT] + wink, p)
                cp(P, "act", x[:, 1 + q * 512:1 + (q + 1) * 512], p[:], [p], [x])
            o = ob[fi % 2]
            ts(P, "dve", t0[:], x[:, 0:SEQ], G.cbw[:, fi, 0:1], ALU.mult, [x, G.cbw, G.cbb], [t0], s2=G.cbb[:, fi:fi + 1], op1=ALU.add)
            stt(P, t0[:], x[:, 1:1 + SEQ], G.cbw[:, fi, 1:2], t0[:], ALU.mult, ALU.add, [x, t0, G.cbw], [t0])
            stt(P, t0[:], x[:, 2:2 + SEQ], G.cbw[:, fi, 2:3], t0[:], ALU.mult, ALU.add, [x, t0, G.cbw], [t0])
            cp(P, "act", o[:], t0[:], [t0], [o])
            P.dma("sp", G.hyd[s, part, ci], o[:], reads=[o], writes=[G.hyd.k((s, part, ci))], semof=o)
            if part == 0:
                for g in range(4):
                    p = ps[4 + g]
                    for jj in range(4):
                        tt_ = g * 4 + jj
                        tr(P, p[:, jj * 128:(jj + 1) * 128], t0[:, tt_ * 128:(tt_ + 1) * 128], G.ident[:], [t0, G.ident], p, first=(jj == 0))
                    cp(P, "dve", v_tm[:, g * 4:(g + 1) * 4, ci * 128:(ci + 1) * 128], p[:].rearrange("p (j c) -> p j c", j=4), [p], [v_tm.k((g, ci))])
        P.dma("sp", G.vtm[s], v_tm[:], reads=[v_tm.k((g, ci)) for g in range(4) for ci in range(4)], writes=[G.vtm.k(s)], semof=v_tm)
    P.barrier()


def bg_step(G):
    if getattr(G, "bg", None):
        G.bg.pop(0)()


def hy_filters(P, G):
    N2 = 2 * SEQ
    with ExitStack() as loc:
        zp = P.sbuf("zp", [33, SEQ + 1], F32, loc)
        w1 = P.sbuf("fw1", [33, 64], F32, loc)
        w2 = P.sbuf("fw2", [64, 64], F32, loc)
        w3 = P.sbuf("fw3", [65, 2048], F32, loc)
        h1 = P.sbuf("fh1", [64, SEQ + 1], F32, loc)
        h2 = P.sbuf("fh2", [65, SEQ + 1], F32, loc)
        rtmp = P.sbuf("rtmp", [64, 512], F32, loc)
        fc = P.sbuf("fc", [64, 8], F32, loc)
        ke = P.sbuf("ke", [128, NT, 2, 512], BF16, loc)
        ko = P.sbuf("ko", [128, NT, 2, 512], BF16, loc)
        wn = [P.sbuf(f"wn{i}", [128, 2, 512], F32, loc) for i in range(2)]
        ft = [P.sbuf(f"ftmp{i}", [128, 2, 512], F32, loc) for i in range(2)]
        cf = [P.sbuf(f"cff{i}", [128, 2, NT, 128], BF16, loc) for i in range(2)]
        kt = [P.sbuf(f"ktb{i}", [128, 2, 512], F32, loc) for i in range(2)]
        k2 = [P.sbuf(f"kt2{i}", [128, 2, 512], F32, loc) for i in range(2)]
        ps = G.ps
        P.dma("sp", zp[:], G.zposT[:], writes=[zp], semof=zp)
        P.dma("sp", w1[:], G.filt_w1[0], writes=[w1], semof=w1)
        P.dma("sp", w2[:], G.filt_w2[0], writes=[w2], semof=w2)
        P.dma("sp", w3[0:64, :], G.filt_w3[0], writes=[w3.k(0)], semof=w3)
        P.dma("sp", w3[64:65, :], G.filt_b3[0:1, :], writes=[w3.k(1)], semof=w3)
        P.dma("sp", fc[:, 0:4], G.filt_c[:], writes=[fc], semof=fc)
        tt(P, "dve", fc[:, 4:6], fc[:, 0:2], fc[:, 2:4], ALU.mult, [fc], [fc.k(1)])
        P.op("dve", lambda e: e.memset(h2[:], 1.0), writes=[h2])
        TWO_PI = 2.0 * math.pi

        def sin_layer(dst, src_ps, fcol, bcol, rd):
            act(P, dst, src_ps, AF.Identity, rd + [fc, fc.k(1)], [h1 if dst is not None else h1], scale=fc[:, fcol:fcol + 1], bias=fc[:, bcol:bcol + 1])

        for layer in range(2):
            src = zp if layer == 0 else h1
            wt = w1 if layer == 0 else w2
            dstT = h1 if layer == 0 else h2
            kk = 33 if layer == 0 else 64
            for q in range(5):
                bg_step(G)
                c0 = q * 512
                n = min(512, SEQ + 1 - c0)
                p = ps[q % 4]
                mm(P, p[0:64, 0:n], wt[0:kk, :], src[0:kk, c0:c0 + n], True, True, [wt, src], p)
                d = dstT[0:64, c0:c0 + n]
                act(P, d, p[0:64, 0:n], AF.Identity, [p, fc, fc.k(1)], [dstT], scale=fc[:, 2 + layer:3 + layer], bias=fc[:, 4 + layer:5 + layer])
                MAGIC = 12582912.0
                kk_ = rtmp[0:64, 0:n]
                ts(P, "dve", kk_, d, 1.0 / TWO_PI, ALU.mult, [dstT], [rtmp], s2=MAGIC, op1=ALU.add)
                ts(P, "dve", kk_, kk_, -MAGIC, ALU.add, [rtmp], [rtmp])
                stt(P, d, kk_, -TWO_PI, d, ALU.mult, ALU.add, [rtmp, dstT], [dstT])
                ts(P, "dve", d, d, -math.pi, ALU.max, [dstT], [dstT], s2=math.pi, op1=ALU.min)
                act(P, d, d, AF.Sin, [dstT], [dstT])
        P.op("dve", lambda e: e.memset(h2[:, SEQ:SEQ + 1], 0.0), writes=[h2])
        it = 0
        for lt in range(NT):
            for o in range(2):
                bg_step(G)
                w = wn[it % 2]
                f = ft[it % 2]
                it += 1
                P.dma("sp", w[:], G.win2[lt], writes=[w], semof=w)
                pf = ps[(it % 2) * 2]
                pb = ps[(it % 2) * 2 + 1]
                mm(P, pf[:], h2[:, lt * 128:(lt + 1) * 128], w3[:, o * 1024:o * 1024 + 512], True, True, [h2, w3.k(0), w3.k(1)], pf)
                mm(P, pb[:], h2[:, lt * 128 + 1:(lt + 1) * 128 + 1], w3[:, o * 1024 + 512:(o + 1) * 1024], True, True, [h2, w3.k(0), w3.k(1)], pb)
                tt(P, "dve", f[:, 0, :], pf[:], w[:, 0, :], ALU.mult, [pf, w], [f.k(0)])
                tt(P, "dve", f[:, 1, :], pb[:], w[:, 1, :], ALU.mult, [pb, w], [f.k(1)])
                tt(P, "dve", ke[:, lt, o, :], f[:, 0, :], f[:, 1, :], ALU.add, [f.k(0), f.k(1)], [ke.k((lt, o))])
                tt(P, "dve", ko[:, lt, o, :], f[:, 1, :], f[:, 0, :], ALU.subtract, [f.k(0), f.k(1)], [ko.k((lt, o))])
        kek = [ke.k((lt, o)) for lt in range(NT) for o in range(2)]
        kok = [ko.k((lt, o)) for lt in range(NT) for o in range(2)]
        for fj in range(NT):
            c = cf[fj % 2]
            P.dma("sp", c[:], G.csf[fj], writes=[c], semof=c)
            for o in range(2):
                bg_step(G)
                pr = ps[4]
                pi_ = ps[5]
                k_ = kt[o]
                k2_ = k2[o]
                for lt in range(NT):
                    mm(P, pr[:], c[:, 0, lt, :], ke[:, lt, o, :], lt == 0, lt == NT - 1, [c] + kek, pr)
                for lt in range(NT):
                    mm(P, pi_[:], c[:, 1, lt, :], ko[:, lt, o, :], lt == 0, lt == NT - 1, [c] + kok, pi_)
                ts(P, "dve", k2_[:, 0, :], pi_[:], G.rot[:, fj, 1:2], ALU.mult, [pi_, G.rot], [k2_.k(0)])
                ts(P, "dve", k2_[:, 1, :], pi_[:], G.rot[:, fj, 0:1], ALU.mult, [pi_, G.rot], [k2_.k(1)])
                stt(P, k_[:, 0, :], pr[:], G.rot[:, fj, 0:1], k2_[:, 0, :], ALU.mult, ALU.subtract, [pr, G.rot, k2_.k(0)], [k_.k(0)])
                stt(P, k_[:, 1, :], pr[:], G.rot[:, fj, 1:2], k2_[:, 1, :], ALU.mult, ALU.add, [pr, G.rot, k2_.k(1)], [k_.k(1)])
                P.dma("sp", G.ktab[o, fj], k_[:], reads=[k_.k(0), k_.k(1)], writes=[G.ktab.k((o, fj))], semof=k_)
        while G.bg:
            bg_step(G)
    P.barrier()


def hy_conv(P, G, o, z_tm, zT, xgT, want_tm):
    with ExitStack() as loc:
        cf = [P.sbuf(f"cf{i}", [128, 2, NT, 128], BF16, loc) for i in range(2)]
        kt = [P.sbuf(f"kt{i}", [128, 2, 512], F32, loc) for i in range(2)]
        Y = P.sbuf("Y", [128, NT, 2, 512], BF16, loc)
        tmp = [P.sbuf(f"yt{i}", [128, 4, 512], F32, loc) for i in range(2)]
        ci_ = [P.sbuf(f"ci{i}", [128, NT, 512], BF16, loc) for i in range(2)]
        ps = G.ps
        ztk = [z_tm.k((g, ci)) for g in range(4) for ci in range(4)]
        for fj in range(NT):
            c = cf[fj % 2]
            k_ = kt[fj % 2]
            t_ = tmp[fj % 2]
            P.dma("sp", c[:], G.csf[fj], writes=[c], semof=c)
            P.dma("act", k_[:], G.ktab[o, fj], writes=[k_], semof=k_)
            pr = ps[(fj % 2) * 2]
            pq = ps[(fj % 2) * 2 + 1]
            for tt_ in range(NT):
                mm(P, pr[:], c[:, 0, tt_, :], z_tm[:, tt_, :], tt_ == 0, tt_ == NT - 1, [c] + ztk, pr)
            for tt_ in range(NT):
                mm(P, pq[:], c[:, 1, tt_, :], z_tm[:, tt_, :], tt_ == 0, tt_ == NT - 1, [c] + ztk, pq)
            tt(P, "dve", t_[:, 0, :], pr[:], k_[:, 0, :], ALU.mult, [pr, k_], [t_.k(0)])
            tt(P, "dve", t_[:, 1, :], pq[:], k_[:, 1, :], ALU.mult, [pq, k_], [t_.k(1)])
            tt(P, "dve", t_[:, 2, :], pq[:], k_[:, 0, :], ALU.mult, [pq, k_], [t_.k(2)])
            tt(P, "dve", t_[:, 3, :], pr[:], k_[:, 1, :], ALU.mult, [pr, k_], [t_.k(3)])
            tt(P, "dve", Y[:, fj, 0, :], t_[:, 0, :], t_[:, 1, :], ALU.add, [t_.k(0), t_.k(1)], [Y.k(fj)])
            tt(P, "dve", Y[:, fj, 1, :], t_[:, 2, :], t_[:, 3, :], ALU.subtract, [t_.k(2), t_.k(3)], [Y.k((fj, 1))])
        Yk = [Y.k(fj) for fj in range(NT)] + [Y.k((fj, 1)) for fj in range(NT)]
        for tq in range(4):
            for cs in range(2):
                P.dma("sp", ci_[cs][:], G.csi[tq, cs], writes=[ci_[cs]], semof=ci_[cs])
            for ci in range(4):
                p = ps[4 + ci]
                n = 0
                for cs in range(2):
                    for fj in range(NT):
                        mm(P, p[:], Y[:, fj, cs, ci * 128:(ci + 1) * 128], ci_[cs][:, fj, :], n == 0, n == 2 * NT - 1, Yk + ci_, p)
                        n += 1
                t_ = tmp[ci % 2]
                sl = slice(tq * 512, (tq + 1) * 512)
                zk = zT.k((ci, tq))
                stt(P, t_[:, 0, :], zT[:, ci, sl], G.fbias[:, o * 4 + ci:o * 4 + ci + 1], p[:], ALU.mult, ALU.add, [zk, G.fbias, p], [t_.k(0)])
                tt(P, "dve", t_[:, 1, :], t_[:, 0, :], xgT[:, ci, sl], ALU.mult, [t_.k(0), xgT], [t_.k(1)])
                cp(P, "act", zT[:, ci, sl], t_[:, 1, :], [t_.k(1)], [zk])
                if want_tm:
                    pt = ps[ci % 4]
                    for jj in range(4):
                        tr(P, pt[:, jj * 128:(jj + 1) * 128], t_[:, 1, jj * 128:(jj + 1) * 128], G.ident[:], [t_.k(1), G.ident], pt, first=(jj == 0))
                    cp(P, "dve", z_tm[:, tq * 4:(tq + 1) * 4, ci * 128:(ci + 1) * 128], pt[:].rearrange("p (j c) -> p j c", j=4), [pt], [z_tm.k((tq, ci))])
    P.barrier()


def mixer_out(P, G, l, s, kT_list, w_ap, bias_ap, igate, src=None):
    with ExitStack() as loc:
        w = P.sbuf("wout", [128, 8, D], BF16, loc)
        g1 = P.sbuf("g1", [128, D], F32, loc)
        xt = [P.sbuf(f"ox{i}", [128, D], F32, loc) for i in range(4)]
        tmp = [P.sbuf(f"ot{i}", [128, D], F32, loc) for i in range(2)]
        ps = G.ps
        for hk in range(2):
            P.dma("pool", w[:, hk * 4:(hk + 1) * 4, :], w_ap[hk * 512:(hk + 1) * 512, :].rearrange("(kt p) f -> p kt f", p=128),
                  writes=[w.k(hk)], semof=w)
        wk = whole(w, 2)
        if bias_ap is not None:
            brow = P.sbuf("brow", [1, D], BF16, loc)
            P.dma("pool", brow[:], bias_ap, writes=[brow], semof=brow)
        P.dma("sp", g1[:], G.mrow[l, s:s + 1, igate * D:(igate + 1) * D].broadcast_to([128, D]), writes=[g1], semof=g1)
        xsrc_ = src if src is not None else G.xres

        def xload(t):
            P.dma("sp", xt[t % 4][:], xsrc_[s, t * 128:(t + 1) * 128, :], writes=[xt[t % 4]], semof=xt[t % 4])

        xload(0)
        xload(1)
        for t in range(NT):
            if t + 2 < NT:
                xload(t + 2)
            x = xt[t % 4]
            tm = tmp[t % 2]
            for dh in range(2):
                p = ps[(t % 4) * 2 + dh]
                for k in range(8):
                    kt_, idx = kT_list[k]
                    mm(P, p[:], kt_[:, idx, t * 128:(t + 1) * 128], w[:, k, dh * 512:(dh + 1) * 512], k == 0,
                       (k == 7 and bias_ap is None), [kt_] + wk, p)
                if bias_ap is not None:
                    mm(P, p[:], G.onesb[0:1, :], brow[0:1, dh * 512:(dh + 1) * 512], False, True, [G.onesb, brow], p)
                tt(P, "dve", tm[:, dh * 512:(dh + 1) * 512], p[:], g1[:, dh * 512:(dh + 1) * 512], ALU.mult, [p, g1], [tm.k(dh)])
            tt(P, "dve", x[:], x[:], tm[:], ALU.add, [x, tm.k(0), tm.k(1)], [x])
            P.dma("sp", G.xres[s, t * 128:(t + 1) * 128, :], x[:], reads=[x], writes=[G.xres.k((s, t))], semof=x)
    P.barrier()


def conformer(P, G, s, hT, sT):
    PADC = 15 * 64
    with ExitStack() as loc:
        u = P.sbuf("cfu", [128, 8, SEQ], F32, loc)
        with ExitStack() as l1:
            w1 = P.sbuf("cfw1", [128, 8, 2 * D], BF16, l1)
            gl = [P.sbuf(f"cfgl{i}", [128, SEQ + 2 * PADC], BF16, l1) for i in range(2)]
            dg = [P.sbuf(f"cfdg{i}", [128, 31, 128], BF16, l1) for i in range(2)]
            sg = [P.sbuf(f"cfsg{i}", [128, 512], F32, l1) for i in range(2)]
            ps = G.ps
            for j in range(4):
                P.dma("pool", w1[:, :, j * 512:(j + 1) * 512], G.cf_w1[0, :, j * 512:(j + 1) * 512].rearrange("(kt p) f -> p kt f", p=128),
                      writes=[w1.k(j)], semof=w1)
            w1k = whole(w1, 4)
            for b in gl:
                P.op("dve", lambda e, b=b: e.memset(b[:], 0.0), writes=[b])
            pi = 0
            for ci in range(8):
                g_ = gl[ci % 2]
                d_ = dg[ci % 2]
                for j in range(31):
                    ts(P, "dve", d_[:, j, :], G.ident[:], G.cfdw[:, ci, j:j + 1], ALU.mult, [G.ident, G.cfdw], [d_.k(j)])
                for q in range(4):
                    pa = ps[pi % 4]; pi += 1
                    pg = ps[pi % 4]; pi += 1
                    s_ = sg[q % 2]
                    for k in range(8):
                        mm(P, pa[:], w1[:, k, ci * 128:(ci + 1) * 128], hT[:, k, q * 512:(q + 1) * 512], k == 0, k == 7, [hT] + w1k, pa)
                    for k in range(8):
                        mm(P, pg[:], w1[:, k, D + ci * 128:D + (ci + 1) * 128], hT[:, k, q * 512:(q + 1) * 512], k == 0, k == 7, [hT] + w1k, pg)
                    act(P, s_[:], pg[:], AF.Sigmoid, [pg, G.cfb1], [s_], bias=G.cfb1[:, 8 + ci:9 + ci], scale=1.0)
                    stt(P, g_[:, PADC + q * 512:PADC + (q + 1) * 512], pa[:], G.cfb1[:, ci:ci + 1], s_[:], ALU.add, ALU.mult, [pa, G.cfb1, s_], [g_.k(q)])
                gk = whole(g_, 4)
                dk = whole(d_, 31)
                for q in range(4):
                    pc = ps[4 + q]
                    taps = [j for j in range(31) if 64 * j + 512 * q + 512 > PADC and 64 * j + 512 * q < PADC + SEQ]
                    for n, j in enumerate(taps):
                        c0 = 64 * j + 512 * q
                        mm(P, pc[:], d_[:, j, :], g_[:, c0:c0 + 512], n == 0, n == len(taps) - 1, gk + dk, pc)
                    act(P, u[:, ci, q * 512:(q + 1) * 512], pc[:], AF.Identity, [pc, G.cfdb], [u.k(ci)] if q == 0 else [], bias=G.cfdb[:, ci:ci + 1], scale=1.0) if q == 0 else \
                        P.op("act", lambda e, o=u[:, ci, q * 512:(q + 1) * 512], i=pc[:], b=G.cfdb[:, ci:ci + 1]: e.activation(out=o, in_=i, func=AF.Identity, bias=b, scale=1.0),
                             reads=[pc, G.cfdb], pwrites=[u.k(ci)])
        P.barrier()
        with ExitStack() as l2:
            sq = [P.sbuf(f"cfsq{i}", [128, 512], F32, l2) for i in range(2)]
            mean = P.sbuf("cfmean", [128, 512], F32, l2)
            rstd = P.sbuf("cfrstd", [128, 512], F32, l2)
            t1 = [P.sbuf(f"cft{i}", [128, 512], F32, l2) for i in range(2)]
            ps = G.ps
            uk = whole(u, 8)
            for q in range(4):
                sl = slice(q * 512, (q + 1) * 512)
                pm = ps[(q % 2) * 2]
                pv = ps[(q % 2) * 2 + 1]
                for ci in range(8):
                    mm(P, pm[:], G.onesm[:], u[:, ci, sl], ci == 0, ci == 7, [G.onesm] + uk, pm)
                for ci in range(8):
                    s_ = sq[ci % 2]
                    act(P, s_[:], u[:, ci, sl], AF.Square, uk, [s_])
                    mm(P, pv[:], G.onesm[:], s_[:], ci == 0, ci == 7, [G.onesm, s_], pv)
                cp(P, "act", mean[:], pm[:], [pm], [mean])
                tt(P, "dve", rstd[:], mean[:], mean[:], ALU.mult, [mean], [rstd])
                tt(P, "dve", rstd[:], pv[:], rstd[:], ALU.subtract, [pv, rstd], [rstd])
                act(P, rstd[:], rstd[:], AF.Sqrt, [rstd], [rstd], scale=1.0, bias=G.epsc[:, 0:1])
                P.op("dve", lambda e: e.reciprocal(out=rstd[:], in_=rstd[:]), reads=[rstd], writes=[rstd])
                for ci in range(8):
                    t_ = t1[ci % 2]
                    tt(P, "dve", t_[:], u[:, ci, sl], mean[:], ALU.subtract, uk + [mean], [t_])
                    tt(P, "dve", t_[:], t_[:], rstd[:], ALU.mult, [t_, rstd], [t_])
                    act(P, sT[:, ci, sl], t_[:], AF.Silu, [t_, G.cfln], [sT.k((ci, q))], scale=G.cfln[:, ci:ci + 1], bias=G.cfln[:, 8 + ci:9 + ci])
    P.barrier()


INPUT_SPECS = [
    ("x", [NS, SEQ, D], F32), ("ctx", [NS, CTX, D], F32), ("csT", [128, 8, 3], F32),
    ("w_mod", [2, D, 6 * D], F32), ("b_mod", [2, 6 * D], F32), ("norm1_g", [2, D], F32), ("norm2_g", [2, D], F32),
    ("final_g", [1, D], F32), ("ident_d", [128, 128], F32), ("iota_d", [128, CAP], F32), ("npos_d", [128, NT, 2], F32),
    ("w_in", [1, D, 2560], F32), ("caw_d", [128, 4, 4], F32), ("cab_d", [128, 4], F32),
    ("lbr_d", [128, 8], F32), ("lbi_d", [128, 8], F32), ("lam_d", [128, 8], F32),
    ("lru_wr", [1, 2, 8, 64, 64], F32), ("lru_wi", [1, 2, 8, 64, 64], F32),
    ("cbw_d", [128, 12, 3], F32), ("cbb_d", [128, 12], F32), ("fbias_d", [128, 8], F32), ("filt_c", [64, 4], F32),
    ("filt_w1", [1, 33, 64], F32), ("filt_w2", [1, 64, 64], F32), ("filt_w3", [1, 64, 2048], F32), ("filt_b3", [1, 2048], F32),
    ("zposT", [33, SEQ + 1], F32), ("win2", [NT, 128, 2, 512], F32), ("rot_d", [128, NT, 2], F32),
    ("csf", [NT, 128, 2, NT, 128], BF16), ("csi", [4, 2, 128, NT, 512], BF16),
    ("w_out", [1, D, D], F32),
    ("cf_w1", [1, D, 2 * D], F32), ("cfb1_d", [128, 16], F32), ("cfdw_d", [128, 8, 31], F32), ("cfdb_d", [128, 8], F32),
    ("cfln_d", [128, 16], F32), ("cf_w2", [1, D, D], F32), ("cf_b2", [1, D], F32),
    ("router_d", [2, 128, 8, NEXP], F32),
    ("exp_w1", [2, NEXP, D, D], F32), ("exp_w3", [2, NEXP, D, D], F32), ("exp_w2", [2, NEXP, D, D], F32),
]


def declare(nc, P, G, dbg):
    for name, shape, dt in INPUT_SPECS:
        setattr(G, name, T(name, nc.dram_tensor(name, list(shape), dt, kind="ExternalInput").ap()))
    kind = "ExternalOutput" if dbg else "Internal"
    G.mrow = P.dram("mrow", [2, 3, 6 * D], F32, kind)
    G.xres = P.dram("xres", [NS, SEQ, D], F32, kind)
    G.ktab = P.dram("ktab", [2, NT, 128, 2, 512], F32, kind)
    G.hyd = P.dram("hyd", [NS, 3, 4, 128, SEQ], BF16, kind)
    G.vtm = P.dram("vtm", [NS, 128, NT, 512], BF16, kind)
    G.yad = P.dram("yad", [NS, 4, 128, SEQ], BF16, kind)
    G.h2d = P.dram("h2d", [NS, SEQ, D], F32, kind)
    G.out = P.dram("out", [NS, SEQ, D], F32, "ExternalOutput")
    if dbg:
        G.dbg_ig = P.dram("dbg_ig", [NEXP, 128, 4, 4], F32, kind)
        G.dbg_xs = P.dram("dbg_xs", [4, 128, D], F32, kind)
        G.dbg_idx = P.dram("dbg_idx", [128, 4], mybir.dt.int32, kind)


def setup_consts(P, G):
    def ld(name, src, shape, dt=F32, q="sp"):
        t = P.sbuf(name, shape, dt)
        P.dma(q, t[:], src[:], writes=[t], semof=t)
        setattr(G, name, t)
        return t

    ld("ident", G.ident_d, [128, 128]); ld("iota", G.iota_d, [128, CAP]); ld("npos", G.npos_d, [128, NT, 2])
    ld("caw", G.caw_d, [128, 4, 4]); ld("cab", G.cab_d, [128, 4]); ld("lbr", G.lbr_d, [128, 8]); ld("lbi", G.lbi_d, [128, 8])
    lam = ld("lam", G.lam_d, [128, 8])
    ld("cbw", G.cbw_d, [128, 12, 3]); ld("cbb", G.cbb_d, [128, 12]); ld("fbias", G.fbias_d, [128, 8])
    ld("rot", G.rot_d, [128, NT, 2])
    ld("cfb1", G.cfb1_d, [128, 16]); ld("cfdw", G.cfdw_d, [128, 8, 31]); ld("cfdb", G.cfdb_d, [128, 8]); ld("cfln", G.cfln_d, [128, 16])
    G.epsc = P.sbuf("epsc", [128, 1], F32)
    G.onec = P.sbuf("onec", [128, 1], F32)
    G.onesm = P.sbuf("onesm", [128, 128], F32)
    G.onesb = P.sbuf("onesb", [1, 128], BF16)
    G.spc = P.sbuf("spc", [128, 8], F32)
    G.bd = P.sbuf("bd", [128, 16, 128], BF16)
    P.op("dve", lambda e: e.memset(G.epsc[:], EPS), writes=[G.epsc])
    P.op("dve", lambda e: e.memset(G.onec[:], 1.0), writes=[G.onec])
    P.op("dve", lambda e: e.memset(G.onesm[:], 1.0 / D), writes=[G.onesm])
    P.op("dve", lambda e: e.memset(G.onesb[:], 1.0), writes=[G.onesb])
    P.op("dve", lambda e: e.memset(G.bd[:], 0.0), writes=[G.bd])
    ep = P.sbuf("ep", [128, 8], F32)
    t = P.sbuf("ept", [128, 8], F32)
    act(P, ep[:], lam[:], AF.Exp, [lam], [ep], scale=-1.0)
    ts(P, "dve", t[:], ep[:], 1.0 / 3.0, ALU.mult, [ep], [t], s2=-0.5, op1=ALU.add)
    tt(P, "dve", t[:], t[:], ep[:], ALU.mult, [t, ep], [t])
    ts(P, "dve", t[:], t[:], 1.0, ALU.add, [t], [t])
    tt(P, "dve", t[:], t[:], ep[:], ALU.mult, [t, ep], [t])
    ts(P, "dve", G.spc[:], t[:], -8.0, ALU.mult, [t], [G.spc])
    G.spc2 = P.sbuf("spc2", [128, 8], F32)
    ts(P, "dve", G.spc2[:], t[:], -16.0, ALU.mult, [t], [G.spc2])
    for gi, wsrc in enumerate((G.lru_wr, G.lru_wi)):
        for k in range(2):
            for ci in range(4):
                for hh in range(2):
                    idx = (gi * 2 + k) * 4 + ci
                    P.dma("pool", G.bd[hh * 64:(hh + 1) * 64, idx, hh * 64:(hh + 1) * 64], wsrc[0, k, ci * 2 + hh],
                          writes=[G.bd], semof=G.bd)
    P.barrier()


def build(dbg=False, upto=99):
    nc = bass.Bass("TRN2", target_bir_lowering=False)
    es = ExitStack()
    with es:
        P = Prog(nc, es)
        G = Ctx()
        declare(nc, P, G, dbg)
        G.ps = [P.psum(f"ps{i}", [128, 512], F32) for i in range(8)]
        setup_consts(P, G)
        ada_stack = ExitStack()
        G.bg = stage_adaln(P, G, ada_stack)
        if upto < 1:
            while G.bg:
                G.bg.pop(0)()
            P.barrier()
            ada_stack.close()

        def xsrc(tensor, s):
            return lambda t: ([], tensor[s, t * 128:(t + 1) * 128, :])

        def moe_layer(l, src):
            with ExitStack() as ml:
                code = P.sbuf("code", [128, NT, NS * NEXP], F32, ml)
                pg = P.sbuf("pg", [128, NT, NS * NEXP], F32, ml)
                with ExitStack() as ns:
                    probs = P.sbuf("probs", [128, NT, NS * NEXP], F32, ns)
                    rt = P.sbuf("rt", [128, 8, NEXP], F32, ns)
                    P.dma("sp", rt[:], G.router_d[l], writes=[rt], semof=rt)
                    norm_jobs(P, G, NT, [dict(src_tile=xsrc(src, s), g_ap=G.norm2_g[l:l + 1, :], l=l, row=s, ish=3, isc=4, hT=None, h_tm=s,
                                              probs=probs, rt=rt, pcol=s * NEXP) for s in range(NS)])
                    route_stage(P, G, probs, code, pg)
                if upto >= 4 + 10 * l:
                    expert_stage(P, G, l, code, pg, 5)

        if upto >= 1:
            hy_filters(P, G)
            ada_stack.close()
            for s in range(NS):
                with ExitStack() as la:
                    hTc = P.sbuf("hTc", [128, 8, CTX], BF16, la)
                    h0 = P.sbuf("h0", [128, 4, 2], F32, la)
                    hT = P.sbuf("hT", [128, 8, SEQ], BF16, la)
                    with ExitStack() as lw:
                        win, wink = load_win(P, G, lw, 0, 1024)
                        norm_stage(P, G, la, xsrc(G.ctx, s), CTX // 128, G.norm1_g[0:1, :], 0, 2, 0, 1, hT=hTc)
                        lru_part(P, G, win, wink, hTc, CTX, None, hfin=h0)
                        norm_stage(P, G, la, xsrc(G.x, s), NT, G.norm1_g[0:1, :], 0, s, 0, 1, hT=hT)
                        lru_part(P, G, win, wink, hT, SEQ, h0, ya_s=s)
                    with ExitStack() as lw:
                        win, wink = load_win(P, G, lw, 1024, 2560)
                        hy_inproj(P, G, win, wink, hT, s)
                if upto >= 2:
                    with ExitStack() as lb:
                        zT = P.sbuf("zT", [128, 4, SEQ], BF16, lb)
                        x1T = P.sbuf("x1T", [128, 4, SEQ], BF16, lb)
                        x2T = P.sbuf("x2T", [128, 4, SEQ], BF16, lb)
                        yaT = P.sbuf("yaT", [128, 4, SEQ], BF16, lb)
                        z_tm = P.sbuf("z_tm", [128, NT, 512], BF16, lb)
                        for ci in range(4):
                            P.dma("sp", zT[:, ci, :], G.hyd[s, 0, ci], writes=[zT.k((ci, 0))], semof=zT)
                            P.dma("sp", x1T[:, ci, :], G.hyd[s, 1, ci], pwrites=[x1T], semof=x1T)
                            P.dma("sp", x2T[:, ci, :], G.hyd[s, 2, ci], pwrites=[x2T], semof=x2T)
                            P.dma("sp", yaT[:, ci, :], G.yad[s, ci], pwrites=[yaT], semof=yaT)
                        P.dma("sp", z_tm[:], G.vtm[s], writes=[z_tm], semof=z_tm)
                        P.barrier()
                        hy_conv(P, G, 0, z_tm, zT, x1T, True)
                        hy_conv(P, G, 1, z_tm, zT, x2T, False)
                        kl = [(yaT, i) for i in range(4)] + [(zT, i) for i in range(4)]
                        mixer_out0(P, G, s, kl)
        def snap(name):
            if not dbg:
                return
            t = P.dram("snap_" + name, [NS, SEQ, D], F32, "ExternalOutput")
            for s in range(NS):
                P.dma("sp", t[s], G.xres[s], writes=[t.k(s)], semof=t)
            P.barrier()

        if upto >= 3:
            moe_layer(0, G.xres)
            snap("xl0")
        if upto >= 11:
            for s in range(NS):
                with ExitStack() as la:
                    hT = P.sbuf("hT1", [128, 8, SEQ], BF16, la)
                    norm_stage(P, G, la, xsrc(G.xres, s), NT, G.norm1_g[1:2, :], 1, s, 0, 1, hT=hT)
                    conformer(P, G, s, hT, hT)
                    mixer_out(P, G, 1, s, [(hT, i) for i in range(8)], G.cf_w2[0], G.cf_b2[0:1, :], 2)
            snap("xl1a")
        if upto >= 12:
            moe_layer(1, G.xres)
        if upto >= 20:
            final_norm(P, G)
        P.barrier()
        P.emit()
    return nc


def mixer_out0(P, G, s, kl):
    mixer_out(P, G, 0, s, kl, G.w_out[0], None, 2, src=G.x)


def final_norm(P, G):
    with ExitStack() as loc:
        NB, AHEAD = 6, 4
        gb = P.sbuf("fg", [128, D], F32, loc)
        xt = [P.sbuf(f"fx{i}", [128, D], F32, loc) for i in range(NB)]
        junks = [P.sbuf(f"fjunk{i}", [128, D], F32, loc) for i in range(2)]
        st = [P.sbuf(f"fst{i}", [128, 4], F32, loc) for i in range(NB)]
        P.dma("sp", gb[:], G.final_g[0:1, :].broadcast_to([128, D]), writes=[gb], semof=gb)
        items = [(t, s) for t in range(NT) for s in range(NS)]

        def load(i):
            t, s = items[i]
            x = xt[i % NB]
            P.dma("sp", x[:], G.xres[s, t * 128:(t + 1) * 128, :], writes=[x], semof=x)

        for i in range(AHEAD):
            load(i)
        for i, (t, s) in enumerate(items):
            if i + AHEAD < len(items):
                load(i + AHEAD)
            x = xt[i % NB]
            s_ = st[i % NB]
            junk = junks[i % 2]
            act(P, junk[:], x[:], AF.Square, [x], [junk, s_.k(0)], accum_out=s_[:, 0:1])
            act(P, s_[:, 1:2], s_[:, 0:1], AF.Sqrt, [s_.k(0)], [s_.k(1)], scale=1.0 / D, bias=G.epsc[:, 0:1])
            P.op("dve", lambda e, o=s_[:, 2:3], i_=s_[:, 1:2]: e.reciprocal(out=o, in_=i_), reads=[s_.k(1)], writes=[s_.k(2)])
            stt(P, x[:], x[:], s_[:, 2:3], gb[:], ALU.mult, ALU.mult, [x, s_.k(2), gb], [x])
            P.dma("sp", G.out[s, t * 128:(t + 1) * 128, :], x[:], reads=[x], writes=[G.out.k((s, t))], semof=x)
    P.barrier()


_CONST = {}


def host_consts():
    if _CONST:
        return _CONST
    import ml_dtypes
    bf = ml_dtypes.bfloat16
    n = SEQ
    N2 = 2 * n
    t = np.arange(n, dtype=np.float64)
    phi = 2 * np.pi * np.outer(t + 0.5, t + 0.5) / N2
    C = np.cos(phi); S = np.sin(phi)
    CS = np.stack([C, S])
    csf = CS.reshape(2, NT, 128, NT, 128).transpose(3, 2, 0, 1, 4)
    csi = CS.reshape(2, NT, 128, 4, 512).transpose(3, 0, 2, 1, 4)
    _CONST["csf"] = np.ascontiguousarray(csf).astype(bf)
    _CONST["csi"] = np.ascontiguousarray(csi).astype(bf)
    al = np.pi * (t + 0.5) / N2
    rot = np.stack([np.cos(al), np.sin(al)], -1) * (2.0 / N2)
    _CONST["rot_d"] = np.ascontiguousarray(rot.reshape(NT, 128, 2).transpose(1, 0, 2)).astype(np.float32)
    f32 = np.float32
    tt_ = np.linspace(0.0, 1.0, n, dtype=f32)[:, None]
    bands = 16
    w = (f32(2.0 * math.pi / n) * np.arange(n, dtype=f32))[:, None]
    fr = np.linspace(1e-4, bands - 1, bands, dtype=f32)[None, :]
    z = np.concatenate([tt_, np.cos(fr * w), -np.sin(fr * w)], axis=-1).astype(f32)
    zT = np.zeros((33, n + 1), f32); zT[:, :n] = z.T
    _CONST["zposT"] = zT
    deltas = np.abs(np.linspace(math.log(1e-2) / 1.5, math.log(1e-2) / 0.3, 512, dtype=f32))
    window = (np.exp(-tt_ * deltas[None, :]) + f32(0.05)).astype(f32)
    wsh = np.zeros_like(window); wsh[:-1] = window[1:]
    w2 = np.stack([window, wsh], 1)
    _CONST["win2"] = np.ascontiguousarray(w2.reshape(NT, 128, 2, 512)).astype(f32)
    _CONST["ident_d"] = np.eye(128, dtype=f32)
    _CONST["iota_d"] = np.tile(np.arange(CAP, dtype=f32)[None, :], (128, 1))
    npos = np.zeros((128, NT, 2), f32); npos[:, :, 0] = np.arange(NT)[None, :]; npos[:, :, 1] = np.arange(128)[:, None]
    _CONST["npos_d"] = npos
    return _CONST


def _cols(v, nt):
    return np.ascontiguousarray(np.asarray(v, np.float32).reshape(nt, 128).T)


def host_shared(inputs):
    f = lambda a: np.ascontiguousarray(np.asarray(a, dtype=np.float32))
    m = dict(host_consts())
    for k in ("w_mod", "b_mod", "norm1_g", "norm2_g", "w_in", "lru_wr", "lru_wi", "filt_w1", "filt_w2", "filt_w3", "filt_b3",
              "w_out", "cf_w1", "cf_w2", "cf_b2", "exp_w1", "exp_w3", "exp_w2"):
        m[k] = f(inputs[k])
    m["final_g"] = f(inputs["final_g"]).reshape(1, D)
    m["caw_d"] = np.ascontiguousarray(f(inputs["conv_a_w"])[0].reshape(4, 4, 128).transpose(2, 1, 0))
    m["cab_d"] = _cols(inputs["conv_a_b"][0], 4)
    m["lbr_d"] = _cols(np.asarray(inputs["lru_br"])[0].reshape(-1), 8)
    m["lbi_d"] = _cols(np.asarray(inputs["lru_bi"])[0].reshape(-1), 8)
    m["lam_d"] = _cols(np.asarray(inputs["lru_lam"])[0].reshape(-1), 8)
    m["cbw_d"] = np.ascontiguousarray(f(inputs["conv_b_w"])[0].reshape(3, 12, 128).transpose(2, 1, 0))
    m["cbb_d"] = _cols(inputs["conv_b_b"][0], 12)
    m["fbias_d"] = _cols(np.asarray(inputs["filt_bias"])[0].reshape(-1), 8)
    m["filt_c"] = np.ascontiguousarray(np.stack([f(inputs["filt_b1"])[0], f(inputs["filt_b2"])[0],
                                                 f(inputs["filt_freq"])[0, 0], f(inputs["filt_freq"])[0, 1]], -1))
    m["cfb1_d"] = _cols(inputs["cf_b1"][0], 16)
    m["cfdw_d"] = np.ascontiguousarray(f(inputs["cf_dw_w"])[0].reshape(31, 8, 128).transpose(2, 1, 0))
    m["cfdb_d"] = _cols(inputs["cf_dw_b"][0], 8)
    m["cfln_d"] = np.ascontiguousarray(np.concatenate([_cols(inputs["cf_ln_g"][0], 8), _cols(inputs["cf_ln_b"][0], 8)], 1))
    m["router_d"] = np.ascontiguousarray(f(inputs["router"]).reshape(2, 8, 128, NEXP).transpose(0, 2, 1, 3))
    return m


def host_prep(inputs, core, shared=None):
    f = lambda a: np.ascontiguousarray(np.asarray(a, dtype=np.float32))
    m = dict(shared if shared is not None else host_shared(inputs))
    s0 = core * NS
    cc = np.stack([np.asarray(inputs["c"])[s0], np.asarray(inputs["c"])[s0 + 1], np.asarray(inputs["c_ctx"])]).astype(np.float32)
    m["x"] = f(inputs["x"][s0:s0 + NS])
    m["ctx"] = f(inputs["ctx"][s0:s0 + NS])
    m["csT"] = f(cc.T.reshape(8, 128, 3).transpose(1, 0, 2))
    return m


def kernel(**inputs):
    nc = build()
    shared = host_shared(inputs)
    in_maps = [host_prep(inputs, c, shared) for c in range(NCORES)]
    res = run_bass_kernel_spmd(nc, in_maps, core_ids=list(range(NCORES)))
    return np.concatenate([np.asarray(r["out"], dtype=np.float32) for r in res.results], axis=0)
```
